# Optimizing a Trainium2 kernel written in Bass

```python
import math
import jax, jax.numpy as jnp
from jax import lax
import numpy as np

D_MODEL = 2048
BATCH = 4
SEQ = 8192
DEPTH = 4

GRID_W = 64
CTX_LEN = 256
E_SSM = D_MODEL
SSM_GROUP = 16
N_GROUPS = E_SSM // SSM_GROUP
N_STATE = 64
E_CONV = D_MODEL
CONV_W = 31
D_IN = 2 * E_SSM + 3 * E_CONV + 2 * D_MODEL
DT_MIN = 1e-3
DT_MAX = 1e-1
EPS = 1e-6

kernel_name = 'hybrid_s5_conformer_prefix_trunk'


def rmsnorm(x, g):
    xf = x.astype(jnp.float32)
    y = xf * lax.rsqrt(jnp.mean(xf * xf, axis=-1, keepdims=True) + EPS)
    return y.astype(x.dtype) * g


def layernorm(x, g, b):
    xf = x.astype(jnp.float32)
    mu = jnp.mean(xf, axis=-1, keepdims=True)
    var = jnp.mean(jnp.square(xf - mu), axis=-1, keepdims=True)
    return ((xf - mu) * lax.rsqrt(var + EPS)).astype(x.dtype) * g + b


def in_proj(x, g_pre, shift, scale, w_in):
    h = rmsnorm(x, g_pre) * (1 + scale) + shift
    p = h @ w_in
    cuts = [E_SSM, 2 * E_SSM, 2 * E_SSM + E_CONV, 2 * E_SSM + 2 * E_CONV,
            2 * E_SSM + 3 * E_CONV, 2 * E_SSM + 3 * E_CONV + D_MODEL]
    return jnp.split(p, cuts, axis=-1)


def ssm_discretise(a_re, a_im, log_dt, b_re, b_im):
    dt = jnp.exp(log_dt)[:, None]
    mag = jnp.exp(dt * a_re)
    abar_re = mag * jnp.cos(dt * a_im)
    abar_im = mag * jnp.sin(dt * a_im)
    den = a_re * a_re + a_im * a_im
    f_re = ((abar_re - 1) * a_re + abar_im * a_im) / den
    f_im = (abar_im * a_re - (abar_re - 1) * a_im) / den
    f_re, f_im = f_re[:, None, :], f_im[:, None, :]
    bbar_re = f_re * b_re - f_im * b_im
    bbar_im = f_re * b_im + f_im * b_re
    return abar_re, abar_im, bbar_re, bbar_im


def complex_scan(abar_re, abar_im, bu_re, bu_im, reverse):
    length = bu_re.shape[1]
    a_re = jnp.broadcast_to(abar_re, (1, length) + abar_re.shape)
    a_im = jnp.broadcast_to(abar_im, (1, length) + abar_im.shape)

    def combine(e1, e2):
        a1r, a1i, b1r, b1i = e1
        a2r, a2i, b2r, b2i = e2
        return (a1r * a2r - a1i * a2i,
                a1r * a2i + a1i * a2r,
                a2r * b1r - a2i * b1i + b2r,
                a2r * b1i + a2i * b1r + b2i)

    _, _, s_re, s_im = lax.associative_scan(combine, (a_re, a_im, bu_re, bu_im),
                                            reverse=reverse, axis=1)
    return s_re, s_im


def s5_direction(u_ctx, u_lat, a_re, a_im, log_dt, b_re, b_im, c_re, c_im, reverse, need_ctx):
    abr, abi, bbr, bbi = ssm_discretise(a_re, a_im, log_dt, b_re, b_im)

    def drive(u):
        ug = u.reshape(u.shape[:2] + (N_GROUPS, SSM_GROUP))
        return (jnp.einsum('btgc,gcn->btgn', ug, bbr), jnp.einsum('btgc,gcn->btgn', ug, bbi))

    def readout(s_re, s_im):
        y = jnp.einsum('btgn,gcn->btgc', s_re, c_re) - jnp.einsum('btgn,gcn->btgc', s_im, c_im)
        return y.reshape(y.shape[:2] + (E_SSM,))

    bc_re, bc_im = drive(u_ctx)
    sc_re, sc_im = complex_scan(abr, abi, bc_re, bc_im, reverse)
    edge = 0 if reverse else -1
    fin_re, fin_im = sc_re[:, edge], sc_im[:, edge]
    bl_re, bl_im = drive(u_lat)
    first = -1 if reverse else 0
    bl_re = bl_re.at[:, first].add(abr * fin_re - abi * fin_im)
    bl_im = bl_im.at[:, first].add(abr * fin_im + abi * fin_re)
    sl_re, sl_im = complex_scan(abr, abi, bl_re, bl_im, reverse)
    y_ctx = readout(sc_re, sc_im) if need_ctx else None
    return y_ctx, readout(sl_re, sl_im)


def depthwise_conv(v, w, b):
    out = lax.conv_general_dilated(v, w[:, None, :], window_strides=(1,),
                                   padding=((CONV_W // 2, CONV_W // 2),),
                                   dimension_numbers=('NWC', 'WIO', 'NWC'),
                                   feature_group_count=v.shape[-1])
    return out + b


def branch_merge(p, y_scan, gate, lp, rows):
    u_a, z_a, v_b, g_b, z_b, r_a, r_b = p
    y = y_scan + lp['ssm_d'] * u_a
    y = y * jax.nn.sigmoid(jax.nn.gelu(y) @ lp['glu_w'] + lp['glu_b'])
    y_a = (y * jax.nn.silu(z_a)) @ lp['ssm_proj']
    v = v_b * jax.nn.sigmoid(g_b)
    bsz, length, _ = v.shape
    if rows is None:
        v = depthwise_conv(v, lp['conv_w'], lp['conv_b'])
    else:
        v = depthwise_conv(v.reshape(bsz * rows, GRID_W, E_CONV), lp['conv_w'], lp['conv_b'])
        v = v.reshape(bsz, length, E_CONV)
    v = jax.nn.silu(layernorm(v, lp['conv_ln_g'], lp['conv_ln_b']))
    y_b = (v * jax.nn.silu(z_b)) @ lp['conv_proj']
    o = (jax.nn.sigmoid(r_a) * y_a + jax.nn.sigmoid(r_b) * y_b) @ lp['w_out']
    return gate * rmsnorm(o, lp['post_g'])


def setup_inputs(seed: int = 0) -> dict:
    key = jax.random.key(seed)
    ks = jax.random.split(key, 26)
    nrm = jax.random.normal
    f32 = jnp.float32
    G, C, N = N_GROUPS, SSM_GROUP, N_STATE
    d = D_MODEL
    return {
        'x': nrm(ks[0], (BATCH, SEQ, d), f32),
        'c': nrm(ks[1], (BATCH, d), f32),
        'ctx': nrm(ks[2], (BATCH, CTX_LEN, d), f32),
        'c_ctx': nrm(ks[3], (d,), f32),
        'mod_w': 0.5 * d ** -0.5 * nrm(ks[4], (DEPTH, d, 3 * d), f32),
        'mod_b': 0.01 * nrm(ks[5], (DEPTH, 3 * d), f32),
        'pre_g': 1.0 + 0.01 * nrm(ks[6], (DEPTH, d), f32),
        'post_g': 1.0 + 0.01 * nrm(ks[7], (DEPTH, d), f32),
        'w_in': d ** -0.5 * nrm(ks[8], (DEPTH, d, D_IN), f32),
        'ssm_a_re': -0.5 + 0.01 * nrm(ks[9], (DEPTH, 2, G, N), f32),
        'ssm_a_im': math.pi * jnp.arange(N, dtype=f32) + 0.01 * nrm(ks[10], (DEPTH, 2, G, N), f32),
        'ssm_log_dt': jax.random.uniform(ks[11], (DEPTH, 2, G), f32,
                                         minval=math.log(DT_MIN), maxval=math.log(DT_MAX)),
        'ssm_b_re': C ** -0.5 * nrm(ks[12], (DEPTH, 2, G, C, N), f32),
        'ssm_b_im': C ** -0.5 * nrm(ks[13], (DEPTH, 2, G, C, N), f32),
        'ssm_c_re': N ** -0.5 * nrm(ks[14], (DEPTH, 2, G, C, N), f32),
        'ssm_c_im': N ** -0.5 * nrm(ks[15], (DEPTH, 2, G, C, N), f32),
        'ssm_d': nrm(ks[16], (DEPTH, E_SSM), f32),
        'glu_w': E_SSM ** -0.5 * nrm(ks[17], (DEPTH, E_SSM, E_SSM), f32),
        'glu_b': 0.01 * nrm(ks[18], (DEPTH, E_SSM), f32),
        'ssm_proj': E_SSM ** -0.5 * nrm(ks[19], (DEPTH, E_SSM, d), f32),
        'conv_w': CONV_W ** -0.5 * nrm(ks[20], (DEPTH, CONV_W, E_CONV), f32),
        'conv_b': 0.01 * nrm(ks[21], (DEPTH, E_CONV), f32),
        'conv_ln_g': 1.0 + 0.01 * nrm(ks[22], (DEPTH, E_CONV), f32),
        'conv_ln_b': 0.01 * nrm(ks[23], (DEPTH, E_CONV), f32),
        'conv_proj': E_CONV ** -0.5 * nrm(ks[24], (DEPTH, E_CONV, d), f32),
        'w_out': d ** -0.5 * nrm(ks[25], (DEPTH, d, d), f32),
    }


def reference(x, c, ctx, c_ctx, mod_w, mod_b, pre_g, post_g, w_in, ssm_a_re, ssm_a_im, ssm_log_dt,
              ssm_b_re, ssm_b_im, ssm_c_re, ssm_c_im, ssm_d, glu_w, glu_b, ssm_proj,
              conv_w, conv_b, conv_ln_g, conv_ln_b, conv_proj, w_out):
    rows = x.shape[1] // GRID_W
    silu_c = jax.nn.silu(c)
    silu_cc = jax.nn.silu(c_ctx)
    x_lat, x_ctx = x, ctx
    for l in range(DEPTH):
        last = l == DEPTH - 1
        lp = {'ssm_d': ssm_d[l], 'glu_w': glu_w[l], 'glu_b': glu_b[l], 'ssm_proj': ssm_proj[l],
              'conv_w': conv_w[l], 'conv_b': conv_b[l], 'conv_ln_g': conv_ln_g[l],
              'conv_ln_b': conv_ln_b[l], 'conv_proj': conv_proj[l], 'w_out': w_out[l],
              'post_g': post_g[l]}
        mod_lat = (silu_c @ mod_w[l] + mod_b[l])[:, None, :]
        mod_ctx = silu_cc @ mod_w[l] + mod_b[l]
        sh_l, sc_l, gt_l = jnp.split(mod_lat, 3, axis=-1)
        sh_c, sc_c, gt_c = jnp.split(mod_ctx, 3, axis=-1)
        p_lat = in_proj(x_lat, pre_g[l], sh_l, sc_l, w_in[l])
        p_ctx = in_proj(x_ctx, pre_g[l], sh_c, sc_c, w_in[l])
        y_lat = jnp.zeros_like(p_lat[0])
        y_ctx = jnp.zeros_like(p_ctx[0])
        for dr, rev in ((0, False), (1, True)):
            yc, yl = s5_direction(p_ctx[0], p_lat[0], ssm_a_re[l, dr], ssm_a_im[l, dr],
                                  ssm_log_dt[l, dr], ssm_b_re[l, dr], ssm_b_im[l, dr],
                                  ssm_c_re[l, dr], ssm_c_im[l, dr], rev, not last)
            y_lat = y_lat + yl
            if not last:
                y_ctx = y_ctx + yc
        new_lat = x_lat + branch_merge(p_lat, y_lat, gt_l, lp, rows)
        if not last:
            x_ctx = x_ctx + branch_merge(p_ctx, y_ctx, gt_c, lp, None)
        x_lat = new_lat
    return x_lat
```

```python
import numpy as np
import concourse.bass as bass
import concourse.mybir as mybir
from concourse.bass_utils import run_bass_kernel_spmd

F32 = mybir.dt.float32
BF16 = mybir.dt.bfloat16
U8 = mybir.dt.uint8
ALU = mybir.AluOpType
AF = mybir.ActivationFunctionType

ENGS = ("pe", "act", "dve", "pool", "sp")
EPS = 1e-6
MAGIC = 12582912.0
TWO_PI = float(2 * np.pi)


class Buf:
    __slots__ = ("name", "w", "readers")

    def __init__(self, name=""):
        self.name = name
        self.w = None
        self.readers = {}


class Op:
    __slots__ = ("eng", "fn", "deps", "signal", "sem", "cnt", "is_dma")

    def __init__(self, eng, fn, is_dma=False):
        self.eng = eng
        self.fn = fn
        self.deps = []
        self.signal = False
        self.sem = None
        self.cnt = 0
        self.is_dma = is_dma


class Sched:
    def __init__(self, nc):
        self.nc = nc
        self.ops = {e: [] for e in ENGS}
        self.all_ops = []
        self.esem = {e: nc.alloc_semaphore("sem_" + e) for e in ("pe", "act", "dve", "pool")}
        self.dma_sems = []
        self.last_dma = {}
        self.last_op = {e: None for e in ENGS}

    RR_N = {"sp": 84, "pool": 2}

    def dma_sem(self, queue="sp"):
        if not hasattr(self, "rr"):
            self.rr = {"sp": [[], 0], "pool": [[], 0]}
        lst, i = self.rr[queue]
        if len(lst) < self.RR_N[queue]:
            h = self.nc.alloc_semaphore("dsem%d" % len(self.dma_sems))
            self.dma_sems.append([h, queue])
            lst.append(len(self.dma_sems) - 1)
        idx = lst[i % self.RR_N[queue]]
        self.rr[queue][1] = i + 1
        return idx

    def _place(self, o):
        self.ops[o.eng].append(o)
        self.all_ops.append(o)

    def _tick(self, eng, flush_all=False):
        pend = self.pending.get(eng)
        if not pend:
            return
        keep = []
        for item in pend:
            item[0] -= 1
            if item[0] < 0 or flush_all:
                self._place(item[1])
            else:
                keep.append(item)
        self.pending[eng] = keep

    def op(self, eng, fn, reads=(), writes=(), dsem=None, defer=0):
        if not hasattr(self, "pending"):
            self.pending = {}
        o = Op(eng, fn, is_dma=dsem is not None)
        if dsem is not None:
            assert self.dma_sems[dsem][1] == eng
            o.sem = dsem
        deps = {}
        for b in reads:
            if b.w is not None:
                deps[id(b.w)] = b.w
        for b in writes:
            if b.w is not None:
                deps[id(b.w)] = b.w
            for r in b.readers.values():
                deps[id(r)] = r
        for d in deps.values():
            if (not d.is_dma) and (not o.is_dma) and d.eng == "pe" and eng == "pe":
                continue
            d.signal = True
            o.deps.append(d)
        key = ("d", o.sem) if o.is_dma else eng
        for b in reads:
            b.readers[key] = o
        for b in writes:
            b.w = o
            b.readers = {}
        if defer > 0:
            self.pending.setdefault(eng, []).append([defer, o])
        else:
            self._tick(eng)
            self._place(o)
        if o.is_dma:
            o.signal = True
            self.last_dma[dsem] = o
        else:
            self.last_op[eng] = o
        return o

    def barrier(self):
        for e in list(getattr(self, "pending", {})):
            self._tick(e, flush_all=True)
        lasts = [o for o in self.last_op.values() if o is not None and not o.is_dma]
        lasts += list(self.last_dma.values())
        for e in ENGS:
            o = Op(e, None)
            for d in lasts:
                if (not d.is_dma) and d.eng == e and e == "pe":
                    continue
                d.signal = True
                o.deps.append(d)
            self.ops[e].append(o)
            self.all_ops.append(o)

    def emit(self):
        nc = self.nc
        cnt = {e: 0 for e in self.esem}
        dcnt = [0] * len(self.dma_sems)
        for o in self.all_ops:
            if o.is_dma:
                dcnt[o.sem] += 16
                o.cnt = dcnt[o.sem]
            elif o.fn is not None and o.signal:
                cnt[o.eng] += 1
                o.cnt = cnt[o.eng]
        self.final_counts = (cnt, dcnt)

        def run(ename, e):
            known = {}
            for o in self.ops[ename]:
                need = {}
                for d in o.deps:
                    if d.is_dma:
                        key = ("d", d.sem)
                        h = self.dma_sems[d.sem][0]
                    else:
                        key = d.eng
                        h = self.esem[d.eng]
                    if d.cnt > need.get(key, (None, 0))[1]:
                        need[key] = (h, d.cnt)
                for key, (h, c) in need.items():
                    if known.get(key, 0) < c:
                        e.wait_ge(h, c)
                        known[key] = c
                if o.fn is None:
                    continue
                ins = o.fn(e)
                if o.is_dma:
                    ins.then_inc(self.dma_sems[o.sem][0], 16)
                elif o.signal:
                    ins.then_inc(self.esem[ename], 1)

        with nc.Block() as block:
            @block.sync
            def _(e):
                run("sp", e)

            @block.tensor
            def _(e):
                run("pe", e)

            @block.scalar
            def _(e):
                run("act", e)

            @block.vector
            def _(e):
                run("dve", e)

            @block.gpsimd
            def _(e):
                run("pool", e)


class Arena:
    def __init__(self, nc, nbytes):
        self.t = nc.alloc_sbuf_tensor("arena", [128, nbytes], U8)
        self.nbytes = nbytes
        self.off = 0
        self.stack = []

    def push(self):
        self.stack.append(self.off)

    def pop(self):
        self.off = self.stack.pop()

    def alloc(self, free, dt):
        if isinstance(free, int):
            free = (free,)
        esz = 4 if dt == F32 else 2
        n = int(np.prod(free))
        off = (self.off + 63) // 64 * 64
        assert off + n * esz <= self.nbytes, "arena overflow %d" % (off + n * esz)
        ap = self.t[:, off:off + n * esz].bitcast(dt)
        if len(free) == 2:
            ap = ap.rearrange("p (a b) -> p a b", a=free[0])
        elif len(free) == 3:
            ap = ap.rearrange("p (a b c) -> p a b c", a=free[0], b=free[1])
        self.off = off + n * esz
        return ap


class Rot:
    def __init__(self, A, n, free, dt, S=None, dma_queue=None):
        self.items = []
        for i in range(n):
            ap = A.alloc(free, dt)
            ds = S.dma_sem(dma_queue) if dma_queue else None
            self.items.append((ap, Buf(), ds))
        self.i = 0

    def next(self):
        it = self.items[self.i % len(self.items)]
        self.i += 1
        return it


def make_cfg(D=2048, LAT=8192, CTX=256, DEPTH=4):
    c = dict(D=D, LAT=LAT, CTX=CTX, DEPTH=DEPTH)
    c["NCT"] = D // 128
    c["G"] = D // 16
    c["NTOK"] = CTX + LAT
    c["NK"] = c["NTOK"] // 8
    c["NKC"] = CTX // 8
    return c


N_ROWS = 7 + 3 + 31
R_PRE, R_POST, R_SSD, R_GLUB, R_CONVB, R_LNG, R_LNB, R_MODB, R_CONVW = 0, 1, 2, 3, 4, 5, 6, 7, 10


def make_consts(cfg):
    NK, NKC = cfg["NK"], cfg["NKC"]
    c = {}
    c["ident"] = np.eye(128, dtype=np.float32)
    c["ones"] = np.ones((128, 128), np.float32)
    c["pswap"] = np.roll(np.eye(128, dtype=np.float32), 64, axis=1)
    sel = np.zeros((8, 8, 128, 128), np.float32)
    selT = np.zeros((8, 8, 128, 128), np.float32)
    for gl in range(8):
        for j in range(8):
            for cc in range(16):
                sel[gl, j, gl * 16 + cc, j * 16 + cc] = 1.0
                selT[gl, j, j * 16 + cc, gl * 16 + cc] = 1.0
    c["sel"] = np.ascontiguousarray(sel.reshape(64, 128, 128).transpose(1, 0, 2).reshape(128, 64 * 128))
    c["selT"] = np.ascontiguousarray(selT.reshape(64, 128, 128).transpose(1, 0, 2).reshape(128, 64 * 128))
    jj = np.arange(128) // 16
    mf = (jj[None, :] >= jj[:, None]).astype(np.float32)
    mr = (jj[:, None] >= jj[None, :]).astype(np.float32)
    c["maskf"] = np.tile(mf, (1, 4))
    c["maskr"] = np.tile(mr, (1, 4))
    k = np.arange(NK)
    tauf = k.astype(np.float32)
    taur = np.where(k < NKC, NKC - 1 - k, NKC + (NK - 1 - k)).astype(np.float32)
    c["tau"] = np.ascontiguousarray(np.broadcast_to(np.stack([tauf, taur])[None], (128, 2, NK)).reshape(128, 2 * NK))
    ms = np.arange(-7, 9).astype(np.float32)
    c["mvec"] = np.ascontiguousarray(np.broadcast_to(np.concatenate([ms / (2 * np.pi), ms])[None], (128, 32))).astype(np.float32)
    sg = np.ones((128, 2), np.float32)
    sg[64:, 0] = -1.0
    sg[:64, 1] = -1.0
    c["sgn"] = sg
    return c


def build(cfg, dbg=False, stages=("pro", "layers", "all", "epi"), nlayers=None):
    D, LAT, CTX, DEPTH = cfg["D"], cfg["LAT"], cfg["CTX"], cfg["DEPTH"]
    NCT, G, NTOK, NK, NKC = cfg["NCT"], cfg["G"], cfg["NTOK"], cfg["NK"], cfg["NKC"]
    DIN = 7 * D
    if nlayers is None:
        nlayers = DEPTH
    nc = bass.Bass("TRN2", target_bir_lowering=False)
    S = Sched(nc)

    def din(name, shape, dt=F32):
        return nc.dram_tensor(name, list(shape), dt, kind="ExternalInput").ap()

    def dscr(name, shape, dt, out=False):
        kind = "ExternalOutput" if (out or dbg) else "Internal"
        return nc.dram_tensor(name, list(shape), dt, kind=kind).ap()

    x_in = din("x", [LAT, D])
    c_in = din("c", [1, D])
    ctx_in = din("ctx", [CTX, D])
    cctx_in = din("c_ctx", [1, D])
    mod_w = din("mod_w", [DEPTH, D, 3 * D])
    mod_b = din("mod_b", [DEPTH, 3, D])
    vec_in = {n: din(n, [DEPTH, 1, D]) for n in
              ("pre_g", "post_g", "ssm_d", "glu_b", "conv_b", "conv_ln_g", "conv_ln_b")}
    conv_w = din("conv_w", [DEPTH, 31, D])
    w_in = din("w_in", [DEPTH, D, DIN])
    wsq = {n: din(n, [DEPTH, D, D]) for n in ("glu_w", "ssm_proj", "conv_proj", "w_out")}
    a_re = din("ssm_a_re", [DEPTH, 2 * G, 64])
    a_im = din("ssm_a_im", [DEPTH, 2 * G, 64])
    log_dt = din("ssm_log_dt", [DEPTH, 1, 2 * G])
    bc_in = {n: din(n, [DEPTH, 2, G * 16, 64]) for n in ("ssm_b_re", "ssm_b_im", "ssm_c_re", "ssm_c_im")}
    cst = {k: din("cst_" + k, v.shape) for k, v in make_consts(cfg).items()}
    out = nc.dram_tensor("out", [LAT, D], F32, kind="ExternalOutput").ap()

    xT = dscr("xT", [D, NTOK], F32)
    Pd = dscr("Pd", [DIN, NTOK], BF16)
    Yd = dscr("Yd", [D, NTOK], BF16)
    MMd = dscr("MMd", [G, 128, 9 * 128], BF16)
    DGd = dscr("DGd", [NCT, 128, 31 * 128], BF16)
    wb_in = dscr("wb_in", [DEPTH, 7 * NCT, 128, NCT * 128], BF16)
    wb_sq = {n: dscr("wb_" + n, [DEPTH, NCT, 128, NCT * 128], BF16) for n in wsq}

    A = Arena(nc, 190 * 1024)
    A.S = S
    PS = [(nc.alloc_psum_tensor("ps%d" % i, [128, 512], F32), Buf("ps%d" % i)) for i in range(8)]

    ident = A.alloc(128, F32)
    ones = A.alloc(128, F32)
    eps_t = A.alloc(1, F32)
    b_const = Buf("const")
    ds_c = S.dma_sem("sp")
    S.op("sp", lambda e: e.dma_start(out=ident, in_=cst["ident"]), writes=[b_const], dsem=ds_c)
    S.op("sp", lambda e: e.dma_start(out=ones, in_=cst["ones"]), writes=[b_const], dsem=ds_c)
    S.op("dve", lambda e: e.memset(eps_t, EPS), writes=[b_const])
    vecT = A.alloc((NCT, N_ROWS), F32)
    b_vec = Buf("vecT")
    modv = A.alloc((3 * NCT, 2), F32)
    gmod = A.alloc((NCT, 2), F32)
    gpost = A.alloc((NCT, 2), F32)
    b_mod = Buf("mod")
    scT = A.alloc((NCT, 2), F32)
    b_sc = Buf("scT")
    Q2 = 2 * G
    identb = A.alloc(128, BF16)
    pswapb = A.alloc(128, BF16)
    sgn_t = A.alloc(2, F32)
    r8T = A.alloc(Q2, F32)
    fCT = A.alloc(Q2, F32)
    fST = A.alloc(Q2, F32)
    b_scan = Buf("scanpar")
    S.op("dve", lambda e: e.tensor_copy(out=identb, in_=ident), reads=[b_const], writes=[b_const])
    S.op("sp", lambda e: e.dma_start(out=sgn_t, in_=cst["sgn"]), writes=[b_const], dsem=ds_c)
    _pw = A.alloc(128, F32)
    S.op("sp", lambda e: e.dma_start(out=_pw, in_=cst["pswap"]), writes=[b_const], dsem=ds_c)
    S.op("dve", lambda e: e.tensor_copy(out=pswapb, in_=_pw), reads=[b_const], writes=[b_const])

    dbg_sem = [None]

    def dump(name, ap, buf, shape, dt=F32):
        if not dbg:
            return
        if dbg_sem[0] is None:
            dbg_sem[0] = S.dma_sem("sp")
        dd = nc.dram_tensor("dbg_" + name, list(shape), dt, kind="ExternalOutput").ap()
        S.op("sp", lambda e: e.dma_start(out=dd, in_=ap), reads=[buf], dsem=dbg_sem[0])

    cast_rr = [0]

    def cast_eng():
        cast_rr[0] += 1
        return ("act", "dve", "pool")[cast_rr[0] % 3]

    def copy_op(eng, out_ap, in_ap, reads, writes):
        if eng == "act":
            return S.op("act", lambda e: e.activation(out=out_ap, in_=in_ap, func=AF.Copy), reads=reads, writes=writes)
        return S.op(eng, lambda e: e.tensor_copy(out=out_ap, in_=in_ap), reads=reads, writes=writes)

    def stage_prologue():
        A.push()
        xin_r = Rot(A, 2, D, F32, S, "sp")
        xo_r = Rot(A, 2, (NCT, 128), F32, S, "sp")
        xT_v = xT.rearrange("(ct p) t -> p ct t", p=128)
        nblk = NTOK // 128
        for tb in range(nblk):
            t0 = tb * 128
            src = ctx_in[t0:t0 + 128, :] if t0 < CTX else x_in[t0 - CTX:t0 - CTX + 128, :]
            xin, bxin, dsx = xin_r.next()
            S.op("sp", lambda e, xin=xin, src=src: e.dma_start(out=xin, in_=src), writes=[bxin], dsem=dsx)
            xo, bxo, dso = xo_r.next()
            for q in range((NCT + 3) // 4):
                ps, bps = PS[q % 2]
                nq = min(4, NCT - 4 * q)
                for i in range(nq):
                    ct = 4 * q + i
                    S.op("pe", lambda e, ps=ps, xin=xin, i=i, ct=ct: e.transpose(
                        ps[:, i * 128:(i + 1) * 128], xin[:, ct * 128:(ct + 1) * 128], ident),
                        reads=[bxin, b_const], writes=[bps])
                copy_op(("act", "dve")[q % 2], xo[:, 4 * q:4 * q + nq, :],
                        ps[:, 0:nq * 128].rearrange("p (a b) -> p a b", a=nq), [bps], [bxo])
            S.op("sp", lambda e, xo=xo, t0=t0: e.dma_start(out=xT_v[:, :, t0:t0 + 128], in_=xo),
                 reads=[bxo], dsem=dso, defer=1)
        S.barrier()
        A.pop()
        A.push()
        wl_r = Rot(A, 3, (NCT, 128), F32, S, "sp")
        wc_r = Rot(A, 3, NCT * 128, BF16, S, "sp")

        def cast_weight(src2d, dst3d, ncols):
            sv = src2d.rearrange("(kc p) c -> p kc c", p=128)
            for co in range(ncols // 128):
                wl, bwl, dsl = wl_r.next()
                S.op("sp", lambda e, wl=wl, co=co, sv=sv: e.dma_start(out=wl, in_=sv[:, :, co * 128:(co + 1) * 128]),
                     writes=[bwl], dsem=dsl)
                wc, bwc, dsc = wc_r.next()
                copy_op(cast_eng(), wc, wl.rearrange("p a b -> p (a b)"), [bwl], [bwc])
                S.op("sp", lambda e, wc=wc, co=co, dst3d=dst3d: e.dma_start(out=dst3d[co], in_=wc),
                     reads=[bwc], dsem=dsc, defer=2)

        for l in range(nlayers):
            cast_weight(w_in[l], wb_in[l], DIN)
            for n in wsq:
                cast_weight(wsq[n][l], wb_sq[n][l], D)
        S.barrier()
        A.pop()
        A.push()
        crow = A.alloc(D, F32)
        b_crow = Buf()
        ds = S.dma_sem("sp")
        S.op("sp", lambda e: e.dma_start(out=crow[0:1, :], in_=c_in), writes=[b_crow], dsem=ds)
        S.op("sp", lambda e: e.dma_start(out=crow[1:2, :], in_=cctx_in), writes=[b_crow], dsem=ds)
        ps, bps = PS[2]
        for ct in range(NCT):
            S.op("pe", lambda e, ct=ct: e.transpose(ps[:, 2 * ct:2 * ct + 2], crow[0:2, ct * 128:(ct + 1) * 128],
                                                    ident[0:2, 0:2]), reads=[b_crow, b_const], writes=[bps])
        S.op("act", lambda e: e.activation(out=scT.rearrange("p a b -> p (a b)"), in_=ps[:, 0:2 * NCT], func=AF.Silu),
             reads=[bps], writes=[b_sc])
        A.pop()
        S.barrier()

    def stage_mod(l):
        A.push()
        rows = A.alloc(D, F32)
        b_rows = Buf()
        ds = S.dma_sem("sp")
        names = ("pre_g", "post_g", "ssm_d", "glu_b", "conv_b", "conv_ln_g", "conv_ln_b")
        for i, n in enumerate(names):
            S.op("sp", lambda e, i=i, n=n: e.dma_start(out=rows[i:i + 1, :], in_=vec_in[n][l]), writes=[b_rows], dsem=ds)
        S.op("sp", lambda e: e.dma_start(out=rows[R_MODB:R_MODB + 3, :], in_=mod_b[l]), writes=[b_rows], dsem=ds)
        S.op("sp", lambda e: e.dma_start(out=rows[R_CONVW:R_CONVW + 31, :], in_=conv_w[l]), writes=[b_rows], dsem=ds)
        for ct in range(NCT):
            ps, bps = PS[ct % 2]
            S.op("pe", lambda e, ct=ct, ps=ps: e.transpose(ps[:, 0:N_ROWS], rows[0:N_ROWS, ct * 128:(ct + 1) * 128],
                                                           ident[0:N_ROWS, 0:N_ROWS]),
                 reads=[b_rows, b_const], writes=[bps])
            copy_op(("act", "dve")[ct % 2], vecT[:, ct, :], ps[:, 0:N_ROWS], [bps], [b_vec])
        mw_r = Rot(A, 3, (NCT, 128), F32, S, "sp")
        mwv = mod_w[l].rearrange("(kc p) c -> p kc c", p=128)
        psm, bpsm = PS[2]
        for j in range(3 * NCT):
            mw, bmw, dsm = mw_r.next()
            S.op("sp", lambda e, mw=mw, j=j: e.dma_start(out=mw, in_=mwv[:, :, j * 128:(j + 1) * 128]),
                 writes=[bmw], dsem=dsm)
            for kc in range(NCT):
                S.op("pe", lambda e, mw=mw, j=j, kc=kc: e.matmul(psm[:, 2 * j:2 * j + 2], lhsT=mw[:, kc, :],
                                                                  rhs=scT[:, kc, :], start=(kc == 0), stop=(kc == NCT - 1)),
                     reads=[bmw, b_sc], writes=[bpsm])
        psv = psm[:, 0:6 * NCT].rearrange("p (j v) -> p j v", v=2)
        for r in range(3):
            for v in range(2):
                S.op("dve", lambda e, r=r, v=v: e.tensor_tensor(out=modv[:, r * NCT:(r + 1) * NCT, v],
                                                                in0=psv[:, r * NCT:(r + 1) * NCT, v],
                                                                in1=vecT[:, :, R_MODB + r], op=ALU.add),
                     reads=[bpsm, b_vec], writes=[b_mod])
        for v in range(2):
            S.op("dve", lambda e, v=v: e.scalar_tensor_tensor(out=gmod[:, :, v], in0=modv[:, NCT:2 * NCT, v], scalar=1.0,
                                                              in1=vecT[:, :, R_PRE], op0=ALU.add, op1=ALU.mult),
                 reads=[b_mod, b_vec], writes=[b_mod])
            S.op("dve", lambda e, v=v: e.tensor_tensor(out=gpost[:, :, v], in0=modv[:, 2 * NCT:3 * NCT, v],
                                                       in1=vecT[:, :, R_POST], op=ALU.mult),
                 reads=[b_mod, b_vec], writes=[b_mod])
        A.pop()
        S.barrier()

    tiles = []
    t = 0
    while t < CTX:
        n = min(512, CTX - t)
        tiles.append((t, n, 1))
        t += n
    while t < NTOK:
        n = min(512, NTOK - t)
        tiles.append((t, n, 0))
        t += n

    PART_FUNC = [AF.Copy, AF.Silu, AF.Copy, AF.Sigmoid, AF.Silu, AF.Sigmoid, AF.Sigmoid]

    def stage_A(l):
        A.push()
        xt_r = Rot(A, 2, (NCT, 512), F32, S, "sp")
        hT_r = Rot(A, 2, (NCT, 512), BF16)
        sq_r = Rot(A, 2, 512, F32)
        tmp_r = Rot(A, 2, 512, F32)
        rstd_r = Rot(A, 2, 512, F32)
        w_r = Rot(A, 3, NCT * 128, BF16, S, "sp")
        ev_r = Rot(A, 4, 512, BF16, S, "sp")
        xT_v = xT.rearrange("(ct p) t -> p ct t", p=128)
        psi = [0]
        for (t0, n, v) in tiles:
            xt, bxt, dsx = xt_r.next()
            S.op("sp", lambda e, xt=xt, t0=t0, n=n: e.dma_start(out=xt[:, :, 0:n], in_=xT_v[:, :, t0:t0 + n]),
                 writes=[bxt], dsem=dsx)
            pst, bpst = PS[7]
            for ct in range(NCT):
                sq, bsq, _ = sq_r.next()
                S.op("act", lambda e, sq=sq, xt=xt, ct=ct, n=n: e.activation(out=sq[:, 0:n], in_=xt[:, ct, 0:n], func=AF.Square),
                     reads=[bxt], writes=[bsq])
                S.op("pe", lambda e, sq=sq, ct=ct, n=n: e.matmul(pst[:, 0:n], lhsT=ones, rhs=sq[:, 0:n], start=(ct == 0),
                                                                  stop=(ct == NCT - 1)), reads=[bsq, b_const], writes=[bpst])
            rstd, brs, _ = rstd_r.next()
            S.op("act", lambda e, rstd=rstd, n=n: e.activation(out=rstd[:, 0:n], in_=pst[:, 0:n], func=AF.Sqrt, bias=eps_t[:, 0:1],
                                                               scale=1.0 / D), reads=[bpst, b_const], writes=[brs])
            S.op("dve", lambda e, rstd=rstd, n=n: e.reciprocal(out=rstd[:, 0:n], in_=rstd[:, 0:n]), reads=[brs], writes=[brs])
            hT, bhT, _ = hT_r.next()
            for ct in range(NCT):
                tmp, btmp, _ = tmp_r.next()
                S.op("dve", lambda e, tmp=tmp, xt=xt, ct=ct, rstd=rstd, n=n: e.tensor_tensor(
                    out=tmp[:, 0:n], in0=xt[:, ct, 0:n], in1=rstd[:, 0:n], op=ALU.mult), reads=[bxt, brs], writes=[btmp])
                S.op("act", lambda e, tmp=tmp, hT=hT, ct=ct, n=n, v=v: e.activation(
                    out=hT[:, ct, 0:n], in_=tmp[:, 0:n], func=AF.Identity, scale=gmod[:, ct, v:v + 1],
                    bias=modv[:, ct, v:v + 1]), reads=[btmp, b_mod], writes=[bhT])
            for co in range(7 * NCT):
                w, bw, dsw = w_r.next()
                S.op("sp", lambda e, w=w, co=co: e.dma_start(out=w, in_=wb_in[l, co]), writes=[bw], dsem=dsw)
                ps, bps = PS[psi[0] % 4]
                psi[0] += 1
                for kc in range(NCT):
                    S.op("pe", lambda e, ps=ps, w=w, hT=hT, kc=kc, n=n: e.matmul(
                        ps[:, 0:n], lhsT=w[:, kc * 128:(kc + 1) * 128], rhs=hT[:, kc, 0:n], start=(kc == 0),
                        stop=(kc == NCT - 1)), reads=[bw, bhT], writes=[bps])
                ev, bev, dse = ev_r.next()
                fn = PART_FUNC[co // NCT]
                if fn == AF.Copy:
                    S.op("dve", lambda e, ev=ev, ps=ps, n=n: e.tensor_copy(out=ev[:, 0:n], in_=ps[:, 0:n]), reads=[bps], writes=[bev])
                else:
                    S.op("act", lambda e, ev=ev, ps=ps, n=n, fn=fn: e.activation(out=ev[:, 0:n], in_=ps[:, 0:n], func=fn),
                         reads=[bps], writes=[bev])
                S.op("sp", lambda e, ev=ev, co=co, t0=t0, n=n: e.dma_start(out=Pd[co * 128:(co + 1) * 128, t0:t0 + n],
                                                                            in_=ev[:, 0:n]), reads=[bev], dsem=dse, defer=2)
        A.pop()
        S.barrier()

    MI = lambda m: m + 7

    def tt(eng, out_ap, in0, in1, op, reads, writes):
        return S.op(eng, lambda e: e.tensor_tensor(out=out_ap, in0=in0, in1=in1, op=op), reads=reads, writes=writes)

    def ts(eng, out_ap, in0, s1, s2, op0, op1, reads, writes):
        if op1 is None:
            return S.op(eng, lambda e: e.tensor_scalar(out=out_ap, in0=in0, scalar1=s1, scalar2=None, op0=op0),
                        reads=reads, writes=writes)
        return S.op(eng, lambda e: e.tensor_scalar(out=out_ap, in0=in0, scalar1=s1, scalar2=s2, op0=op0, op1=op1),
                    reads=reads, writes=writes)

    def act(out_ap, in_ap, func, reads, writes, scale=1.0, bias=None):
        if bias is None:
            return S.op("act", lambda e: e.activation(out=out_ap, in_=in_ap, func=func, scale=scale), reads=reads, writes=writes)
        return S.op("act", lambda e: e.activation(out=out_ap, in_=in_ap, func=func, scale=scale, bias=bias),
                    reads=reads, writes=writes)

    def stage_prep(l):
        A.push()
        ds = S.dma_sem("sp")
        maskf_t = A.alloc(512, F32)
        maskr_t = A.alloc(512, F32)
        mvec_t = A.alloc(32, F32)
        b_pc = Buf("prepconst")
        for dst, src in ((maskf_t, cst["maskf"]), (maskr_t, cst["maskr"]), (mvec_t, cst["mvec"])):
            S.op("sp", lambda e, dst=dst, src=src: e.dma_start(out=dst, in_=src), writes=[b_pc], dsem=ds)
        areT = A.alloc(Q2, F32)
        aimT = A.alloc(Q2, F32)
        dtT = A.alloc(Q2, F32)
        b_a = Buf("aT")
        z_r = Rot(A, 4, 128, F32, S, "sp")
        for r0 in range(0, Q2, 128):
            rows = min(128, Q2 - r0)
            for src, dstT, pi_ in ((a_re, areT, 0), (a_im, aimT, 1)):
                z, bz, dsz = z_r.next()
                S.op("sp", lambda e, z=z, src=src, r0=r0, rows=rows: e.dma_start(out=z[0:rows, 0:64], in_=src[l, r0:r0 + rows, :]),
                     writes=[bz], dsem=dsz)
                S.op("sp", lambda e, z=z, src=src, r0=r0, rows=rows: e.dma_start(out=z[0:rows, 64:128], in_=src[l, r0:r0 + rows, :]),
                     writes=[bz], dsem=dsz)
                ps, bps = PS[pi_]
                S.op("pe", lambda e, z=z, ps=ps, rows=rows: e.transpose(ps[:, 0:rows], z[0:rows, :], ident[0:rows, 0:rows]),
                     reads=[bz, b_const], writes=[bps])
                copy_op("dve", dstT[:, r0:r0 + rows], ps[:, 0:rows], [bps], [b_a])
        S.op("sp", lambda e: e.dma_start(out=dtT, in_=log_dt[l].to_broadcast([128, Q2])), writes=[b_a], dsem=ds)
        act(dtT, dtT, AF.Exp, [b_a], [b_a])
        xre = A.alloc(Q2, F32)
        th = A.alloc(Q2, F32)
        tt("dve", xre, dtT, areT, ALU.mult, [b_a], [b_a])
        tt("dve", th, dtT, aimT, ALU.mult, [b_a], [b_a])
        NM = 16
        TS_ = A.alloc((NM, Q2), F32)
        TC_ = A.alloc((NM, Q2), F32)
        R_ = A.alloc((NM, Q2), F32)
        MG = A.alloc((NM, Q2), F32)
        PIMN = A.alloc((NM, Q2), F32)
        b_p = Buf("pow")
        fl = lambda ap: ap.rearrange("p a b -> p (a b)")
        bc_m = lambda v: v.unsqueeze(2).to_broadcast([128, NM, Q2])
        bc_q = lambda v: v.unsqueeze(1).to_broadcast([128, NM, Q2])
        tt("dve", TS_, bc_q(th), bc_m(mvec_t[:, 0:NM]), ALU.mult, [b_a, b_pc], [b_p])
        ts("dve", fl(R_), fl(TS_), MAGIC, -MAGIC, ALU.add, ALU.add, [b_p], [b_p])
        ts("dve", fl(TC_), fl(TS_), 0.25, None, ALU.add, None, [b_p], [b_p])
        tt("dve", fl(TS_), fl(TS_), fl(R_), ALU.subtract, [b_p], [b_p])
        ts("dve", fl(R_), fl(TC_), MAGIC, -MAGIC, ALU.add, ALU.add, [b_p], [b_p])
        tt("dve", fl(TC_), fl(TC_), fl(R_), ALU.subtract, [b_p], [b_p])
        copy_op("dve", fCT, TS_[:, MI(8), :], [b_p], [b_scan])
        ts("dve", fST, TS_[:, MI(8), :], sgn_t[:, 0:1], None, ALU.mult, None, [b_p, b_const], [b_scan])
        act(fl(TS_), fl(TS_), AF.Sin, [b_p], [b_p], scale=TWO_PI)
        act(fl(TC_), fl(TC_), AF.Sin, [b_p], [b_p], scale=TWO_PI)
        tt("dve", MG, bc_q(xre), bc_m(mvec_t[:, NM:2 * NM]), ALU.mult, [b_a, b_pc], [b_p])
        act(fl(MG), fl(MG), AF.Exp, [b_p], [b_p])
        copy_op("dve", r8T, MG[:, MI(8), :], [b_p], [b_scan])
        tt("dve", fl(TS_), fl(TS_), fl(MG), ALU.mult, [b_p], [b_p])
        tt("dve", fl(TC_), fl(TC_), fl(MG), ALU.mult, [b_p], [b_p])
        PIM, PRE = TS_, TC_
        sm = [A.alloc(Q2, F32) for _ in range(6)]
        u_, den, t1_, t2_, fre, fim = sm
        b_f = Buf("f")
        ts("dve", u_, PRE[:, MI(1), :], -1.0, None, ALU.add, None, [b_p], [b_f])
        tt("dve", den, areT, areT, ALU.mult, [b_a], [b_f])
        tt("dve", t1_, aimT, aimT, ALU.mult, [b_a], [b_f])
        tt("dve", den, den, t1_, ALU.add, [b_f], [b_f])
        S.op("dve", lambda e: e.reciprocal(out=den, in_=den), reads=[b_f], writes=[b_f])
        tt("dve", t1_, u_, areT, ALU.mult, [b_f, b_a], [b_f])
        tt("dve", t2_, PIM[:, MI(1), :], aimT, ALU.mult, [b_p, b_a], [b_f])
        tt("dve", t1_, t1_, t2_, ALU.add, [b_f], [b_f])
        tt("dve", fre, t1_, den, ALU.mult, [b_f], [b_f])
        tt("dve", t1_, PIM[:, MI(1), :], areT, ALU.mult, [b_p, b_a], [b_f])
        tt("dve", t2_, u_, aimT, ALU.mult, [b_f, b_a], [b_f])
        tt("dve", t1_, t1_, t2_, ALU.subtract, [b_f], [b_f])
        tt("dve", fim, t1_, den, ALU.mult, [b_f], [b_f])
        QRE = A.alloc((8, Q2), F32)
        QIMS = A.alloc((8, Q2), F32)
        QT = A.alloc((8, Q2), F32)
        b_q = Buf("Q")
        bq8 = lambda v: v.unsqueeze(1).to_broadcast([128, 8, Q2])
        P8re, P8im = PRE[:, MI(0):MI(8), :], PIM[:, MI(0):MI(8), :]
        tt("dve", QRE, bq8(fre), P8re, ALU.mult, [b_f, b_p], [b_q])
        tt("dve", QT, bq8(fim), P8im, ALU.mult, [b_f, b_p], [b_q])
        tt("dve", QRE, QRE, QT, ALU.subtract, [b_q], [b_q])
        tt("dve", QIMS, bq8(fre), P8im, ALU.mult, [b_f, b_p], [b_q])
        tt("dve", QT, bq8(fim), P8re, ALU.mult, [b_f, b_p], [b_q])
        tt("dve", QIMS, QIMS, QT, ALU.add, [b_q], [b_q])
        ts("dve", fl(QIMS), fl(QIMS), sgn_t[:, 1:2], None, ALU.mult, None, [b_q, b_const], [b_q])
        PRES2, PRES1 = R_, MG
        ts("dve", fl(PRES2), fl(PRE), sgn_t[:, 0:1], None, ALU.mult, None, [b_p, b_const], [b_p])
        ts("dve", fl(PRES1), fl(PRE), sgn_t[:, 1:2], None, ALU.mult, None, [b_p, b_const], [b_p])
        ts("dve", fl(PIMN), fl(PIM), -1.0, None, ALU.mult, None, [b_p], [b_p])
        zb_r = Rot(A, 4, 128, F32, S, "sp")
        BC = A.alloc((4, 128), F32)
        b_bc = Buf("BC")
        tA_r = Rot(A, 2, 1024, F32)
        tB_r = Rot(A, 2, 1024, F32)
        t2b_r = Rot(A, 2, (8, 128), BF16)
        tcb_r = Rot(A, 2, (8, 128), BF16)
        mms_r = Rot(A, 1, (8, 9, 128), BF16, S, "sp")
        m1acc = A.alloc((2, 512), F32)
        m1tmp = A.alloc((2, 512), F32)
        b_m1 = Buf("m1acc")
        MMv = MMd.rearrange("g p x -> p g x")
        eng_rr = [0]
        NB8 = min(8, G)
        for g0 in range(0, G, NB8):
            mms, bmms, dsmm = mms_r.next()
            for d in range(2):
                q0 = d * G + g0
                srcs = (("ssm_b_re", "ssm_b_im"), ("ssm_b_im", "ssm_b_re"), ("ssm_c_re", "ssm_c_im"), ("ssm_c_im", "ssm_c_re"))
                ps, bps = PS[2]
                for i, (n0, n1) in enumerate(srcs):
                    z, bz, dsz = zb_r.next()
                    for hh, nm in enumerate((n0, n1)):
                        S.op("sp", lambda e, z=z, nm=nm, hh=hh, d=d, g0=g0: e.dma_start(
                            out=z[:, hh * 64:(hh + 1) * 64], in_=bc_in[nm][l, d, g0 * 16:(g0 + NB8) * 16, :]), writes=[bz], dsem=dsz)
                    S.op("pe", lambda e, z=z, ps=ps, i=i: e.transpose(ps[:, i * 128:(i + 1) * 128], z, ident),
                         reads=[bz, b_const], writes=[bps])
                copy_op("act", fl(BC), ps[:, 0:512], [bps], [b_bc])
                Ba, Bb, Ca, Cb = [BC[:, i, :] for i in range(4)]

                def coef(tab, lo, hi, rev):
                    v = tab[:, lo:hi, q0:q0 + NB8]
                    if rev:
                        v = v[:, ::-1, :]
                    return v.rearrange("p m g -> p g m").unsqueeze(3).to_broadcast([128, NB8, 8, 16])

                def data(X):
                    return X.rearrange("p (g c) -> p g c", c=16).unsqueeze(2).to_broadcast([128, NB8, 8, 16])

                def table(out4, cA, dA, cB, dB, deps_r):
                    eng = ("dve", "pool")[eng_rr[0] % 2]
                    eng_rr[0] += 1
                    tA, btA, _ = tA_r.next()
                    tB, btB, _ = tB_r.next()
                    tA4 = tA.rearrange("p (g j c) -> p g j c", g=NB8, j=8)
                    tB4 = tB.rearrange("p (g j c) -> p g j c", g=NB8, j=8)
                    tt(eng, tA4, cA, dA, ALU.mult, deps_r, [btA])
                    tt(eng, tB4, cB, dB, ALU.mult, deps_r, [btB])
                    return eng, tA4, tB4, btA, btB

                fwd = (d == 0)
                t2b, bt2b, _ = t2b_r.next()
                tcb, btcb, _ = tcb_r.next()
                rd = [b_p, b_q, b_bc]
                eng, tA4, tB4, btA, btB = table(None, coef(QRE, 0, 8, fwd), data(Ba), coef(QIMS, 0, 8, fwd), data(Bb), rd)
                tt(eng, t2b.rearrange("p g (j c) -> p g j c", j=8), tA4, tB4, ALU.add, [btA, btB], [bt2b])
                base = 1 + 4 * d
                lo, hi, rv = MI(1), MI(8) + 1, (not fwd)
                eng, tA4, tB4, btA, btB = table(None, coef(PRES2, lo, hi, rv), data(Ca), coef(PIMN, lo, hi, rv), data(Cb), rd)
                tt(eng, mms[:, :, base + 2, :].rearrange("p g (j c) -> p g j c", j=8), tA4, tB4, ALU.add, [btA, btB], [bmms])
                eng, tA4, tB4, btA, btB = table(None, coef(PIMN, lo, hi, rv), data(Ca), coef(PRES1, lo, hi, rv), data(Cb), rd)
                tt(eng, mms[:, :, base + 3, :].rearrange("p g (j c) -> p g j c", j=8), tA4, tB4, ALU.add, [btA, btB], [bmms])
                lo, hi, rv = MI(-7), MI(0) + 1, (not fwd)
                eng, tA4, tB4, btA, btB = table(None, coef(PRES2, lo, hi, rv), data(Ca), coef(PIMN, lo, hi, rv), data(Cb), rd)
                tt(eng, tcb.rearrange("p g (j c) -> p g j c", j=8), tA4, tB4, ALU.add, [btA, btB], [btcb])
                for h in range((NB8 + 3) // 4):
                    ng = min(4, NB8 - 4 * h)
                    p2, bp2 = PS[3]
                    p2s, bp2s = PS[4]
                    p1, bp1 = PS[5 + (h % 2)]
                    for i in range(ng):
                        gi = 4 * h + i
                        S.op("pe", lambda e, p2=p2, t2b=t2b, gi=gi, i=i: e.matmul(p2[:, i * 128:(i + 1) * 128], lhsT=t2b[:, gi, :],
                                                                              rhs=identb, start=True, stop=True),
                             reads=[bt2b, b_const], writes=[bp2])
                        S.op("pe", lambda e, p2s=p2s, t2b=t2b, gi=gi, i=i: e.matmul(p2s[:, i * 128:(i + 1) * 128], lhsT=t2b[:, gi, :],
                                                                                rhs=pswapb, start=True, stop=True),
                             reads=[bt2b, b_const], writes=[bp2s])
                        S.op("pe", lambda e, p1=p1, t2b=t2b, tcb=tcb, gi=gi, i=i: e.matmul(p1[:, i * 128:(i + 1) * 128], lhsT=t2b[:, gi, :],
                                                                                       rhs=tcb[:, gi, :], start=True, stop=True),
                             reads=[bt2b, btcb], writes=[bp1])
                    copy_op("act", mms[:, 4 * h:4 * h + ng, base + 0, :], p2[:, 0:ng * 128].rearrange("p (g x) -> p g x", g=ng),
                            [bp2], [bmms])
                    copy_op("act", mms[:, 4 * h:4 * h + ng, base + 1, :], p2s[:, 0:ng * 128].rearrange("p (g x) -> p g x", g=ng),
                            [bp2s], [bmms])
                    if fwd:
                        tt("dve", m1acc[:, h, 0:ng * 128], p1[:, 0:ng * 128], maskf_t[:, 0:ng * 128], ALU.mult, [bp1, b_pc], [b_m1])
                    else:
                        tt("dve", m1tmp[:, h, 0:ng * 128], p1[:, 0:ng * 128], maskr_t[:, 0:ng * 128], ALU.mult, [bp1, b_pc], [b_m1])
                        tt("dve", mms[:, 4 * h:4 * h + ng, 0, :], m1acc[:, h, 0:ng * 128].rearrange("p (g x) -> p g x", g=ng),
                           m1tmp[:, h, 0:ng * 128].rearrange("p (g x) -> p g x", g=ng), ALU.add, [b_m1], [bmms])
            S.op("sp", lambda e, mms=mms, g0=g0: e.dma_start(out=MMv[:, g0:g0 + NB8, :], in_=mms.rearrange("p g s x -> p g (s x)")),
                 reads=[bmms], dsem=dsmm, defer=2)
        A.pop()
        S.barrier()

    def stage_S(l):
        A.push()
        ds = S.dma_sem("sp")
        selb = A.alloc((64, 128), BF16)
        selTb = A.alloc((64, 128), BF16)
        tau_t = A.alloc((2, NK), F32)
        b_sel = Buf("sel")
        A.push()
        stg_r = Rot(A, 2, 2048, F32, S, "sp")
        for dst, src in ((selb, cst["sel"]), (selTb, cst["selT"])):
            dflat = dst.rearrange("p a b -> p (a b)")
            for pc in range(4):
                stg, bstg, dss = stg_r.next()
                S.op("sp", lambda e, stg=stg, src=src, pc=pc: e.dma_start(out=stg, in_=src[:, pc * 2048:(pc + 1) * 2048]),
                     writes=[bstg], dsem=dss)
                copy_op(("dve", "act")[pc % 2], dflat[:, pc * 2048:(pc + 1) * 2048], stg, [bstg], [b_sel])
        S.op("sp", lambda e: e.dma_start(out=tau_t.rearrange("p a b -> p (a b)"), in_=cst["tau"]), writes=[b_sel], dsem=ds)
        S.barrier()
        A.pop()
        blocks = [(0, NKC)] + [(k0, min(512, NK - k0)) for k0 in range(NKC, NK, 512)]
        U_r = Rot(A, 1, NTOK, BF16, S, "sp")
        yct_r = Rot(A, 1, NTOK, BF16, S, "sp")
        mm_r = Rot(A, 2, (9, 128), BF16, S, "sp")
        ug_r = Rot(A, 2, NK, BF16)
        yg_all = A.alloc((8, NK), BF16)
        b_yg = [Buf() for _ in range(8)]
        ctab_r = Rot(A, 2, NK, F32)
        stab_r = Rot(A, 2, NK, F32)
        ptt_r = Rot(A, 2, NK, F32)
        pr_r = Rot(A, 2, NK, F32)
        t1_r = Rot(A, 2, 512, F32)
        t2_r = Rot(A, 2, 512, F32)
        lt_r = Rot(A, 2, NK, F32)
        st_r = Rot(A, 2, NK, F32)
        x1_r = Rot(A, 4, NK, BF16)
        x2_r = Rot(A, 4, NK, BF16)
        MMg = MMd.rearrange("g p (s x) -> g p s x", s=9)
        for ct in range(NCT):
            U, bU, dsU = U_r.next()
            S.op("sp", lambda e, U=U, ct=ct: e.dma_start(out=U, in_=Pd[ct * 128:(ct + 1) * 128, :]), writes=[bU], dsem=dsU)
            ngl = min(8, G - 8 * ct)
            for gl in range(ngl):
                g = ct * 8 + gl
                mm, bmm, dsm = mm_r.next()
                S.op("sp", lambda e, mm=mm, g=g: e.dma_start(out=mm, in_=MMg[g]), writes=[bmm], dsem=dsm)
                ug, bug, _ = ug_r.next()
                for bi, (k0, nb) in enumerate(blocks):
                    ps, bps = PS[bi % 2]
                    for j in range(8):
                        S.op("pe", lambda e, ps=ps, gl=gl, j=j, U=U, k0=k0, nb=nb: e.matmul(
                            ps[:, 0:nb], lhsT=selb[:, gl * 8 + j, :], rhs=U[:, 8 * k0 + j:8 * (k0 + nb):8],
                            start=(j == 0), stop=(j == 7)), reads=[b_sel, bU], writes=[bps])
                    copy_op("act", ug[:, k0:k0 + nb], ps[:, 0:nb], [bps], [bug])
                Xs = {}
                for d in range(2):
                    q = d * G + g
                    ctab, bct, _ = ctab_r.next()
                    stab, bst, _ = stab_r.next()
                    ptt, bptt, _ = ptt_r.next()
                    pr, bpr, _ = pr_r.next()
                    tau_d = tau_t[:, d, :]
                    ts("pool", ptt, tau_d, fST[:, q:q + 1], None, ALU.mult, None, [b_sel, b_scan], [bptt])
                    ts("pool", pr, ptt, MAGIC, -MAGIC, ALU.add, ALU.add, [bptt], [bpr])
                    tt("pool", stab, ptt, pr, ALU.subtract, [bptt, bpr], [bst])
                    ts("pool", ptt, tau_d, fCT[:, q:q + 1], 0.25, ALU.mult, ALU.add, [b_sel, b_scan], [bptt])
                    ts("pool", pr, ptt, MAGIC, -MAGIC, ALU.add, ALU.add, [bptt], [bpr])
                    tt("pool", ctab, ptt, pr, ALU.subtract, [bptt, bpr], [bct])
                    act(stab, stab, AF.Sin, [bst], [bst], scale=TWO_PI)
                    act(ctab, ctab, AF.Sin, [bct], [bct], scale=TWO_PI)
                    base = 1 + 4 * d
                    lt, blt, _ = lt_r.next()
                    for bi, (k0, nb) in enumerate(blocks):
                        pL, bpL = PS[2 + bi % 2]
                        pLs, bpLs = PS[4 + bi % 2]
                        S.op("pe", lambda e, pL=pL, mm=mm, ug=ug, k0=k0, nb=nb, base=base: e.matmul(
                            pL[:, 0:nb], lhsT=mm[:, base, :], rhs=ug[:, k0:k0 + nb], start=True, stop=True),
                            reads=[bmm, bug], writes=[bpL])
                        S.op("pe", lambda e, pLs=pLs, mm=mm, ug=ug, k0=k0, nb=nb, base=base: e.matmul(
                            pLs[:, 0:nb], lhsT=mm[:, base + 1, :], rhs=ug[:, k0:k0 + nb], start=True, stop=True),
                            reads=[bmm, bug], writes=[bpLs])
                        t1, bt1, _ = t1_r.next()
                        t2, bt2, _ = t2_r.next()
                        tt("dve", t1[:, 0:nb], pL[:, 0:nb], ctab[:, k0:k0 + nb], ALU.mult, [bpL, bct], [bt1])
                        tt("dve", t2[:, 0:nb], pLs[:, 0:nb], stab[:, k0:k0 + nb], ALU.mult, [bpLs, bst], [bt2])
                        tt("dve", lt[:, k0:k0 + nb], t1[:, 0:nb], t2[:, 0:nb], ALU.add, [bt1, bt2], [blt])
                    st, bst_, _ = st_r.next()
                    r8c = r8T[:, q:q + 1]

                    def scan(o_ap, d1_ap, n, init, extra=(), r8c=r8c, blt=blt, bst_=bst_):
                        S.op("dve", lambda e: e.tensor_tensor_scan(out=o_ap, data0=r8c.to_broadcast([128, n]), data1=d1_ap,
                                                                   initial=init, op0=ALU.mult, op1=ALU.add),
                             reads=[blt, b_scan] + list(extra), writes=[bst_])
                    if d == 0:
                        scan(st[:, 0:NK], lt[:, 0:NK], NK, 0.0)
                    else:
                        scan(st[:, 0:NKC][:, ::-1], lt[:, 0:NKC][:, ::-1], NKC, 0.0)
                        scan(st[:, NKC:NK][:, ::-1], lt[:, NKC:NK][:, ::-1], NK - NKC, st[:, 0:1], extra=[bst_])
                    x1, bx1, _ = x1_r.next()
                    x2, bx2, _ = x2_r.next()
                    tt("dve", x1, ctab, st, ALU.mult, [bct, bst_], [bx1])
                    tt("dve", x2, stab, st, ALU.mult, [bst, bst_], [bx2])
                    Xs[d] = (x1, bx1, x2, bx2)
                    if g == 0:
                        dump("ctab%d" % d, ctab, bct, [128, NK])
                        dump("stab%d" % d, stab, bst, [128, NK])
                        dump("lt%d" % d, lt, blt, [128, NK])
                        dump("st%d" % d, st, bst_, [128, NK])
                        dump("x1%d" % d, x1, bx1, [128, NK], BF16)
                        if d == 0:
                            dump("ug", ug, bug, [128, NK], BF16)
                            dump("r8T", r8T, b_scan, [128, Q2])
                            dump("fCT", fCT, b_scan, [128, Q2])
                for bi, (k0, nb) in enumerate(blocks):
                    pY, bpY = PS[6 + bi % 2]
                    mml = [(slice(0, nb), 0, ug[:, k0:k0 + nb], bug)]
                    x1, bx1, x2, bx2 = Xs[0]
                    lo = max(k0, 1)
                    if k0 + nb > lo:
                        mml.append((slice(lo - k0, nb), 3, x1[:, lo - 1:k0 + nb - 1], bx1))
                        mml.append((slice(lo - k0, nb), 4, x2[:, lo - 1:k0 + nb - 1], bx2))
                    x1, bx1, x2, bx2 = Xs[1]
                    if k0 < NKC:
                        n1 = nb - 1
                        if n1 > 0:
                            mml.append((slice(0, n1), 7, x1[:, 1:1 + n1], bx1))
                            mml.append((slice(0, n1), 8, x2[:, 1:1 + n1], bx2))
                    else:
                        hi = min(k0 + nb, NK - 1)
                        n1 = hi - k0
                        if n1 > 0:
                            mml.append((slice(0, n1), 7, x1[:, k0 + 1:k0 + 1 + n1], bx1))
                            mml.append((slice(0, n1), 8, x2[:, k0 + 1:k0 + 1 + n1], bx2))
                        if k0 + nb == NK:
                            mml.append((slice(nb - 1, nb), 7, x1[:, 0:1], bx1))
                            mml.append((slice(nb - 1, nb), 8, x2[:, 0:1], bx2))
                    for i, (sl, mi_, rhs, brhs) in enumerate(mml):
                        S.op("pe", lambda e, pY=pY, sl=sl, mi_=mi_, rhs=rhs, i=i, nmm=len(mml), mm=mm: e.matmul(
                            pY[:, sl], lhsT=mm[:, mi_, :], rhs=rhs, start=(i == 0), stop=(i == nmm - 1)),
                            reads=[bmm, brhs], writes=[bpY])
                    copy_op("act", yg_all[:, gl, k0:k0 + nb], pY[:, 0:nb], [bpY], [b_yg[gl]])
            yct, byct, dsy = yct_r.next()
            cnt = 0
            for bi, (k0, nb) in enumerate(blocks):
                for jp in range(8):
                    ps, bps = PS[cnt % 2]
                    cnt += 1
                    for gl in range(ngl):
                        S.op("pe", lambda e, ps=ps, gl=gl, jp=jp, k0=k0, nb=nb: e.matmul(
                            ps[:, 0:nb], lhsT=selTb[:, gl * 8 + jp, :], rhs=yg_all[:, gl, k0:k0 + nb],
                            start=(gl == 0), stop=(gl == ngl - 1)), reads=[b_sel, b_yg[gl]], writes=[bps])
                    S.op("dve", lambda e, yct=yct, U=U, ps=ps, jp=jp, k0=k0, nb=nb, ct=ct: e.scalar_tensor_tensor(
                        out=yct[:, 8 * k0 + jp:8 * (k0 + nb):8], in0=U[:, 8 * k0 + jp:8 * (k0 + nb):8],
                        scalar=vecT[:, ct, R_SSD:R_SSD + 1], in1=ps[:, 0:nb], op0=ALU.mult, op1=ALU.add),
                        reads=[bU, bps, b_vec], writes=[byct])
            S.op("sp", lambda e, yct=yct, ct=ct: e.dma_start(out=Yd[ct * 128:(ct + 1) * 128, :], in_=yct), reads=[byct], dsem=dsy, defer=2)
        A.pop()
        S.barrier()

    def stage_diag(l):
        A.push()
        dg_r = Rot(A, 2, (31, 128), BF16, S, "sp")
        for ct in range(NCT):
            dg, bdg, dsd = dg_r.next()
            for k in range(31):
                if k % 2 == 0:
                    S.op("act", lambda e, dg=dg, k=k, ct=ct: e.activation(out=dg[:, k, :], in_=identb, func=AF.Copy,
                                                                          scale=vecT[:, ct, R_CONVW + k:R_CONVW + k + 1]),
                         reads=[b_const, b_vec], writes=[bdg])
                else:
                    ts("dve", dg[:, k, :], identb, vecT[:, ct, R_CONVW + k:R_CONVW + k + 1], None, ALU.mult, None,
                       [b_const, b_vec], [bdg])
            S.op("sp", lambda e, dg=dg, ct=ct: e.dma_start(out=DGd[ct], in_=dg.rearrange("p k x -> p (k x)")), reads=[bdg], dsem=dsd)
        A.pop()
        S.barrier()

    def stage_B(l, last):
        import os
        BSTOP = int(os.environ.get("B_STOP", "9"))
        A.push()
        big = lambda dt: (A.alloc((NCT, 512), dt), Buf())
        o_t, b_o = big(BF16)
        cv_t, b_cv = big(BF16)
        gy_t, b_gy = big(BF16)
        y2_t, b_y2 = cv_t, b_cv
        ya_t, b_ya = big(BF16)
        yb_t, b_yb = big(BF16)
        mb_t, b_mb = big(BF16)
        ds_y = S.dma_sem("sp")
        PADL = 64 + 30
        vp_r = Rot(A, 2, 8 * PADL, BF16)
        vpc_r = Rot(A, 2, 512 + 30, BF16)
        for vp, bvp, _ in vp_r.items + vpc_r.items:
            S.op("pool", lambda e, vp=vp: e.memset(vp, 0.0), writes=[bvp])
        dg_r = Rot(A, 2, (31, 128), BF16, S, "sp")
        w_r = Rot(A, 3, NCT * 128, BF16, S, "sp")
        la_r = Rot(A, 3, 512, BF16, S, "sp")
        lb_r = Rot(A, 3, 512, BF16, S, "sp")
        xr_r = Rot(A, 3, 512, F32, S, "sp")
        xo_r = Rot(A, 3, 512, F32, S, "sp")
        sq_r = Rot(A, 2, 512, BF16)
        f1_r = Rot(A, 2, 512, F32)
        f2_r = Rot(A, 2, 512, F32)
        h1_r = Rot(A, 2, 512, BF16)
        st_t = [A.alloc(512, F32) for _ in range(4)]
        b_stat = Buf()
        onesb = A.alloc(128, BF16)
        b_ob = Buf()
        S.op("dve", lambda e: e.tensor_copy(out=onesb, in_=ones), reads=[b_const], writes=[b_ob])
        xT_v = xT.rearrange("(ct p) t -> p ct t", p=128)
        Yv = Yd.rearrange("(ct p) t -> p ct t", p=128)
        psi = [0]

        def load_part(rot, part, ct, t0, n):
            tl, btl, dstl = rot.next()
            r0 = part * D + ct * 128
            S.op("sp", lambda e: e.dma_start(out=tl[:, 0:n], in_=Pd[r0:r0 + 128, t0:t0 + n]), writes=[btl], dsem=dstl)
            return tl, btl

        def proj(wname, rhs_t, b_rhs, n, evac):
            for co in range(NCT):
                w, bw, dsw = w_r.next()
                S.op("sp", lambda e, w=w, co=co: e.dma_start(out=w, in_=wb_sq[wname][l, co]), writes=[bw], dsem=dsw)
                ps, bps = PS[psi[0] % 4]
                psi[0] += 1
                for kc in range(NCT):
                    S.op("pe", lambda e, ps=ps, w=w, kc=kc: e.matmul(ps[:, 0:n], lhsT=w[:, kc * 128:(kc + 1) * 128],
                                                                     rhs=rhs_t[:, kc, 0:n], start=(kc == 0), stop=(kc == NCT - 1)),
                         reads=[bw, b_rhs], writes=[bps])
                evac(co, ps, bps)

        def rstd_from(ps_ap, bps_, out_t, n, sub_msq=None):
            if sub_msq is None:
                act(out_t[:, 0:n], ps_ap, AF.Sqrt, [bps_, b_const], [b_stat], scale=1.0 / D, bias=eps_t[:, 0:1])
            else:
                S.op("dve", lambda e: e.scalar_tensor_tensor(out=out_t[:, 0:n], in0=ps_ap, scalar=1.0 / D, in1=sub_msq,
                                                             op0=ALU.mult, op1=ALU.subtract), reads=[bps_, b_stat], writes=[b_stat])
                act(out_t[:, 0:n], out_t[:, 0:n], AF.Sqrt, [b_stat, b_const], [b_stat], bias=eps_t[:, 0:1])
            S.op("dve", lambda e: e.reciprocal(out=out_t[:, 0:n], in_=out_t[:, 0:n]), reads=[b_stat], writes=[b_stat])

        for (t0, n, v) in tiles:
            if (last and v == 1) or BSTOP <= 0:
                continue
            rowlen = n if v == 1 else 64
            nrows = n // rowlen
            padl = rowlen + 30
            ps1, bps1 = PS[6]
            ps2, bps2 = PS[7]
            for ct in range(NCT):
                vb, bvb = load_part(la_r, 2, ct, t0, n)
                sg, bsg = load_part(lb_r, 3, ct, t0, n)
                vp, bvp, _ = (vpc_r if v == 1 else vp_r).next()
                vpv = vp[:, 0:nrows * padl].rearrange("p (r x) -> p r x", x=padl)
                tt("dve", vpv[:, :, 15:15 + rowlen], vb[:, 0:n].rearrange("p (r x) -> p r x", x=rowlen),
                   sg[:, 0:n].rearrange("p (r x) -> p r x", x=rowlen), ALU.mult, [bvb, bsg], [bvp])
                dg, bdg, dsd = dg_r.next()
                S.op("sp", lambda e, dg=dg, ct=ct: e.dma_start(out=dg.rearrange("p k x -> p (k x)"), in_=DGd[ct]), writes=[bdg], dsem=dsd)
                pc, bpc = PS[4 + ct % 2]
                for k in range(31):
                    S.op("pe", lambda e, pc=pc, dg=dg, k=k, vpv=vpv, n=n, rowlen=rowlen: e.matmul(
                        pc[:, 0:n].rearrange("p (r x) -> p r x", x=rowlen), lhsT=dg[:, k, :], rhs=vpv[:, :, k:k + rowlen],
                        start=(k == 0), stop=(k == 30)), reads=[bdg, bvp], writes=[bpc])
                cb = vecT[:, ct, R_CONVB:R_CONVB + 1]
                act(cv_t[:, ct, 0:n], pc[:, 0:n], AF.Identity, [bpc, b_vec], [b_cv], bias=cb)
                sq, bsq, _ = sq_r.next()
                act(sq[:, 0:n], pc[:, 0:n], AF.Square, [bpc, b_vec], [bsq], bias=cb)
                S.op("pe", lambda e, ct=ct, n=n: e.matmul(ps1[:, 0:n], lhsT=onesb, rhs=cv_t[:, ct, 0:n], start=(ct == 0), stop=(ct == NCT - 1)),
                     reads=[b_ob, b_cv], writes=[bps1])
                S.op("pe", lambda e, ct=ct, sq=sq, n=n: e.matmul(ps2[:, 0:n], lhsT=onesb, rhs=sq[:, 0:n], start=(ct == 0), stop=(ct == NCT - 1)),
                     reads=[b_ob, bsq], writes=[bps2])
            if BSTOP <= 1:
                continue
            mean_t, rstd_t, nmr_t, rstdo_t = st_t
            ts("dve", mean_t[:, 0:n], ps1[:, 0:n], 1.0 / D, None, ALU.mult, None, [bps1], [b_stat])
            msq, bmsq, _ = f1_r.next()
            tt("dve", msq[:, 0:n], mean_t[:, 0:n], mean_t[:, 0:n], ALU.mult, [b_stat], [bmsq])
            rstd_from(ps2[:, 0:n], bps2, rstd_t, n, sub_msq=msq[:, 0:n])
            S.op("dve", lambda e, n=n: e.scalar_tensor_tensor(out=nmr_t[:, 0:n], in0=mean_t[:, 0:n], scalar=-1.0, in1=rstd_t[:, 0:n],
                                                              op0=ALU.mult, op1=ALU.mult), reads=[b_stat, bmsq], writes=[b_stat])
            for ct in range(NCT):
                f1, bf1, _ = f1_r.next()
                tt("dve", f1[:, 0:n], cv_t[:, ct, 0:n], rstd_t[:, 0:n], ALU.mult, [b_cv, b_stat], [bf1])
                tt("dve", f1[:, 0:n], f1[:, 0:n], nmr_t[:, 0:n], ALU.add, [bf1, b_stat], [bf1])
                h1, bh1, _ = h1_r.next()
                act(h1[:, 0:n], f1[:, 0:n], AF.Silu, [bf1, b_vec], [bh1], scale=vecT[:, ct, R_LNG:R_LNG + 1],
                    bias=vecT[:, ct, R_LNB:R_LNB + 1])
                zb, bzb = load_part(la_r, 4, ct, t0, n)
                tt("dve", yb_t[:, ct, 0:n], h1[:, 0:n], zb[:, 0:n], ALU.mult, [bh1, bzb], [b_yb])

            def ev_cp(co, ps, bps):
                rb, brb = load_part(lb_r, 6, co, t0, n)
                tt("dve", mb_t[:, co, 0:n], ps[:, 0:n], rb[:, 0:n], ALU.mult, [bps, brb], [b_mb])
            if BSTOP <= 2:
                continue
            proj("conv_proj", yb_t, b_yb, n, ev_cp)
            if BSTOP <= 3:
                continue
            S.op("sp", lambda e, t0=t0, n=n: e.dma_start(out=ya_t[:, :, 0:n], in_=Yv[:, :, t0:t0 + n]), writes=[b_ya], dsem=ds_y)
            for ct in range(NCT):
                act(gy_t[:, ct, 0:n], ya_t[:, ct, 0:n], AF.Gelu_apprx_tanh, [b_ya], [b_gy])

            def ev_glu(co, ps, bps):
                gt_, bgt, _ = h1_r.next()
                act(gt_[:, 0:n], ps[:, 0:n], AF.Sigmoid, [bps, b_vec], [bgt], bias=vecT[:, co, R_GLUB:R_GLUB + 1])
                za, bza = load_part(la_r, 1, co, t0, n)
                tt("dve", gt_[:, 0:n], gt_[:, 0:n], ya_t[:, co, 0:n], ALU.mult, [bgt, b_ya], [bgt])
                tt("dve", y2_t[:, co, 0:n], gt_[:, 0:n], za[:, 0:n], ALU.mult, [bgt, bza], [b_y2])
            proj("glu_w", gy_t, b_gy, n, ev_glu)

            def ev_sp(co, ps, bps):
                ra, bra = load_part(lb_r, 5, co, t0, n)
                f1, bf1, _ = f1_r.next()
                tt("dve", f1[:, 0:n], ps[:, 0:n], ra[:, 0:n], ALU.mult, [bps, bra], [bf1])
                tt("dve", mb_t[:, co, 0:n], f1[:, 0:n], mb_t[:, co, 0:n], ALU.add, [bf1, b_mb], [b_mb])
            proj("ssm_proj", y2_t, b_y2, n, ev_sp)
            if BSTOP <= 4:
                continue
            pso, bpso = PS[6]

            def ev_wo(co, ps, bps):
                act(o_t[:, co, 0:n], ps[:, 0:n], AF.Copy, [bps], [b_o])
                sq, bsq, _ = sq_r.next()
                act(sq[:, 0:n], ps[:, 0:n], AF.Square, [bps], [bsq])
                S.op("pe", lambda e, co=co, sq=sq, n=n: e.matmul(pso[:, 0:n], lhsT=onesb, rhs=sq[:, 0:n], start=(co == 0), stop=(co == NCT - 1)),
                     reads=[b_ob, bsq], writes=[bpso])
            proj("w_out", mb_t, b_mb, n, ev_wo)
            if BSTOP <= 5:
                continue
            rstd_from(pso[:, 0:n], bpso, rstdo_t, n)
            if BSTOP <= 6:
                continue
            for ct in range(NCT):
                xr, bxr, dsxr = xr_r.next()
                S.op("sp", lambda e, xr=xr, ct=ct, t0=t0, n=n: e.dma_start(out=xr[:, 0:n], in_=xT[ct * 128:(ct + 1) * 128, t0:t0 + n]),
                     writes=[bxr], dsem=dsxr)
                f2, bf2, _ = f2_r.next()
                tt("dve", f2[:, 0:n], o_t[:, ct, 0:n], rstdo_t[:, 0:n], ALU.mult, [b_o, b_stat], [bf2])
                xo, bxo, dsxo = xo_r.next()
                S.op("dve", lambda e, xo=xo, f2=f2, xr=xr, ct=ct, n=n, v=v: e.scalar_tensor_tensor(
                    out=xo[:, 0:n], in0=f2[:, 0:n], scalar=gpost[:, ct, v:v + 1], in1=xr[:, 0:n], op0=ALU.mult, op1=ALU.add),
                    reads=[bf2, bxr, b_mod], writes=[bxo])
                S.op("sp", lambda e, xo=xo, ct=ct, t0=t0, n=n: e.dma_start(out=xT[ct * 128:(ct + 1) * 128, t0:t0 + n], in_=xo[:, 0:n]),
                     reads=[bxo], dsem=dsxo, defer=1)
        A.pop()
        S.barrier()

    def stage_epilogue():
        A.push()
        xi_r = Rot(A, 2, (NCT, 128), F32, S, "sp")
        xo_r = Rot(A, 2, D, F32, S, "sp")
        xT_v = xT.rearrange("(ct p) t -> p ct t", p=128)
        for tb in range(LAT // 128):
            t0 = CTX + tb * 128
            xi, bxi, dsi = xi_r.next()
            S.op("sp", lambda e, xi=xi, t0=t0: e.dma_start(out=xi, in_=xT_v[:, :, t0:t0 + 128]), writes=[bxi], dsem=dsi)
            xo, bxo, dso = xo_r.next()
            for q in range((NCT + 3) // 4):
                ps, bps = PS[q % 2]
                nq = min(4, NCT - 4 * q)
                for i in range(nq):
                    ct = 4 * q + i
                    S.op("pe", lambda e, ps=ps, xi=xi, i=i, ct=ct: e.transpose(ps[:, i * 128:(i + 1) * 128], xi[:, ct, :], ident),
                         reads=[bxi, b_const], writes=[bps])
                copy_op(("act", "dve")[q % 2], xo[:, 4 * q * 128:(4 * q + nq) * 128], ps[:, 0:nq * 128], [bps], [bxo])
            S.op("sp", lambda e, xo=xo, tb=tb: e.dma_start(out=out[tb * 128:(tb + 1) * 128, :], in_=xo), reads=[bxo], dsem=dso, defer=1)
        A.pop()
        S.barrier()

    if "pro" in stages:
        stage_prologue()
    if "layers" in stages:
        for l in range(nlayers):
            stage_mod(l)
            if "prep" in stages or "all" in stages:
                stage_prep(l)
            if "A" in stages or "all" in stages:
                stage_A(l)
            if "S" in stages or "all" in stages:
                stage_S(l)
            if "B" in stages or "all" in stages:
                stage_diag(l)
                stage_B(l, l == DEPTH - 1)
    if "epi" in stages:
        stage_epilogue()
    S.barrier()
    S.emit()
    return nc, S


def core_inputs(cfg, inp, b):
    D, DEPTH = cfg["D"], cfg["DEPTH"]
    f = lambda a: np.ascontiguousarray(a, dtype=np.float32)
    m = {
        "x": f(inp["x"][b]), "c": f(inp["c"][b]).reshape(1, D), "ctx": f(inp["ctx"][b]),
        "c_ctx": f(inp["c_ctx"]).reshape(1, D),
        "mod_w": f(inp["mod_w"]), "mod_b": f(inp["mod_b"]).reshape(DEPTH, 3, D),
        "conv_w": f(inp["conv_w"]), "w_in": f(inp["w_in"]),
    }
    for n in ("pre_g", "post_g", "ssm_d", "glu_b", "conv_b", "conv_ln_g", "conv_ln_b"):
        m[n] = f(inp[n]).reshape(DEPTH, 1, D)
    for n in ("glu_w", "ssm_proj", "conv_proj", "w_out"):
        m[n] = f(inp[n])
    G = cfg["G"]
    m["ssm_a_re"] = f(inp["ssm_a_re"]).reshape(DEPTH, 2 * G, 64)
    m["ssm_a_im"] = f(inp["ssm_a_im"]).reshape(DEPTH, 2 * G, 64)
    m["ssm_log_dt"] = f(inp["ssm_log_dt"]).reshape(DEPTH, 1, 2 * G)
    for n in ("ssm_b_re", "ssm_b_im", "ssm_c_re", "ssm_c_im"):
        m[n] = f(inp[n]).reshape(DEPTH, 2, G * 16, 64)
    for k, v in make_consts(cfg).items():
        m["cst_" + k] = v
    return m


N_CORES_USED = 4


def kernel(**inputs):
    cfg = make_cfg()
    nc, _ = build(cfg)
    maps = [core_inputs(cfg, inputs, b) for b in range(N_CORES_USED)]
    res = run_bass_kernel_spmd(nc, maps, core_ids=list(range(N_CORES_USED)))
    out = np.stack([np.asarray(r["out"], dtype=np.float32) for r in res.results], 0)
    return out
```

```python
import os
import numpy as np
import concourse.bass as bass
import concourse.mybir as mybir
from concourse.bass_utils import run_bass_kernel_spmd

F32 = mybir.dt.float32
BF16 = mybir.dt.bfloat16
U8 = mybir.dt.uint8
ALU = mybir.AluOpType
AF = mybir.ActivationFunctionType

ENGS = ("pe", "act", "dve", "pool", "sp")
EPS = 1e-6
MAGIC = 12582912.0
TWO_PI = float(2 * np.pi)


class Buf:
    __slots__ = ("name", "w", "readers")

    def __init__(self, name=""):
        self.name = name
        self.w = None
        self.readers = {}


class Op:
    __slots__ = ("eng", "fn", "deps", "signal", "sem", "cnt", "is_dma")

    def __init__(self, eng, fn, is_dma=False):
        self.eng = eng
        self.fn = fn
        self.deps = []
        self.signal = False
        self.sem = None
        self.cnt = 0
        self.is_dma = is_dma


class Sched:
    def __init__(self, nc):
        self.nc = nc
        self.ops = {e: [] for e in ENGS}
        self.all_ops = []
        self.esem = {e: nc.alloc_semaphore("sem_" + e) for e in ("pe", "act", "dve", "pool")}
        self.dma_sems = []
        self.last_dma = {}
        self.last_op = {e: None for e in ENGS}

    RR_N = {"sp": 84, "pool": 2}

    def dma_sem(self, queue="sp"):
        if not hasattr(self, "rr"):
            self.rr = {"sp": [[], 0], "pool": [[], 0]}
        lst, i = self.rr[queue]
        if len(lst) < self.RR_N[queue]:
            h = self.nc.alloc_semaphore("dsem%d" % len(self.dma_sems))
            self.dma_sems.append([h, queue])
            lst.append(len(self.dma_sems) - 1)
        idx = lst[i % self.RR_N[queue]]
        self.rr[queue][1] = i + 1
        return idx

    def _place(self, o):
        self.ops[o.eng].append(o)
        self.all_ops.append(o)

    def _tick(self, eng, flush_all=False):
        pend = self.pending.get(eng)
        if not pend:
            return
        keep = []
        for item in pend:
            item[0] -= 1
            if item[0] < 0 or flush_all:
                self._place(item[1])
            else:
                keep.append(item)
        self.pending[eng] = keep

    def op(self, eng, fn, reads=(), writes=(), dsem=None, defer=0):
        if not hasattr(self, "pending"):
            self.pending = {}
        o = Op(eng, fn, is_dma=dsem is not None)
        if dsem is not None:
            assert self.dma_sems[dsem][1] == eng
            o.sem = dsem
        deps = {}
        for b in reads:
            if b.w is not None:
                deps[id(b.w)] = b.w
        for b in writes:
            if b.w is not None:
                deps[id(b.w)] = b.w
            for r in b.readers.values():
                deps[id(r)] = r
        for d in deps.values():
            if (not d.is_dma) and (not o.is_dma) and d.eng == "pe" and eng == "pe":
                continue
            d.signal = True
            o.deps.append(d)
        key = ("d", o.sem) if o.is_dma else eng
        for b in reads:
            b.readers[key] = o
        for b in writes:
            b.w = o
            b.readers = {}
        if defer > 0:
            self.pending.setdefault(eng, []).append([defer, o])
        else:
            self._tick(eng)
            self._place(o)
        if o.is_dma:
            o.signal = True
            self.last_dma[dsem] = o
        else:
            self.last_op[eng] = o
        return o

    def barrier(self):
        for e in list(getattr(self, "pending", {})):
            self._tick(e, flush_all=True)
        lasts = [o for o in self.last_op.values() if o is not None and not o.is_dma]
        lasts += list(self.last_dma.values())
        for e in ENGS:
            o = Op(e, None)
            for d in lasts:
                if (not d.is_dma) and d.eng == e and e == "pe":
                    continue
                d.signal = True
                o.deps.append(d)
            self.ops[e].append(o)
            self.all_ops.append(o)

    def emit(self):
        nc = self.nc
        cnt = {e: 0 for e in self.esem}
        dcnt = [0] * len(self.dma_sems)
        for o in self.all_ops:
            if o.is_dma:
                dcnt[o.sem] += 16
                o.cnt = dcnt[o.sem]
            elif o.fn is not None and o.signal:
                cnt[o.eng] += 1
                o.cnt = cnt[o.eng]
        self.final_counts = (cnt, dcnt)

        def run(ename, e):
            known = {}
            for o in self.ops[ename]:
                need = {}
                for d in o.deps:
                    if d.is_dma:
                        key = ("d", d.sem)
                        h = self.dma_sems[d.sem][0]
                    else:
                        key = d.eng
                        h = self.esem[d.eng]
                    if d.cnt > need.get(key, (None, 0))[1]:
                        need[key] = (h, d.cnt)
                for key, (h, c) in need.items():
                    if known.get(key, 0) < c:
                        e.wait_ge(h, c)
                        known[key] = c
                if o.fn is None:
                    continue
                ins = o.fn(e)
                if o.is_dma:
                    ins.then_inc(self.dma_sems[o.sem][0], 16)
                elif o.signal:
                    ins.then_inc(self.esem[ename], 1)

        with nc.Block() as block:
            @block.sync
            def _(e):
                run("sp", e)

            @block.tensor
            def _(e):
                run("pe", e)

            @block.scalar
            def _(e):
                run("act", e)

            @block.vector
            def _(e):
                run("dve", e)

            @block.gpsimd
            def _(e):
                run("pool", e)


class Arena:
    def __init__(self, nc, nbytes):
        self.t = nc.alloc_sbuf_tensor("arena", [128, nbytes], U8)
        self.nbytes = nbytes
        self.off = 0
        self.stack = []

    def push(self):
        self.stack.append(self.off)

    def pop(self):
        self.off = self.stack.pop()

    def alloc(self, free, dt):
        if isinstance(free, int):
            free = (free,)
        esz = 4 if dt == F32 else 2
        n = int(np.prod(free))
        off = (self.off + 63) // 64 * 64
        assert off + n * esz <= self.nbytes, "arena overflow %d" % (off + n * esz)
        ap = self.t[:, off:off + n * esz].bitcast(dt)
        if len(free) == 2:
            ap = ap.rearrange("p (a b) -> p a b", a=free[0])
        elif len(free) == 3:
            ap = ap.rearrange("p (a b c) -> p a b c", a=free[0], b=free[1])
        self.off = off + n * esz
        return ap


class Rot:
    def __init__(self, A, n, free, dt, S=None, dma_queue=None):
        self.items = []
        for i in range(n):
            ap = A.alloc(free, dt)
            ds = S.dma_sem(dma_queue) if dma_queue else None
            self.items.append((ap, Buf(), ds))
        self.i = 0

    def next(self):
        it = self.items[self.i % len(self.items)]
        self.i += 1
        return it


def make_cfg(D=2048, LAT=8192, CTX=256, DEPTH=4):
    c = dict(D=D, LAT=LAT, CTX=CTX, DEPTH=DEPTH)
    c["NCT"] = D // 128
    c["G"] = D // 16
    c["NTOK"] = CTX + LAT
    c["NK"] = c["NTOK"] // 8
    c["NKC"] = CTX // 8
    return c


N_ROWS = 7 + 3 + 31
R_PRE, R_POST, R_SSD, R_GLUB, R_CONVB, R_LNG, R_LNB, R_MODB, R_CONVW = 0, 1, 2, 3, 4, 5, 6, 7, 10


def make_consts(cfg):
    NK, NKC = cfg["NK"], cfg["NKC"]
    c = {}
    c["ident"] = np.eye(128, dtype=np.float32)
    c["ones"] = np.ones((128, 128), np.float32)
    c["pswap"] = np.roll(np.eye(128, dtype=np.float32), 64, axis=1)
    sel = np.zeros((8, 8, 128, 128), np.float32)
    selT = np.zeros((8, 8, 128, 128), np.float32)
    for gl in range(8):
        for j in range(8):
            for cc in range(16):
                sel[gl, j, gl * 16 + cc, j * 16 + cc] = 1.0
                selT[gl, j, j * 16 + cc, gl * 16 + cc] = 1.0
    c["sel"] = np.ascontiguousarray(sel.reshape(64, 128, 128).transpose(1, 0, 2).reshape(128, 64 * 128))
    c["selT"] = np.ascontiguousarray(selT.reshape(64, 128, 128).transpose(1, 0, 2).reshape(128, 64 * 128))
    jj = np.arange(128) // 16
    mf = (jj[None, :] >= jj[:, None]).astype(np.float32)
    mr = (jj[:, None] >= jj[None, :]).astype(np.float32)
    c["maskf"] = np.tile(mf, (1, 4))
    c["maskr"] = np.tile(mr, (1, 4))
    k = np.arange(NK)
    tauf = k.astype(np.float32)
    taur = np.where(k < NKC, NKC - 1 - k, NKC + (NK - 1 - k)).astype(np.float32)
    c["tau"] = np.ascontiguousarray(np.broadcast_to(np.stack([tauf, taur])[None], (128, 2, NK)).reshape(128, 2 * NK))
    ms = np.arange(-7, 9).astype(np.float32)
    c["mvec"] = np.ascontiguousarray(np.broadcast_to(np.concatenate([ms / (2 * np.pi), ms])[None], (128, 32))).astype(np.float32)
    sg = np.ones((128, 2), np.float32)
    sg[64:, 0] = -1.0
    sg[:64, 1] = -1.0
    c["sgn"] = sg
    return c


def build(cfg, dbg=False, stages=("pro", "layers", "all", "epi"), nlayers=None):
    D, LAT, CTX, DEPTH = cfg["D"], cfg["LAT"], cfg["CTX"], cfg["DEPTH"]
    NCT, G, NTOK, NK, NKC = cfg["NCT"], cfg["G"], cfg["NTOK"], cfg["NK"], cfg["NKC"]
    DIN = 7 * D
    if nlayers is None:
        nlayers = DEPTH
    nc = bass.Bass("TRN2", target_bir_lowering=False)
    S = Sched(nc)

    def din(name, shape, dt=F32):
        return nc.dram_tensor(name, list(shape), dt, kind="ExternalInput").ap()

    def dscr(name, shape, dt, out=False):
        kind = "ExternalOutput" if (out or dbg) else "Internal"
        return nc.dram_tensor(name, list(shape), dt, kind=kind).ap()

    x_in = din("x", [LAT, D])
    c_in = din("c", [1, D])
    ctx_in = din("ctx", [CTX, D])
    cctx_in = din("c_ctx", [1, D])
    mod_w = din("mod_w", [DEPTH, D, 3 * D])
    mod_b = din("mod_b", [DEPTH, 3, D])
    vec_in = {n: din(n, [DEPTH, 1, D]) for n in
              ("pre_g", "post_g", "ssm_d", "glu_b", "conv_b", "conv_ln_g", "conv_ln_b")}
    conv_w = din("conv_w", [DEPTH, 31, D])
    w_in = din("w_in", [DEPTH, D, DIN])
    wsq = {n: din(n, [DEPTH, D, D]) for n in ("glu_w", "ssm_proj", "conv_proj", "w_out")}
    a_re = din("ssm_a_re", [DEPTH, 2 * G, 64])
    a_im = din("ssm_a_im", [DEPTH, 2 * G, 64])
    log_dt = din("ssm_log_dt", [DEPTH, 1, 2 * G])
    bc_in = {n: din(n, [DEPTH, 2, G * 16, 64]) for n in ("ssm_b_re", "ssm_b_im", "ssm_c_re", "ssm_c_im")}
    cst = {k: din("cst_" + k, v.shape) for k, v in make_consts(cfg).items()}
    out = nc.dram_tensor("out", [LAT, D], F32, kind="ExternalOutput").ap()

    xT = dscr("xT", [D, NTOK], F32)
    Pd = dscr("Pd", [DIN, NTOK], BF16)
    Yd = dscr("Yd", [D, NTOK], BF16)
    MMd = dscr("MMd", [G, 128, 9 * 128], BF16)
    DGd = dscr("DGd", [NCT, 128, 31 * 128], BF16)
    wb_in = dscr("wb_in", [DEPTH, 7 * NCT, 128, NCT * 128], BF16)
    wb_sq = {n: dscr("wb_" + n, [DEPTH, NCT, 128, NCT * 128], BF16) for n in wsq}

    A = Arena(nc, 190 * 1024)
    A.S = S
    PS = [(nc.alloc_psum_tensor("ps%d" % i, [128, 512], F32), Buf("ps%d" % i)) for i in range(8)]

    ident = A.alloc(128, F32)
    ones = A.alloc(128, F32)
    eps_t = A.alloc(1, F32)
    b_const = Buf("const")
    ds_c = S.dma_sem("sp")
    S.op("sp", lambda e: e.dma_start(out=ident, in_=cst["ident"]), writes=[b_const], dsem=ds_c)
    S.op("sp", lambda e: e.dma_start(out=ones, in_=cst["ones"]), writes=[b_const], dsem=ds_c)
    S.op("dve", lambda e: e.memset(eps_t, EPS), writes=[b_const])
    vecT = A.alloc((NCT, N_ROWS), F32)
    b_vec = Buf("vecT")
    modv = A.alloc((3 * NCT, 2), F32)
    gmod = A.alloc((NCT, 2), F32)
    gpost = A.alloc((NCT, 2), F32)
    b_mod = Buf("mod")
    scT = A.alloc((NCT, 2), F32)
    b_sc = Buf("scT")
    Q2 = 2 * G
    identb = A.alloc(128, BF16)
    pswapb = A.alloc(128, BF16)
    sgn_t = A.alloc(2, F32)
    r8T = A.alloc(Q2, F32)
    fCT = A.alloc(Q2, F32)
    fST = A.alloc(Q2, F32)
    b_scan = Buf("scanpar")
    S.op("dve", lambda e: e.tensor_copy(out=identb, in_=ident), reads=[b_const], writes=[b_const])
    S.op("sp", lambda e: e.dma_start(out=sgn_t, in_=cst["sgn"]), writes=[b_const], dsem=ds_c)
    _pw = A.alloc(128, F32)
    S.op("sp", lambda e: e.dma_start(out=_pw, in_=cst["pswap"]), writes=[b_const], dsem=ds_c)
    S.op("dve", lambda e: e.tensor_copy(out=pswapb, in_=_pw), reads=[b_const], writes=[b_const])

    dbg_sem = [None]

    def dump(name, ap, buf, shape, dt=F32):
        if not dbg:
            return
        if dbg_sem[0] is None:
            dbg_sem[0] = S.dma_sem("sp")
        dd = nc.dram_tensor("dbg_" + name, list(shape), dt, kind="ExternalOutput").ap()
        S.op("sp", lambda e: e.dma_start(out=dd, in_=ap), reads=[buf], dsem=dbg_sem[0])

    cast_rr = [0]

    def cast_eng():
        cast_rr[0] += 1
        return ("act", "dve", "pool")[cast_rr[0] % 3]

    def copy_op(eng, out_ap, in_ap, reads, writes):
        if eng == "act":
            return S.op("act", lambda e: e.activation(out=out_ap, in_=in_ap, func=AF.Copy), reads=reads, writes=writes)
        return S.op(eng, lambda e: e.tensor_copy(out=out_ap, in_=in_ap), reads=reads, writes=writes)

    def stage_prologue():
        A.push()
        xin_r = Rot(A, 2, D, F32, S, "sp")
        xo_r = Rot(A, 2, (NCT, 128), F32, S, "sp")
        xT_v = xT.rearrange("(ct p) t -> p ct t", p=128)
        nblk = NTOK // 128
        for tb in range(nblk):
            t0 = tb * 128
            src = ctx_in[t0:t0 + 128, :] if t0 < CTX else x_in[t0 - CTX:t0 - CTX + 128, :]
            xin, bxin, dsx = xin_r.next()
            S.op("sp", lambda e, xin=xin, src=src: e.dma_start(out=xin, in_=src), writes=[bxin], dsem=dsx)
            xo, bxo, dso = xo_r.next()
            for q in range((NCT + 3) // 4):
                ps, bps = PS[q % 2]
                nq = min(4, NCT - 4 * q)
                for i in range(nq):
                    ct = 4 * q + i
                    S.op("pe", lambda e, ps=ps, xin=xin, i=i, ct=ct: e.transpose(
                        ps[:, i * 128:(i + 1) * 128], xin[:, ct * 128:(ct + 1) * 128], ident),
                        reads=[bxin, b_const], writes=[bps])
                copy_op(("act", "dve")[q % 2], xo[:, 4 * q:4 * q + nq, :],
                        ps[:, 0:nq * 128].rearrange("p (a b) -> p a b", a=nq), [bps], [bxo])
            S.op("sp", lambda e, xo=xo, t0=t0: e.dma_start(out=xT_v[:, :, t0:t0 + 128], in_=xo),
                 reads=[bxo], dsem=dso, defer=1)
        S.barrier()
        A.pop()
        A.push()
        wl_r = Rot(A, 3, (NCT, 128), F32, S, "sp")
        wc_r = Rot(A, 3, NCT * 128, BF16, S, "sp")

        def cast_weight(src2d, dst3d, ncols):
            sv = src2d.rearrange("(kc p) c -> p kc c", p=128)
            for co in range(ncols // 128):
                wl, bwl, dsl = wl_r.next()
                S.op("sp", lambda e, wl=wl, co=co, sv=sv: e.dma_start(out=wl, in_=sv[:, :, co * 128:(co + 1) * 128]),
                     writes=[bwl], dsem=dsl)
                wc, bwc, dsc = wc_r.next()
                copy_op(cast_eng(), wc, wl.rearrange("p a b -> p (a b)"), [bwl], [bwc])
                S.op("sp", lambda e, wc=wc, co=co, dst3d=dst3d: e.dma_start(out=dst3d[co], in_=wc),
                     reads=[bwc], dsem=dsc, defer=2)

        for l in range(nlayers):
            cast_weight(w_in[l], wb_in[l], DIN)
            for n in wsq:
                cast_weight(wsq[n][l], wb_sq[n][l], D)
        S.barrier()
        A.pop()
        A.push()
        crow = A.alloc(D, F32)
        b_crow = Buf()
        ds = S.dma_sem("sp")
        S.op("sp", lambda e: e.dma_start(out=crow[0:1, :], in_=c_in), writes=[b_crow], dsem=ds)
        S.op("sp", lambda e: e.dma_start(out=crow[1:2, :], in_=cctx_in), writes=[b_crow], dsem=ds)
        ps, bps = PS[2]
        for ct in range(NCT):
            S.op("pe", lambda e, ct=ct: e.transpose(ps[:, 2 * ct:2 * ct + 2], crow[0:2, ct * 128:(ct + 1) * 128],
                                                    ident[0:2, 0:2]), reads=[b_crow, b_const], writes=[bps])
        S.op("act", lambda e: e.activation(out=scT.rearrange("p a b -> p (a b)"), in_=ps[:, 0:2 * NCT], func=AF.Silu),
             reads=[bps], writes=[b_sc])
        A.pop()
        S.barrier()

    def stage_mod(l):
        A.push()
        rows = A.alloc(D, F32)
        b_rows = Buf()
        ds = S.dma_sem("sp")
        names = ("pre_g", "post_g", "ssm_d", "glu_b", "conv_b", "conv_ln_g", "conv_ln_b")
        for i, n in enumerate(names):
            S.op("sp", lambda e, i=i, n=n: e.dma_start(out=rows[i:i + 1, :], in_=vec_in[n][l]), writes=[b_rows], dsem=ds)
        S.op("sp", lambda e: e.dma_start(out=rows[R_MODB:R_MODB + 3, :], in_=mod_b[l]), writes=[b_rows], dsem=ds)
        S.op("sp", lambda e: e.dma_start(out=rows[R_CONVW:R_CONVW + 31, :], in_=conv_w[l]), writes=[b_rows], dsem=ds)
        for ct in range(NCT):
            ps, bps = PS[ct % 2]
            S.op("pe", lambda e, ct=ct, ps=ps: e.transpose(ps[:, 0:N_ROWS], rows[0:N_ROWS, ct * 128:(ct + 1) * 128],
                                                           ident[0:N_ROWS, 0:N_ROWS]),
                 reads=[b_rows, b_const], writes=[bps])
            copy_op(("act", "dve")[ct % 2], vecT[:, ct, :], ps[:, 0:N_ROWS], [bps], [b_vec])
        mw_r = Rot(A, 3, (NCT, 128), F32, S, "sp")
        mwv = mod_w[l].rearrange("(kc p) c -> p kc c", p=128)
        psm, bpsm = PS[2]
        for j in range(3 * NCT):
            mw, bmw, dsm = mw_r.next()
            S.op("sp", lambda e, mw=mw, j=j: e.dma_start(out=mw, in_=mwv[:, :, j * 128:(j + 1) * 128]),
                 writes=[bmw], dsem=dsm)
            for kc in range(NCT):
                S.op("pe", lambda e, mw=mw, j=j, kc=kc: e.matmul(psm[:, 2 * j:2 * j + 2], lhsT=mw[:, kc, :],
                                                                  rhs=scT[:, kc, :], start=(kc == 0), stop=(kc == NCT - 1)),
                     reads=[bmw, b_sc], writes=[bpsm])
        psv = psm[:, 0:6 * NCT].rearrange("p (j v) -> p j v", v=2)
        for r in range(3):
            for v in range(2):
                S.op("dve", lambda e, r=r, v=v: e.tensor_tensor(out=modv[:, r * NCT:(r + 1) * NCT, v],
                                                                in0=psv[:, r * NCT:(r + 1) * NCT, v],
                                                                in1=vecT[:, :, R_MODB + r], op=ALU.add),
                     reads=[bpsm, b_vec], writes=[b_mod])
        for v in range(2):
            S.op("dve", lambda e, v=v: e.scalar_tensor_tensor(out=gmod[:, :, v], in0=modv[:, NCT:2 * NCT, v], scalar=1.0,
                                                              in1=vecT[:, :, R_PRE], op0=ALU.add, op1=ALU.mult),
                 reads=[b_mod, b_vec], writes=[b_mod])
            S.op("dve", lambda e, v=v: e.tensor_tensor(out=gpost[:, :, v], in0=modv[:, 2 * NCT:3 * NCT, v],
                                                       in1=vecT[:, :, R_POST], op=ALU.mult),
                 reads=[b_mod, b_vec], writes=[b_mod])
        A.pop()
        S.barrier()

    tiles = []
    t = 0
    while t < CTX:
        n = min(512, CTX - t)
        tiles.append((t, n, 1))
        t += n
    while t < NTOK:
        n = min(512, NTOK - t)
        tiles.append((t, n, 0))
        t += n

    PART_FUNC = [AF.Copy, AF.Silu, AF.Copy, AF.Sigmoid, AF.Silu, AF.Sigmoid, AF.Sigmoid]

    def stage_A(l):
        A.push()
        xt_r = Rot(A, 2, (NCT, 512), F32, S, "sp")
        hT_r = Rot(A, 2, (NCT, 512), BF16)
        sq_r = Rot(A, 2, 512, F32)
        tmp_r = Rot(A, 2, 512, F32)
        rstd_r = Rot(A, 2, 512, F32)
        w_r = Rot(A, 3, NCT * 128, BF16, S, "sp")
        ev_r = Rot(A, 4, 512, BF16, S, "sp")
        xT_v = xT.rearrange("(ct p) t -> p ct t", p=128)
        psi = [0]
        for (t0, n, v) in tiles:
            xt, bxt, dsx = xt_r.next()
            S.op("sp", lambda e, xt=xt, t0=t0, n=n: e.dma_start(out=xt[:, :, 0:n], in_=xT_v[:, :, t0:t0 + n]),
                 writes=[bxt], dsem=dsx)
            pst, bpst = PS[7]
            for ct in range(NCT):
                sq, bsq, _ = sq_r.next()
                S.op("act", lambda e, sq=sq, xt=xt, ct=ct, n=n: e.activation(out=sq[:, 0:n], in_=xt[:, ct, 0:n], func=AF.Square),
                     reads=[bxt], writes=[bsq])
                S.op("pe", lambda e, sq=sq, ct=ct, n=n: e.matmul(pst[:, 0:n], lhsT=ones, rhs=sq[:, 0:n], start=(ct == 0),
                                                                  stop=(ct == NCT - 1)), reads=[bsq, b_const], writes=[bpst])
            rstd, brs, _ = rstd_r.next()
            S.op("act", lambda e, rstd=rstd, n=n: e.activation(out=rstd[:, 0:n], in_=pst[:, 0:n], func=AF.Sqrt, bias=eps_t[:, 0:1],
                                                               scale=1.0 / D), reads=[bpst, b_const], writes=[brs])
            S.op("dve", lambda e, rstd=rstd, n=n: e.reciprocal(out=rstd[:, 0:n], in_=rstd[:, 0:n]), reads=[brs], writes=[brs])
            hT, bhT, _ = hT_r.next()
            for ct in range(NCT):
                tmp, btmp, _ = tmp_r.next()
                S.op("dve", lambda e, tmp=tmp, xt=xt, ct=ct, rstd=rstd, n=n: e.tensor_tensor(
                    out=tmp[:, 0:n], in0=xt[:, ct, 0:n], in1=rstd[:, 0:n], op=ALU.mult), reads=[bxt, brs], writes=[btmp])
                S.op("act", lambda e, tmp=tmp, hT=hT, ct=ct, n=n, v=v: e.activation(
                    out=hT[:, ct, 0:n], in_=tmp[:, 0:n], func=AF.Identity, scale=gmod[:, ct, v:v + 1],
                    bias=modv[:, ct, v:v + 1]), reads=[btmp, b_mod], writes=[bhT])
            for co in range(7 * NCT):
                w, bw, dsw = w_r.next()
                S.op("sp", lambda e, w=w, co=co: e.dma_start(out=w, in_=wb_in[l, co]), writes=[bw], dsem=dsw)
                ps, bps = PS[psi[0] % 4]
                psi[0] += 1
                for kc in range(NCT):
                    S.op("pe", lambda e, ps=ps, w=w, hT=hT, kc=kc, n=n: e.matmul(
                        ps[:, 0:n], lhsT=w[:, kc * 128:(kc + 1) * 128], rhs=hT[:, kc, 0:n], start=(kc == 0),
                        stop=(kc == NCT - 1)), reads=[bw, bhT], writes=[bps])
                ev, bev, dse = ev_r.next()
                fn = PART_FUNC[co // NCT]
                if fn == AF.Copy:
                    S.op("dve", lambda e, ev=ev, ps=ps, n=n: e.tensor_copy(out=ev[:, 0:n], in_=ps[:, 0:n]), reads=[bps], writes=[bev])
                else:
                    S.op("act", lambda e, ev=ev, ps=ps, n=n, fn=fn: e.activation(out=ev[:, 0:n], in_=ps[:, 0:n], func=fn),
                         reads=[bps], writes=[bev])
                S.op("sp", lambda e, ev=ev, co=co, t0=t0, n=n: e.dma_start(out=Pd[co * 128:(co + 1) * 128, t0:t0 + n],
                                                                            in_=ev[:, 0:n]), reads=[bev], dsem=dse, defer=2)
        A.pop()
        S.barrier()

    MI = lambda m: m + 7

    def tt(eng, out_ap, in0, in1, op, reads, writes):
        return S.op(eng, lambda e: e.tensor_tensor(out=out_ap, in0=in0, in1=in1, op=op), reads=reads, writes=writes)

    def ts(eng, out_ap, in0, s1, s2, op0, op1, reads, writes):
        if op1 is None:
            return S.op(eng, lambda e: e.tensor_scalar(out=out_ap, in0=in0, scalar1=s1, scalar2=None, op0=op0),
                        reads=reads, writes=writes)
        return S.op(eng, lambda e: e.tensor_scalar(out=out_ap, in0=in0, scalar1=s1, scalar2=s2, op0=op0, op1=op1),
                    reads=reads, writes=writes)

    def act(out_ap, in_ap, func, reads, writes, scale=1.0, bias=None):
        if bias is None:
            return S.op("act", lambda e: e.activation(out=out_ap, in_=in_ap, func=func, scale=scale), reads=reads, writes=writes)
        return S.op("act", lambda e: e.activation(out=out_ap, in_=in_ap, func=func, scale=scale, bias=bias),
                    reads=reads, writes=writes)

    def stage_prep(l):
        A.push()
        ds = S.dma_sem("sp")
        maskf_t = A.alloc(512, F32)
        maskr_t = A.alloc(512, F32)
        mvec_t = A.alloc(32, F32)
        b_pc = Buf("prepconst")
        for dst, src in ((maskf_t, cst["maskf"]), (maskr_t, cst["maskr"]), (mvec_t, cst["mvec"])):
            S.op("sp", lambda e, dst=dst, src=src: e.dma_start(out=dst, in_=src), writes=[b_pc], dsem=ds)
        areT = A.alloc(Q2, F32)
        aimT = A.alloc(Q2, F32)
        dtT = A.alloc(Q2, F32)
        b_a = Buf("aT")
        z_r = Rot(A, 4, 128, F32, S, "sp")
        for r0 in range(0, Q2, 128):
            rows = min(128, Q2 - r0)
            for src, dstT, pi_ in ((a_re, areT, 0), (a_im, aimT, 1)):
                z, bz, dsz = z_r.next()
                S.op("sp", lambda e, z=z, src=src, r0=r0, rows=rows: e.dma_start(out=z[0:rows, 0:64], in_=src[l, r0:r0 + rows, :]),
                     writes=[bz], dsem=dsz)
                S.op("sp", lambda e, z=z, src=src, r0=r0, rows=rows: e.dma_start(out=z[0:rows, 64:128], in_=src[l, r0:r0 + rows, :]),
                     writes=[bz], dsem=dsz)
                ps, bps = PS[pi_]
                S.op("pe", lambda e, z=z, ps=ps, rows=rows: e.transpose(ps[:, 0:rows], z[0:rows, :], ident[0:rows, 0:rows]),
                     reads=[bz, b_const], writes=[bps])
                copy_op("dve", dstT[:, r0:r0 + rows], ps[:, 0:rows], [bps], [b_a])
        S.op("sp", lambda e: e.dma_start(out=dtT, in_=log_dt[l].to_broadcast([128, Q2])), writes=[b_a], dsem=ds)
        act(dtT, dtT, AF.Exp, [b_a], [b_a])
        xre = A.alloc(Q2, F32)
        th = A.alloc(Q2, F32)
        tt("dve", xre, dtT, areT, ALU.mult, [b_a], [b_a])
        tt("dve", th, dtT, aimT, ALU.mult, [b_a], [b_a])
        NM = 16
        TS_ = A.alloc((NM, Q2), F32)
        TC_ = A.alloc((NM, Q2), F32)
        R_ = A.alloc((NM, Q2), F32)
        MG = A.alloc((NM, Q2), F32)
        PIMN = A.alloc((NM, Q2), F32)
        b_p = Buf("pow")
        fl = lambda ap: ap.rearrange("p a b -> p (a b)")
        bc_m = lambda v: v.unsqueeze(2).to_broadcast([128, NM, Q2])
        bc_q = lambda v: v.unsqueeze(1).to_broadcast([128, NM, Q2])
        tt("dve", TS_, bc_q(th), bc_m(mvec_t[:, 0:NM]), ALU.mult, [b_a, b_pc], [b_p])
        ts("dve", fl(R_), fl(TS_), MAGIC, -MAGIC, ALU.add, ALU.add, [b_p], [b_p])
        ts("dve", fl(TC_), fl(TS_), 0.25, None, ALU.add, None, [b_p], [b_p])
        tt("dve", fl(TS_), fl(TS_), fl(R_), ALU.subtract, [b_p], [b_p])
        ts("dve", fl(R_), fl(TC_), MAGIC, -MAGIC, ALU.add, ALU.add, [b_p], [b_p])
        tt("dve", fl(TC_), fl(TC_), fl(R_), ALU.subtract, [b_p], [b_p])
        copy_op("dve", fCT, TS_[:, MI(8), :], [b_p], [b_scan])
        ts("dve", fST, TS_[:, MI(8), :], sgn_t[:, 0:1], None, ALU.mult, None, [b_p, b_const], [b_scan])
        act(fl(TS_), fl(TS_), AF.Sin, [b_p], [b_p], scale=TWO_PI)
        act(fl(TC_), fl(TC_), AF.Sin, [b_p], [b_p], scale=TWO_PI)
        tt("dve", MG, bc_q(xre), bc_m(mvec_t[:, NM:2 * NM]), ALU.mult, [b_a, b_pc], [b_p])
        act(fl(MG), fl(MG), AF.Exp, [b_p], [b_p])
        copy_op("dve", r8T, MG[:, MI(8), :], [b_p], [b_scan])
        tt("dve", fl(TS_), fl(TS_), fl(MG), ALU.mult, [b_p], [b_p])
        tt("dve", fl(TC_), fl(TC_), fl(MG), ALU.mult, [b_p], [b_p])
        PIM, PRE = TS_, TC_
        sm = [A.alloc(Q2, F32) for _ in range(6)]
        u_, den, t1_, t2_, fre, fim = sm
        b_f = Buf("f")
        ts("dve", u_, PRE[:, MI(1), :], -1.0, None, ALU.add, None, [b_p], [b_f])
        tt("dve", den, areT, areT, ALU.mult, [b_a], [b_f])
        tt("dve", t1_, aimT, aimT, ALU.mult, [b_a], [b_f])
        tt("dve", den, den, t1_, ALU.add, [b_f], [b_f])
        S.op("dve", lambda e: e.reciprocal(out=den, in_=den), reads=[b_f], writes=[b_f])
        tt("dve", t1_, u_, areT, ALU.mult, [b_f, b_a], [b_f])
        tt("dve", t2_, PIM[:, MI(1), :], aimT, ALU.mult, [b_p, b_a], [b_f])
        tt("dve", t1_, t1_, t2_, ALU.add, [b_f], [b_f])
        tt("dve", fre, t1_, den, ALU.mult, [b_f], [b_f])
        tt("dve", t1_, PIM[:, MI(1), :], areT, ALU.mult, [b_p, b_a], [b_f])
        tt("dve", t2_, u_, aimT, ALU.mult, [b_f, b_a], [b_f])
        tt("dve", t1_, t1_, t2_, ALU.subtract, [b_f], [b_f])
        tt("dve", fim, t1_, den, ALU.mult, [b_f], [b_f])
        QRE = A.alloc((8, Q2), F32)
        QIMS = A.alloc((8, Q2), F32)
        QT = A.alloc((8, Q2), F32)
        b_q = Buf("Q")
        bq8 = lambda v: v.unsqueeze(1).to_broadcast([128, 8, Q2])
        P8re, P8im = PRE[:, MI(0):MI(8), :], PIM[:, MI(0):MI(8), :]
        tt("dve", QRE, bq8(fre), P8re, ALU.mult, [b_f, b_p], [b_q])
        tt("dve", QT, bq8(fim), P8im, ALU.mult, [b_f, b_p], [b_q])
        tt("dve", QRE, QRE, QT, ALU.subtract, [b_q], [b_q])
        tt("dve", QIMS, bq8(fre), P8im, ALU.mult, [b_f, b_p], [b_q])
        tt("dve", QT, bq8(fim), P8re, ALU.mult, [b_f, b_p], [b_q])
        tt("dve", QIMS, QIMS, QT, ALU.add, [b_q], [b_q])
        ts("dve", fl(QIMS), fl(QIMS), sgn_t[:, 1:2], None, ALU.mult, None, [b_q, b_const], [b_q])
        PRES2, PRES1 = R_, MG
        ts("dve", fl(PRES2), fl(PRE), sgn_t[:, 0:1], None, ALU.mult, None, [b_p, b_const], [b_p])
        ts("dve", fl(PRES1), fl(PRE), sgn_t[:, 1:2], None, ALU.mult, None, [b_p, b_const], [b_p])
        ts("dve", fl(PIMN), fl(PIM), -1.0, None, ALU.mult, None, [b_p], [b_p])
        zb_r = Rot(A, 4, 128, F32, S, "sp")
        BC = A.alloc((4, 128), F32)
        b_bc = Buf("BC")
        tA_r = Rot(A, 2, 1024, F32)
        tB_r = Rot(A, 2, 1024, F32)
        t2b_r = Rot(A, 2, (8, 128), BF16)
        tcb_r = Rot(A, 2, (8, 128), BF16)
        mms_r = Rot(A, 1, (8, 9, 128), BF16, S, "sp")
        m1acc = A.alloc((2, 512), F32)
        m1tmp = A.alloc((2, 512), F32)
        b_m1 = Buf("m1acc")
        MMv = MMd.rearrange("g p x -> p g x")
        eng_rr = [0]
        NB8 = min(8, G)
        for g0 in range(0, G, NB8):
            mms, bmms, dsmm = mms_r.next()
            for d in range(2):
                q0 = d * G + g0
                srcs = (("ssm_b_re", "ssm_b_im"), ("ssm_b_im", "ssm_b_re"), ("ssm_c_re", "ssm_c_im"), ("ssm_c_im", "ssm_c_re"))
                ps, bps = PS[2]
                for i, (n0, n1) in enumerate(srcs):
                    z, bz, dsz = zb_r.next()
                    for hh, nm in enumerate((n0, n1)):
                        S.op("sp", lambda e, z=z, nm=nm, hh=hh, d=d, g0=g0: e.dma_start(
                            out=z[:, hh * 64:(hh + 1) * 64], in_=bc_in[nm][l, d, g0 * 16:(g0 + NB8) * 16, :]), writes=[bz], dsem=dsz)
                    S.op("pe", lambda e, z=z, ps=ps, i=i: e.transpose(ps[:, i * 128:(i + 1) * 128], z, ident),
                         reads=[bz, b_const], writes=[bps])
                copy_op("act", fl(BC), ps[:, 0:512], [bps], [b_bc])
                Ba, Bb, Ca, Cb = [BC[:, i, :] for i in range(4)]

                def coef(tab, lo, hi, rev):
                    v = tab[:, lo:hi, q0:q0 + NB8]
                    if rev:
                        v = v[:, ::-1, :]
                    return v.rearrange("p m g -> p g m").unsqueeze(3).to_broadcast([128, NB8, 8, 16])

                def data(X):
                    return X.rearrange("p (g c) -> p g c", c=16).unsqueeze(2).to_broadcast([128, NB8, 8, 16])

                def table(out4, cA, dA, cB, dB, deps_r):
                    eng = ("dve", "pool")[eng_rr[0] % 2]
                    eng_rr[0] += 1
                    tA, btA, _ = tA_r.next()
                    tB, btB, _ = tB_r.next()
                    tA4 = tA.rearrange("p (g j c) -> p g j c", g=NB8, j=8)
                    tB4 = tB.rearrange("p (g j c) -> p g j c", g=NB8, j=8)
                    tt(eng, tA4, cA, dA, ALU.mult, deps_r, [btA])
                    tt(eng, tB4, cB, dB, ALU.mult, deps_r, [btB])
                    return eng, tA4, tB4, btA, btB

                fwd = (d == 0)
                t2b, bt2b, _ = t2b_r.next()
                tcb, btcb, _ = tcb_r.next()
                rd = [b_p, b_q, b_bc]
                eng, tA4, tB4, btA, btB = table(None, coef(QRE, 0, 8, fwd), data(Ba), coef(QIMS, 0, 8, fwd), data(Bb), rd)
                tt(eng, t2b.rearrange("p g (j c) -> p g j c", j=8), tA4, tB4, ALU.add, [btA, btB], [bt2b])
                base = 1 + 4 * d
                lo, hi, rv = MI(1), MI(8) + 1, (not fwd)
                eng, tA4, tB4, btA, btB = table(None, coef(PRES2, lo, hi, rv), data(Ca), coef(PIMN, lo, hi, rv), data(Cb), rd)
                tt(eng, mms[:, :, base + 2, :].rearrange("p g (j c) -> p g j c", j=8), tA4, tB4, ALU.add, [btA, btB], [bmms])
                eng, tA4, tB4, btA, btB = table(None, coef(PIMN, lo, hi, rv), data(Ca), coef(PRES1, lo, hi, rv), data(Cb), rd)
                tt(eng, mms[:, :, base + 3, :].rearrange("p g (j c) -> p g j c", j=8), tA4, tB4, ALU.add, [btA, btB], [bmms])
                lo, hi, rv = MI(-7), MI(0) + 1, (not fwd)
                eng, tA4, tB4, btA, btB = table(None, coef(PRES2, lo, hi, rv), data(Ca), coef(PIMN, lo, hi, rv), data(Cb), rd)
                tt(eng, tcb.rearrange("p g (j c) -> p g j c", j=8), tA4, tB4, ALU.add, [btA, btB], [btcb])
                for h in range((NB8 + 3) // 4):
                    ng = min(4, NB8 - 4 * h)
                    p2, bp2 = PS[3]
                    p2s, bp2s = PS[4]
                    p1, bp1 = PS[5 + (h % 2)]
                    for i in range(ng):
                        gi = 4 * h + i
                        S.op("pe", lambda e, p2=p2, t2b=t2b, gi=gi, i=i: e.matmul(p2[:, i * 128:(i + 1) * 128], lhsT=t2b[:, gi, :],
                                                                              rhs=identb, start=True, stop=True),
                             reads=[bt2b, b_const], writes=[bp2])
                        S.op("pe", lambda e, p2s=p2s, t2b=t2b, gi=gi, i=i: e.matmul(p2s[:, i * 128:(i + 1) * 128], lhsT=t2b[:, gi, :],
                                                                                rhs=pswapb, start=True, stop=True),
                             reads=[bt2b, b_const], writes=[bp2s])
                        S.op("pe", lambda e, p1=p1, t2b=t2b, tcb=tcb, gi=gi, i=i: e.matmul(p1[:, i * 128:(i + 1) * 128], lhsT=t2b[:, gi, :],
                                                                                       rhs=tcb[:, gi, :], start=True, stop=True),
                             reads=[bt2b, btcb], writes=[bp1])
                    copy_op("act", mms[:, 4 * h:4 * h + ng, base + 0, :], p2[:, 0:ng * 128].rearrange("p (g x) -> p g x", g=ng),
                            [bp2], [bmms])
                    copy_op("act", mms[:, 4 * h:4 * h + ng, base + 1, :], p2s[:, 0:ng * 128].rearrange("p (g x) -> p g x", g=ng),
                            [bp2s], [bmms])
                    if fwd:
                        tt("dve", m1acc[:, h, 0:ng * 128], p1[:, 0:ng * 128], maskf_t[:, 0:ng * 128], ALU.mult, [bp1, b_pc], [b_m1])
                    else:
                        tt("dve", m1tmp[:, h, 0:ng * 128], p1[:, 0:ng * 128], maskr_t[:, 0:ng * 128], ALU.mult, [bp1, b_pc], [b_m1])
                        tt("dve", mms[:, 4 * h:4 * h + ng, 0, :], m1acc[:, h, 0:ng * 128].rearrange("p (g x) -> p g x", g=ng),
                           m1tmp[:, h, 0:ng * 128].rearrange("p (g x) -> p g x", g=ng), ALU.add, [b_m1], [bmms])
            S.op("sp", lambda e, mms=mms, g0=g0: e.dma_start(out=MMv[:, g0:g0 + NB8, :], in_=mms.rearrange("p g s x -> p g (s x)")),
                 reads=[bmms], dsem=dsmm, defer=2)
        A.pop()
        S.barrier()

    def stage_S(l):
        A.push()
        ds = S.dma_sem("sp")
        selb = A.alloc((64, 128), BF16)
        selTb = A.alloc((64, 128), BF16)
        tau_t = A.alloc((2, NK), F32)
        b_sel = Buf("sel")
        A.push()
        stg_r = Rot(A, 2, 2048, F32, S, "sp")
        for dst, src in ((selb, cst["sel"]), (selTb, cst["selT"])):
            dflat = dst.rearrange("p a b -> p (a b)")
            for pc in range(4):
                stg, bstg, dss = stg_r.next()
                S.op("sp", lambda e, stg=stg, src=src, pc=pc: e.dma_start(out=stg, in_=src[:, pc * 2048:(pc + 1) * 2048]),
                     writes=[bstg], dsem=dss)
                copy_op(("dve", "act")[pc % 2], dflat[:, pc * 2048:(pc + 1) * 2048], stg, [bstg], [b_sel])
        S.op("sp", lambda e: e.dma_start(out=tau_t.rearrange("p a b -> p (a b)"), in_=cst["tau"]), writes=[b_sel], dsem=ds)
        S.barrier()
        A.pop()
        blocks = [(0, NKC)] + [(k0, min(512, NK - k0)) for k0 in range(NKC, NK, 512)]
        U_r = Rot(A, 1, NTOK, BF16, S, "sp")
        yct_r = Rot(A, 1, NTOK, BF16, S, "sp")
        mm_r = Rot(A, 2, (9, 128), BF16, S, "sp")
        ug_r = Rot(A, 2, NK, BF16)
        yg_all = A.alloc((8, NK), BF16)
        b_yg = [Buf() for _ in range(8)]
        ctab_r = Rot(A, 2, NK, F32)
        stab_r = Rot(A, 2, NK, F32)
        ptt_r = Rot(A, 2, NK, F32)
        pr_r = Rot(A, 2, NK, F32)
        t1_r = Rot(A, 2, 512, F32)
        t2_r = Rot(A, 2, 512, F32)
        lt_r = Rot(A, 2, NK, F32)
        st_r = Rot(A, 2, NK, F32)
        x1_r = Rot(A, 4, NK, BF16)
        x2_r = Rot(A, 4, NK, BF16)
        MMg = MMd.rearrange("g p (s x) -> g p s x", s=9)
        for ct in range(NCT):
            U, bU, dsU = U_r.next()
            S.op("sp", lambda e, U=U, ct=ct: e.dma_start(out=U, in_=Pd[ct * 128:(ct + 1) * 128, :]), writes=[bU], dsem=dsU)
            ngl = min(8, G - 8 * ct)
            for gl in range(ngl):
                g = ct * 8 + gl
                mm, bmm, dsm = mm_r.next()
                S.op("sp", lambda e, mm=mm, g=g: e.dma_start(out=mm, in_=MMg[g]), writes=[bmm], dsem=dsm)
                ug, bug, _ = ug_r.next()
                for bi, (k0, nb) in enumerate(blocks):
                    ps, bps = PS[bi % 2]
                    for j in range(8):
                        S.op("pe", lambda e, ps=ps, gl=gl, j=j, U=U, k0=k0, nb=nb: e.matmul(
                            ps[:, 0:nb], lhsT=selb[:, gl * 8 + j, :], rhs=U[:, 8 * k0 + j:8 * (k0 + nb):8],
                            start=(j == 0), stop=(j == 7)), reads=[b_sel, bU], writes=[bps])
                    copy_op("act", ug[:, k0:k0 + nb], ps[:, 0:nb], [bps], [bug])
                Xs = {}
                for d in range(2):
                    q = d * G + g
                    ctab, bct, _ = ctab_r.next()
                    stab, bst, _ = stab_r.next()
                    ptt, bptt, _ = ptt_r.next()
                    pr, bpr, _ = pr_r.next()
                    tau_d = tau_t[:, d, :]
                    TE = "dve"
                    ts(TE, ptt, tau_d, fST[:, q:q + 1], None, ALU.mult, None, [b_sel, b_scan], [bptt])
                    ts(TE, pr, ptt, MAGIC, -MAGIC, ALU.add, ALU.add, [bptt], [bpr])
                    tt(TE, stab, ptt, pr, ALU.subtract, [bptt, bpr], [bst])
                    ts(TE, ptt, tau_d, fCT[:, q:q + 1], 0.25, ALU.mult, ALU.add, [b_sel, b_scan], [bptt])
                    ts(TE, pr, ptt, MAGIC, -MAGIC, ALU.add, ALU.add, [bptt], [bpr])
                    tt(TE, ctab, ptt, pr, ALU.subtract, [bptt, bpr], [bct])
                    act(stab, stab, AF.Sin, [bst], [bst], scale=TWO_PI)
                    act(ctab, ctab, AF.Sin, [bct], [bct], scale=TWO_PI)
                    base = 1 + 4 * d
                    lt, blt, _ = lt_r.next()
                    for bi, (k0, nb) in enumerate(blocks):
                        pL, bpL = PS[2 + bi % 2]
                        pLs, bpLs = PS[4 + bi % 2]
                        S.op("pe", lambda e, pL=pL, mm=mm, ug=ug, k0=k0, nb=nb, base=base: e.matmul(
                            pL[:, 0:nb], lhsT=mm[:, base, :], rhs=ug[:, k0:k0 + nb], start=True, stop=True),
                            reads=[bmm, bug], writes=[bpL])
                        S.op("pe", lambda e, pLs=pLs, mm=mm, ug=ug, k0=k0, nb=nb, base=base: e.matmul(
                            pLs[:, 0:nb], lhsT=mm[:, base + 1, :], rhs=ug[:, k0:k0 + nb], start=True, stop=True),
                            reads=[bmm, bug], writes=[bpLs])
                        t1, bt1, _ = t1_r.next()
                        t2, bt2, _ = t2_r.next()
                        tt("dve", t1[:, 0:nb], pL[:, 0:nb], ctab[:, k0:k0 + nb], ALU.mult, [bpL, bct], [bt1])
                        tt("dve", t2[:, 0:nb], pLs[:, 0:nb], stab[:, k0:k0 + nb], ALU.mult, [bpLs, bst], [bt2])
                        tt("dve", lt[:, k0:k0 + nb], t1[:, 0:nb], t2[:, 0:nb], ALU.add, [bt1, bt2], [blt])
                    st, bst_, _ = st_r.next()
                    r8c = r8T[:, q:q + 1]

                    def scan(o_ap, d1_ap, n, init, extra=(), r8c=r8c, blt=blt, bst_=bst_):
                        S.op("dve", lambda e: e.tensor_tensor_scan(out=o_ap, data0=r8c.to_broadcast([128, n]), data1=d1_ap,
                                                                   initial=init, op0=ALU.mult, op1=ALU.add),
                             reads=[blt, b_scan] + list(extra), writes=[bst_])
                    if d == 0:
                        scan(st[:, 0:NK], lt[:, 0:NK], NK, 0.0)
                    else:
                        scan(st[:, 0:NKC][:, ::-1], lt[:, 0:NKC][:, ::-1], NKC, 0.0)
                        scan(st[:, NKC:NK][:, ::-1], lt[:, NKC:NK][:, ::-1], NK - NKC, st[:, 0:1], extra=[bst_])
                    x1, bx1, _ = x1_r.next()
                    x2, bx2, _ = x2_r.next()
                    tt("dve", x1, ctab, st, ALU.mult, [bct, bst_], [bx1])
                    tt("dve", x2, stab, st, ALU.mult, [bst, bst_], [bx2])
                    Xs[d] = (x1, bx1, x2, bx2)
                    if g == 0:
                        dump("ctab%d" % d, ctab, bct, [128, NK])
                        dump("stab%d" % d, stab, bst, [128, NK])
                        dump("lt%d" % d, lt, blt, [128, NK])
                        dump("st%d" % d, st, bst_, [128, NK])
                        dump("x1%d" % d, x1, bx1, [128, NK], BF16)
                        if d == 0:
                            dump("ug", ug, bug, [128, NK], BF16)
                            dump("r8T", r8T, b_scan, [128, Q2])
                            dump("fCT", fCT, b_scan, [128, Q2])
                for bi, (k0, nb) in enumerate(blocks):
                    pY, bpY = PS[6 + bi % 2]
                    mml = [(slice(0, nb), 0, ug[:, k0:k0 + nb], bug)]
                    x1, bx1, x2, bx2 = Xs[0]
                    lo = max(k0, 1)
                    if k0 + nb > lo:
                        mml.append((slice(lo - k0, nb), 3, x1[:, lo - 1:k0 + nb - 1], bx1))
                        mml.append((slice(lo - k0, nb), 4, x2[:, lo - 1:k0 + nb - 1], bx2))
                    x1, bx1, x2, bx2 = Xs[1]
                    if k0 < NKC:
                        n1 = nb - 1
                        if n1 > 0:
                            mml.append((slice(0, n1), 7, x1[:, 1:1 + n1], bx1))
                            mml.append((slice(0, n1), 8, x2[:, 1:1 + n1], bx2))
                    else:
                        hi = min(k0 + nb, NK - 1)
                        n1 = hi - k0
                        if n1 > 0:
                            mml.append((slice(0, n1), 7, x1[:, k0 + 1:k0 + 1 + n1], bx1))
                            mml.append((slice(0, n1), 8, x2[:, k0 + 1:k0 + 1 + n1], bx2))
                        if k0 + nb == NK:
                            mml.append((slice(nb - 1, nb), 7, x1[:, 0:1], bx1))
                            mml.append((slice(nb - 1, nb), 8, x2[:, 0:1], bx2))
                    for i, (sl, mi_, rhs, brhs) in enumerate(mml):
                        S.op("pe", lambda e, pY=pY, sl=sl, mi_=mi_, rhs=rhs, i=i, nmm=len(mml), mm=mm: e.matmul(
                            pY[:, sl], lhsT=mm[:, mi_, :], rhs=rhs, start=(i == 0), stop=(i == nmm - 1)),
                            reads=[bmm, brhs], writes=[bpY])
                    copy_op("act", yg_all[:, gl, k0:k0 + nb], pY[:, 0:nb], [bpY], [b_yg[gl]])
            yct, byct, dsy = yct_r.next()
            cnt = 0
            for bi, (k0, nb) in enumerate(blocks):
                for jp in range(8):
                    ps, bps = PS[cnt % 2]
                    cnt += 1
                    for gl in range(ngl):
                        S.op("pe", lambda e, ps=ps, gl=gl, jp=jp, k0=k0, nb=nb: e.matmul(
                            ps[:, 0:nb], lhsT=selTb[:, gl * 8 + jp, :], rhs=yg_all[:, gl, k0:k0 + nb],
                            start=(gl == 0), stop=(gl == ngl - 1)), reads=[b_sel, b_yg[gl]], writes=[bps])
                    S.op("dve", lambda e, yct=yct, U=U, ps=ps, jp=jp, k0=k0, nb=nb, ct=ct: e.scalar_tensor_tensor(
                        out=yct[:, 8 * k0 + jp:8 * (k0 + nb):8], in0=U[:, 8 * k0 + jp:8 * (k0 + nb):8],
                        scalar=vecT[:, ct, R_SSD:R_SSD + 1], in1=ps[:, 0:nb], op0=ALU.mult, op1=ALU.add),
                        reads=[bU, bps, b_vec], writes=[byct])
            S.op("sp", lambda e, yct=yct, ct=ct: e.dma_start(out=Yd[ct * 128:(ct + 1) * 128, :], in_=yct), reads=[byct], dsem=dsy, defer=2)
        A.pop()
        S.barrier()

    def stage_diag(l):
        A.push()
        dg_r = Rot(A, 2, (31, 128), BF16, S, "sp")
        for ct in range(NCT):
            dg, bdg, dsd = dg_r.next()
            for k in range(31):
                if k % 2 == 0:
                    S.op("act", lambda e, dg=dg, k=k, ct=ct: e.activation(out=dg[:, k, :], in_=identb, func=AF.Copy,
                                                                          scale=vecT[:, ct, R_CONVW + k:R_CONVW + k + 1]),
                         reads=[b_const, b_vec], writes=[bdg])
                else:
                    ts("dve", dg[:, k, :], identb, vecT[:, ct, R_CONVW + k:R_CONVW + k + 1], None, ALU.mult, None,
                       [b_const, b_vec], [bdg])
            S.op("sp", lambda e, dg=dg, ct=ct: e.dma_start(out=DGd[ct], in_=dg.rearrange("p k x -> p (k x)")), reads=[bdg], dsem=dsd)
        A.pop()
        S.barrier()

    def stage_B(l, last):
        import os
        BSTOP = int(os.environ.get("B_STOP", "9"))
        A.push()
        big = lambda dt: (A.alloc((NCT, 512), dt), Buf())
        o_t, b_o = big(BF16)
        cv_t, b_cv = big(BF16)
        gy_t, b_gy = big(BF16)
        y2_t, b_y2 = cv_t, b_cv
        ya_t, b_ya = big(BF16)
        yb_t, b_yb = big(BF16)
        mb_t, b_mb = big(BF16)
        ds_y = S.dma_sem("sp")
        PADL = 64 + 30
        vp_r = Rot(A, 2, 8 * PADL, BF16)
        vpc_r = Rot(A, 2, 512 + 30, BF16)
        for vp, bvp, _ in vp_r.items + vpc_r.items:
            S.op("pool", lambda e, vp=vp: e.memset(vp, 0.0), writes=[bvp])
        dg_r = Rot(A, 2, (31, 128), BF16, S, "sp")
        w_r = Rot(A, 3, NCT * 128, BF16, S, "sp")
        la_r = Rot(A, 3, 512, BF16, S, "sp")
        lb_r = Rot(A, 3, 512, BF16, S, "sp")
        xr_r = Rot(A, 3, 512, F32, S, "sp")
        xo_r = Rot(A, 3, 512, F32, S, "sp")
        sq_r = Rot(A, 2, 512, BF16)
        f1_r = Rot(A, 2, 512, F32)
        f2_r = Rot(A, 2, 512, F32)
        h1_r = Rot(A, 2, 512, BF16)
        st_t = [A.alloc(512, F32) for _ in range(4)]
        b_stat = Buf()
        onesb = A.alloc(128, BF16)
        b_ob = Buf()
        S.op("dve", lambda e: e.tensor_copy(out=onesb, in_=ones), reads=[b_const], writes=[b_ob])
        xT_v = xT.rearrange("(ct p) t -> p ct t", p=128)
        Yv = Yd.rearrange("(ct p) t -> p ct t", p=128)
        psi = [0]

        def load_part(rot, part, ct, t0, n):
            tl, btl, dstl = rot.next()
            r0 = part * D + ct * 128
            S.op("sp", lambda e: e.dma_start(out=tl[:, 0:n], in_=Pd[r0:r0 + 128, t0:t0 + n]), writes=[btl], dsem=dstl)
            return tl, btl

        def proj(wname, rhs_t, b_rhs, n, evac):
            for co in range(NCT):
                w, bw, dsw = w_r.next()
                S.op("sp", lambda e, w=w, co=co: e.dma_start(out=w, in_=wb_sq[wname][l, co]), writes=[bw], dsem=dsw)
                ps, bps = PS[psi[0] % 4]
                psi[0] += 1
                for kc in range(NCT):
                    S.op("pe", lambda e, ps=ps, w=w, kc=kc: e.matmul(ps[:, 0:n], lhsT=w[:, kc * 128:(kc + 1) * 128],
                                                                     rhs=rhs_t[:, kc, 0:n], start=(kc == 0), stop=(kc == NCT - 1)),
                         reads=[bw, b_rhs], writes=[bps])
                evac(co, ps, bps)

        def rstd_from(ps_ap, bps_, out_t, n, sub_msq=None):
            if sub_msq is None:
                act(out_t[:, 0:n], ps_ap, AF.Sqrt, [bps_, b_const], [b_stat], scale=1.0 / D, bias=eps_t[:, 0:1])
            else:
                S.op("dve", lambda e: e.scalar_tensor_tensor(out=out_t[:, 0:n], in0=ps_ap, scalar=1.0 / D, in1=sub_msq,
                                                             op0=ALU.mult, op1=ALU.subtract), reads=[bps_, b_stat], writes=[b_stat])
                act(out_t[:, 0:n], out_t[:, 0:n], AF.Sqrt, [b_stat, b_const], [b_stat], bias=eps_t[:, 0:1])
            S.op("dve", lambda e: e.reciprocal(out=out_t[:, 0:n], in_=out_t[:, 0:n]), reads=[b_stat], writes=[b_stat])

        for (t0, n, v) in tiles:
            if (last and v == 1) or BSTOP <= 0:
                continue
            rowlen = n if v == 1 else 64
            nrows = n // rowlen
            padl = rowlen + 30
            ps1, bps1 = PS[6]
            ps2, bps2 = PS[7]
            for ct in range(NCT):
                vb, bvb = load_part(la_r, 2, ct, t0, n)
                sg, bsg = load_part(lb_r, 3, ct, t0, n)
                vp, bvp, _ = (vpc_r if v == 1 else vp_r).next()
                vpv = vp[:, 0:nrows * padl].rearrange("p (r x) -> p r x", x=padl)
                tt("dve", vpv[:, :, 15:15 + rowlen], vb[:, 0:n].rearrange("p (r x) -> p r x", x=rowlen),
                   sg[:, 0:n].rearrange("p (r x) -> p r x", x=rowlen), ALU.mult, [bvb, bsg], [bvp])
                dg, bdg, dsd = dg_r.next()
                S.op("sp", lambda e, dg=dg, ct=ct: e.dma_start(out=dg.rearrange("p k x -> p (k x)"), in_=DGd[ct]), writes=[bdg], dsem=dsd)
                pc, bpc = PS[4 + ct % 2]
                for k in range(31):
                    S.op("pe", lambda e, pc=pc, dg=dg, k=k, vpv=vpv, n=n, rowlen=rowlen: e.matmul(
                        pc[:, 0:n].rearrange("p (r x) -> p r x", x=rowlen), lhsT=dg[:, k, :], rhs=vpv[:, :, k:k + rowlen],
                        start=(k == 0), stop=(k == 30)), reads=[bdg, bvp], writes=[bpc])
                cb = vecT[:, ct, R_CONVB:R_CONVB + 1]
                act(cv_t[:, ct, 0:n], pc[:, 0:n], AF.Identity, [bpc, b_vec], [b_cv], bias=cb)
                sq, bsq, _ = sq_r.next()
                act(sq[:, 0:n], pc[:, 0:n], AF.Square, [bpc, b_vec], [bsq], bias=cb)
                S.op("pe", lambda e, ct=ct, n=n: e.matmul(ps1[:, 0:n], lhsT=onesb, rhs=cv_t[:, ct, 0:n], start=(ct == 0), stop=(ct == NCT - 1)),
                     reads=[b_ob, b_cv], writes=[bps1])
                S.op("pe", lambda e, ct=ct, sq=sq, n=n: e.matmul(ps2[:, 0:n], lhsT=onesb, rhs=sq[:, 0:n], start=(ct == 0), stop=(ct == NCT - 1)),
                     reads=[b_ob, bsq], writes=[bps2])
            if BSTOP <= 1:
                continue
            mean_t, rstd_t, nmr_t, rstdo_t = st_t
            ts("dve", mean_t[:, 0:n], ps1[:, 0:n], 1.0 / D, None, ALU.mult, None, [bps1], [b_stat])
            msq, bmsq, _ = f1_r.next()
            tt("dve", msq[:, 0:n], mean_t[:, 0:n], mean_t[:, 0:n], ALU.mult, [b_stat], [bmsq])
            rstd_from(ps2[:, 0:n], bps2, rstd_t, n, sub_msq=msq[:, 0:n])
            S.op("dve", lambda e, n=n: e.scalar_tensor_tensor(out=nmr_t[:, 0:n], in0=mean_t[:, 0:n], scalar=-1.0, in1=rstd_t[:, 0:n],
                                                              op0=ALU.mult, op1=ALU.mult), reads=[b_stat, bmsq], writes=[b_stat])
            for ct in range(NCT):
                f1, bf1, _ = f1_r.next()
                tt("dve", f1[:, 0:n], cv_t[:, ct, 0:n], rstd_t[:, 0:n], ALU.mult, [b_cv, b_stat], [bf1])
                tt("dve", f1[:, 0:n], f1[:, 0:n], nmr_t[:, 0:n], ALU.add, [bf1, b_stat], [bf1])
                h1, bh1, _ = h1_r.next()
                act(h1[:, 0:n], f1[:, 0:n], AF.Silu, [bf1, b_vec], [bh1], scale=vecT[:, ct, R_LNG:R_LNG + 1],
                    bias=vecT[:, ct, R_LNB:R_LNB + 1])
                zb, bzb = load_part(la_r, 4, ct, t0, n)
                tt("dve", yb_t[:, ct, 0:n], h1[:, 0:n], zb[:, 0:n], ALU.mult, [bh1, bzb], [b_yb])

            def ev_cp(co, ps, bps):
                rb, brb = load_part(lb_r, 6, co, t0, n)
                tt("dve", mb_t[:, co, 0:n], ps[:, 0:n], rb[:, 0:n], ALU.mult, [bps, brb], [b_mb])
            if BSTOP <= 2:
                continue
            proj("conv_proj", yb_t, b_yb, n, ev_cp)
            if BSTOP <= 3:
                continue
            S.op("sp", lambda e, t0=t0, n=n: e.dma_start(out=ya_t[:, :, 0:n], in_=Yv[:, :, t0:t0 + n]), writes=[b_ya], dsem=ds_y)
            for ct in range(NCT):
                act(gy_t[:, ct, 0:n], ya_t[:, ct, 0:n], AF.Gelu_apprx_tanh, [b_ya], [b_gy])

            def ev_glu(co, ps, bps):
                gt_, bgt, _ = h1_r.next()
                act(gt_[:, 0:n], ps[:, 0:n], AF.Sigmoid, [bps, b_vec], [bgt], bias=vecT[:, co, R_GLUB:R_GLUB + 1])
                za, bza = load_part(la_r, 1, co, t0, n)
                tt("dve", gt_[:, 0:n], gt_[:, 0:n], ya_t[:, co, 0:n], ALU.mult, [bgt, b_ya], [bgt])
                tt("dve", y2_t[:, co, 0:n], gt_[:, 0:n], za[:, 0:n], ALU.mult, [bgt, bza], [b_y2])
            proj("glu_w", gy_t, b_gy, n, ev_glu)

            def ev_sp(co, ps, bps):
                ra, bra = load_part(lb_r, 5, co, t0, n)
                f1, bf1, _ = f1_r.next()
                tt("dve", f1[:, 0:n], ps[:, 0:n], ra[:, 0:n], ALU.mult, [bps, bra], [bf1])
                tt("dve", mb_t[:, co, 0:n], f1[:, 0:n], mb_t[:, co, 0:n], ALU.add, [bf1, b_mb], [b_mb])
            proj("ssm_proj", y2_t, b_y2, n, ev_sp)
            if BSTOP <= 4:
                continue
            pso, bpso = PS[6]

            def ev_wo(co, ps, bps):
                act(o_t[:, co, 0:n], ps[:, 0:n], AF.Copy, [bps], [b_o])
                sq, bsq, _ = sq_r.next()
                act(sq[:, 0:n], ps[:, 0:n], AF.Square, [bps], [bsq])
                S.op("pe", lambda e, co=co, sq=sq, n=n: e.matmul(pso[:, 0:n], lhsT=onesb, rhs=sq[:, 0:n], start=(co == 0), stop=(co == NCT - 1)),
                     reads=[b_ob, bsq], writes=[bpso])
            proj("w_out", mb_t, b_mb, n, ev_wo)
            if BSTOP <= 5:
                continue
            rstd_from(pso[:, 0:n], bpso, rstdo_t, n)
            if BSTOP <= 6:
                continue
            for ct in range(NCT):
                xr, bxr, dsxr = xr_r.next()
                S.op("sp", lambda e, xr=xr, ct=ct, t0=t0, n=n: e.dma_start(out=xr[:, 0:n], in_=xT[ct * 128:(ct + 1) * 128, t0:t0 + n]),
                     writes=[bxr], dsem=dsxr)
                f2, bf2, _ = f2_r.next()
                tt("dve", f2[:, 0:n], o_t[:, ct, 0:n], rstdo_t[:, 0:n], ALU.mult, [b_o, b_stat], [bf2])
                xo, bxo, dsxo = xo_r.next()
                S.op("dve", lambda e, xo=xo, f2=f2, xr=xr, ct=ct, n=n, v=v: e.scalar_tensor_tensor(
                    out=xo[:, 0:n], in0=f2[:, 0:n], scalar=gpost[:, ct, v:v + 1], in1=xr[:, 0:n], op0=ALU.mult, op1=ALU.add),
                    reads=[bf2, bxr, b_mod], writes=[bxo])
                S.op("sp", lambda e, xo=xo, ct=ct, t0=t0, n=n: e.dma_start(out=xT[ct * 128:(ct + 1) * 128, t0:t0 + n], in_=xo[:, 0:n]),
                     reads=[bxo], dsem=dsxo, defer=1)
        A.pop()
        S.barrier()

    def stage_epilogue():
        A.push()
        xi_r = Rot(A, 2, (NCT, 128), F32, S, "sp")
        xo_r = Rot(A, 2, D, F32, S, "sp")
        xT_v = xT.rearrange("(ct p) t -> p ct t", p=128)
        for tb in range(LAT // 128):
            t0 = CTX + tb * 128
            xi, bxi, dsi = xi_r.next()
            S.op("sp", lambda e, xi=xi, t0=t0: e.dma_start(out=xi, in_=xT_v[:, :, t0:t0 + 128]), writes=[bxi], dsem=dsi)
            xo, bxo, dso = xo_r.next()
            for q in range((NCT + 3) // 4):
                ps, bps = PS[q % 2]
                nq = min(4, NCT - 4 * q)
                for i in range(nq):
                    ct = 4 * q + i
                    S.op("pe", lambda e, ps=ps, xi=xi, i=i, ct=ct: e.transpose(ps[:, i * 128:(i + 1) * 128], xi[:, ct, :], ident),
                         reads=[bxi, b_const], writes=[bps])
                copy_op(("act", "dve")[q % 2], xo[:, 4 * q * 128:(4 * q + nq) * 128], ps[:, 0:nq * 128], [bps], [bxo])
            S.op("sp", lambda e, xo=xo, tb=tb: e.dma_start(out=out[tb * 128:(tb + 1) * 128, :], in_=xo), reads=[bxo], dsem=dso, defer=1)
        A.pop()
        S.barrier()

    if "pro" in stages:
        stage_prologue()
    if "layers" in stages:
        for l in range(nlayers):
            stage_mod(l)
            if "prep" in stages or "all" in stages:
                stage_prep(l)
            if "A" in stages or "all" in stages:
                stage_A(l)
            if "S" in stages or "all" in stages:
                stage_S(l)
            if "B" in stages or "all" in stages:
                stage_diag(l)
                stage_B(l, l == DEPTH - 1)
    if "epi" in stages:
        stage_epilogue()
    S.barrier()
    S.emit()
    return nc, S


def core_inputs(cfg, inp, b):
    D, DEPTH = cfg["D"], cfg["DEPTH"]
    f = lambda a: np.ascontiguousarray(a, dtype=np.float32)
    m = {
        "x": f(inp["x"][b]), "c": f(inp["c"][b]).reshape(1, D), "ctx": f(inp["ctx"][b]),
        "c_ctx": f(inp["c_ctx"]).reshape(1, D),
        "mod_w": f(inp["mod_w"]), "mod_b": f(inp["mod_b"]).reshape(DEPTH, 3, D),
        "conv_w": f(inp["conv_w"]), "w_in": f(inp["w_in"]),
    }
    for n in ("pre_g", "post_g", "ssm_d", "glu_b", "conv_b", "conv_ln_g", "conv_ln_b"):
        m[n] = f(inp[n]).reshape(DEPTH, 1, D)
    for n in ("glu_w", "ssm_proj", "conv_proj", "w_out"):
        m[n] = f(inp[n])
    G = cfg["G"]
    m["ssm_a_re"] = f(inp["ssm_a_re"]).reshape(DEPTH, 2 * G, 64)
    m["ssm_a_im"] = f(inp["ssm_a_im"]).reshape(DEPTH, 2 * G, 64)
    m["ssm_log_dt"] = f(inp["ssm_log_dt"]).reshape(DEPTH, 1, 2 * G)
    for n in ("ssm_b_re", "ssm_b_im", "ssm_c_re", "ssm_c_im"):
        m[n] = f(inp[n]).reshape(DEPTH, 2, G * 16, 64)
    for k, v in make_consts(cfg).items():
        m["cst_" + k] = v
    return m


N_CORES_USED = 4


def kernel(**inputs):
    cfg = make_cfg()
    nc, _ = build(cfg)
    maps = [core_inputs(cfg, inputs, b) for b in range(N_CORES_USED)]
    res = run_bass_kernel_spmd(nc, maps, core_ids=list(range(N_CORES_USED)))
    out = np.stack([np.asarray(r["out"], dtype=np.float32) for r in res.results], 0)
    return out
```

```python
import os
import numpy as np
import concourse.bass as bass
import concourse.mybir as mybir
from concourse.bass_utils import run_bass_kernel_spmd

F32 = mybir.dt.float32
BF16 = mybir.dt.bfloat16
U8 = mybir.dt.uint8
ALU = mybir.AluOpType
AF = mybir.ActivationFunctionType

ENGS = ("pe", "act", "dve", "pool", "sp")
EPS = 1e-6
MAGIC = 12582912.0
TWO_PI = float(2 * np.pi)


class Buf:
    __slots__ = ("name", "w", "readers")

    def __init__(self, name=""):
        self.name = name
        self.w = None
        self.readers = {}


class Op:
    __slots__ = ("eng", "fn", "deps", "signal", "sem", "cnt", "is_dma")

    def __init__(self, eng, fn, is_dma=False):
        self.eng = eng
        self.fn = fn
        self.deps = []
        self.signal = False
        self.sem = None
        self.cnt = 0
        self.is_dma = is_dma


class Sched:
    def __init__(self, nc):
        self.nc = nc
        self.ops = {e: [] for e in ENGS}
        self.all_ops = []
        self.esem = {e: nc.alloc_semaphore("sem_" + e) for e in ("pe", "act", "dve", "pool")}
        self.dma_sems = []
        self.last_dma = {}
        self.last_op = {e: None for e in ENGS}

    RR_N = {"sp": 84, "pool": 2}

    def dma_sem(self, queue="sp"):
        if not hasattr(self, "rr"):
            self.rr = {"sp": [[], 0], "pool": [[], 0]}
        lst, i = self.rr[queue]
        if len(lst) < self.RR_N[queue]:
            h = self.nc.alloc_semaphore("dsem%d" % len(self.dma_sems))
            self.dma_sems.append([h, queue])
            lst.append(len(self.dma_sems) - 1)
        idx = lst[i % self.RR_N[queue]]
        self.rr[queue][1] = i + 1
        return idx

    def _place(self, o):
        self.ops[o.eng].append(o)
        self.all_ops.append(o)

    def _tick(self, eng, flush_all=False):
        pend = self.pending.get(eng)
        if not pend:
            return
        keep = []
        for item in pend:
            item[0] -= 1
            if item[0] < 0 or flush_all:
                self._place(item[1])
            else:
                keep.append(item)
        self.pending[eng] = keep

    def op(self, eng, fn, reads=(), writes=(), dsem=None, defer=0):
        if not hasattr(self, "pending"):
            self.pending = {}
        o = Op(eng, fn, is_dma=dsem is not None)
        if dsem is not None:
            assert self.dma_sems[dsem][1] == eng
            o.sem = dsem
        deps = {}
        for b in reads:
            if b.w is not None:
                deps[id(b.w)] = b.w
        for b in writes:
            if b.w is not None:
                deps[id(b.w)] = b.w
            for r in b.readers.values():
                deps[id(r)] = r
        for d in deps.values():
            if (not d.is_dma) and (not o.is_dma) and d.eng == "pe" and eng == "pe":
                continue
            d.signal = True
            o.deps.append(d)
        key = ("d", o.sem) if o.is_dma else eng
        for b in reads:
            b.readers[key] = o
        for b in writes:
            b.w = o
            b.readers = {}
        if defer > 0:
            self.pending.setdefault(eng, []).append([defer, o])
        else:
            self._tick(eng)
            self._place(o)
        if o.is_dma:
            o.signal = True
            self.last_dma[dsem] = o
        else:
            self.last_op[eng] = o
        return o

    def barrier(self):
        for e in list(getattr(self, "pending", {})):
            self._tick(e, flush_all=True)
        lasts = [o for o in self.last_op.values() if o is not None and not o.is_dma]
        lasts += list(self.last_dma.values())
        for e in ENGS:
            o = Op(e, None)
            for d in lasts:
                if (not d.is_dma) and d.eng == e and e == "pe":
                    continue
                d.signal = True
                o.deps.append(d)
            self.ops[e].append(o)
            self.all_ops.append(o)

    def emit(self):
        nc = self.nc
        cnt = {e: 0 for e in self.esem}
        dcnt = [0] * len(self.dma_sems)
        for o in self.all_ops:
            if o.is_dma:
                dcnt[o.sem] += 16
                o.cnt = dcnt[o.sem]
            elif o.fn is not None and o.signal:
                cnt[o.eng] += 1
                o.cnt = cnt[o.eng]
        self.final_counts = (cnt, dcnt)

        def run(ename, e):
            known = {}
            for o in self.ops[ename]:
                need = {}
                for d in o.deps:
                    if d.is_dma:
                        key = ("d", d.sem)
                        h = self.dma_sems[d.sem][0]
                    else:
                        key = d.eng
                        h = self.esem[d.eng]
                    if d.cnt > need.get(key, (None, 0))[1]:
                        need[key] = (h, d.cnt)
                for key, (h, c) in need.items():
                    if known.get(key, 0) < c:
                        e.wait_ge(h, c)
                        known[key] = c
                if o.fn is None:
                    continue
                ins = o.fn(e)
                if o.is_dma:
                    ins.then_inc(self.dma_sems[o.sem][0], 16)
                elif o.signal:
                    ins.then_inc(self.esem[ename], 1)

        with nc.Block() as block:
            @block.sync
            def _(e):
                run("sp", e)

            @block.tensor
            def _(e):
                run("pe", e)

            @block.scalar
            def _(e):
                run("act", e)

            @block.vector
            def _(e):
                run("dve", e)

            @block.gpsimd
            def _(e):
                run("pool", e)


class Arena:
    def __init__(self, nc, nbytes):
        self.t = nc.alloc_sbuf_tensor("arena", [128, nbytes], U8)
        self.nbytes = nbytes
        self.off = 0
        self.stack = []

    def push(self):
        self.stack.append(self.off)

    def pop(self):
        self.off = self.stack.pop()

    def alloc(self, free, dt):
        if isinstance(free, int):
            free = (free,)
        esz = 4 if dt == F32 else 2
        n = int(np.prod(free))
        off = (self.off + 63) // 64 * 64
        assert off + n * esz <= self.nbytes, "arena overflow %d" % (off + n * esz)
        ap = self.t[:, off:off + n * esz].bitcast(dt)
        if len(free) == 2:
            ap = ap.rearrange("p (a b) -> p a b", a=free[0])
        elif len(free) == 3:
            ap = ap.rearrange("p (a b c) -> p a b c", a=free[0], b=free[1])
        self.off = off + n * esz
        return ap


class Rot:
    def __init__(self, A, n, free, dt, S=None, dma_queue=None):
        self.items = []
        for i in range(n):
            ap = A.alloc(free, dt)
            ds = S.dma_sem(dma_queue) if dma_queue else None
            self.items.append((ap, Buf(), ds))
        self.i = 0

    def next(self):
        it = self.items[self.i % len(self.items)]
        self.i += 1
        return it


def make_cfg(D=2048, LAT=8192, CTX=256, DEPTH=4):
    c = dict(D=D, LAT=LAT, CTX=CTX, DEPTH=DEPTH)
    c["NCT"] = D // 128
    c["G"] = D // 16
    c["NTOK"] = CTX + LAT
    c["NK"] = c["NTOK"] // 8
    c["NKC"] = CTX // 8
    return c


N_ROWS = 7 + 3 + 31
R_PRE, R_POST, R_SSD, R_GLUB, R_CONVB, R_LNG, R_LNB, R_MODB, R_CONVW = 0, 1, 2, 3, 4, 5, 6, 7, 10


def make_consts(cfg):
    NK, NKC = cfg["NK"], cfg["NKC"]
    c = {}
    c["ident"] = np.eye(128, dtype=np.float32)
    c["ones"] = np.ones((128, 128), np.float32)
    c["pswap"] = np.roll(np.eye(128, dtype=np.float32), 64, axis=1)
    sel = np.zeros((8, 8, 128, 128), np.float32)
    selT = np.zeros((8, 8, 128, 128), np.float32)
    for gl in range(8):
        for j in range(8):
            for cc in range(16):
                sel[gl, j, gl * 16 + cc, j * 16 + cc] = 1.0
                selT[gl, j, j * 16 + cc, gl * 16 + cc] = 1.0
    c["sel"] = np.ascontiguousarray(sel.reshape(64, 128, 128).transpose(1, 0, 2).reshape(128, 64 * 128))
    c["selT"] = np.ascontiguousarray(selT.reshape(64, 128, 128).transpose(1, 0, 2).reshape(128, 64 * 128))
    jj = np.arange(128) // 16
    mf = (jj[None, :] >= jj[:, None]).astype(np.float32)
    mr = (jj[:, None] >= jj[None, :]).astype(np.float32)
    c["maskf"] = np.tile(mf, (1, 4))
    c["maskr"] = np.tile(mr, (1, 4))
    k = np.arange(NK)
    tauf = k.astype(np.float32)
    taur = np.where(k < NKC, NKC - 1 - k, NKC + (NK - 1 - k)).astype(np.float32)
    c["tau"] = np.ascontiguousarray(np.broadcast_to(np.stack([tauf, taur])[None], (128, 2, NK)).reshape(128, 2 * NK))
    ms = np.arange(-7, 9).astype(np.float32)
    c["mvec"] = np.ascontiguousarray(np.broadcast_to(np.concatenate([ms / (2 * np.pi), ms])[None], (128, 32))).astype(np.float32)
    sg = np.ones((128, 2), np.float32)
    sg[64:, 0] = -1.0
    sg[:64, 1] = -1.0
    c["sgn"] = sg
    return c


def build(cfg, dbg=False, stages=("pro", "layers", "all", "epi"), nlayers=None):
    D, LAT, CTX, DEPTH = cfg["D"], cfg["LAT"], cfg["CTX"], cfg["DEPTH"]
    NCT, G, NTOK, NK, NKC = cfg["NCT"], cfg["G"], cfg["NTOK"], cfg["NK"], cfg["NKC"]
    DIN = 7 * D
    if nlayers is None:
        nlayers = DEPTH
    nc = bass.Bass("TRN2", target_bir_lowering=False)
    S = Sched(nc)

    def din(name, shape, dt=F32):
        return nc.dram_tensor(name, list(shape), dt, kind="ExternalInput").ap()

    def dscr(name, shape, dt, out=False):
        kind = "ExternalOutput" if (out or dbg) else "Internal"
        return nc.dram_tensor(name, list(shape), dt, kind=kind).ap()

    x_in = din("x", [LAT, D])
    c_in = din("c", [1, D])
    ctx_in = din("ctx", [CTX, D])
    cctx_in = din("c_ctx", [1, D])
    mod_w = din("mod_w", [DEPTH, D, 3 * D])
    mod_b = din("mod_b", [DEPTH, 3, D])
    vec_in = {n: din(n, [DEPTH, 1, D]) for n in
              ("pre_g", "post_g", "ssm_d", "glu_b", "conv_b", "conv_ln_g", "conv_ln_b")}
    conv_w = din("conv_w", [DEPTH, 31, D])
    w_in = din("w_in", [DEPTH, D, DIN])
    wsq = {n: din(n, [DEPTH, D, D]) for n in ("glu_w", "ssm_proj", "conv_proj", "w_out")}
    a_re = din("ssm_a_re", [DEPTH, 2 * G, 64])
    a_im = din("ssm_a_im", [DEPTH, 2 * G, 64])
    log_dt = din("ssm_log_dt", [DEPTH, 1, 2 * G])
    bc_in = {n: din(n, [DEPTH, 2, G * 16, 64]) for n in ("ssm_b_re", "ssm_b_im", "ssm_c_re", "ssm_c_im")}
    cst = {k: din("cst_" + k, v.shape) for k, v in make_consts(cfg).items()}
    out = nc.dram_tensor("out", [LAT, D], F32, kind="ExternalOutput").ap()

    xT = dscr("xT", [D, NTOK], F32)
    Pd = dscr("Pd", [DIN, NTOK], BF16)
    Yd = dscr("Yd", [D, NTOK], BF16)
    MMd = dscr("MMd", [G, 128, 9 * 128], BF16)
    DGd = dscr("DGd", [NCT, 128, 31 * 128], BF16)
    wb_in = dscr("wb_in", [DEPTH, 7 * NCT, 128, NCT * 128], BF16)
    wb_sq = {n: dscr("wb_" + n, [DEPTH, NCT, 128, NCT * 128], BF16) for n in wsq}

    A = Arena(nc, 190 * 1024)
    A.S = S
    PS = [(nc.alloc_psum_tensor("ps%d" % i, [128, 512], F32), Buf("ps%d" % i)) for i in range(8)]

    ident = A.alloc(128, F32)
    ones = A.alloc(128, F32)
    eps_t = A.alloc(1, F32)
    b_const = Buf("const")
    ds_c = S.dma_sem("sp")
    S.op("sp", lambda e: e.dma_start(out=ident, in_=cst["ident"]), writes=[b_const], dsem=ds_c)
    S.op("sp", lambda e: e.dma_start(out=ones, in_=cst["ones"]), writes=[b_const], dsem=ds_c)
    S.op("dve", lambda e: e.memset(eps_t, EPS), writes=[b_const])
    vecT = A.alloc((NCT, N_ROWS), F32)
    b_vec = Buf("vecT")
    modv = A.alloc((3 * NCT, 2), F32)
    gmod = A.alloc((NCT, 2), F32)
    gpost = A.alloc((NCT, 2), F32)
    b_mod = Buf("mod")
    scT = A.alloc((NCT, 2), F32)
    b_sc = Buf("scT")
    Q2 = 2 * G
    identb = A.alloc(128, BF16)
    pswapb = A.alloc(128, BF16)
    sgn_t = A.alloc(2, F32)
    r8T = A.alloc(Q2, F32)
    fCT = A.alloc(Q2, F32)
    fST = A.alloc(Q2, F32)
    b_scan = Buf("scanpar")
    S.op("dve", lambda e: e.tensor_copy(out=identb, in_=ident), reads=[b_const], writes=[b_const])
    S.op("sp", lambda e: e.dma_start(out=sgn_t, in_=cst["sgn"]), writes=[b_const], dsem=ds_c)
    _pw = A.alloc(128, F32)
    S.op("sp", lambda e: e.dma_start(out=_pw, in_=cst["pswap"]), writes=[b_const], dsem=ds_c)
    S.op("dve", lambda e: e.tensor_copy(out=pswapb, in_=_pw), reads=[b_const], writes=[b_const])

    dbg_sem = [None]

    def dump(name, ap, buf, shape, dt=F32):
        if not dbg:
            return
        if dbg_sem[0] is None:
            dbg_sem[0] = S.dma_sem("sp")
        dd = nc.dram_tensor("dbg_" + name, list(shape), dt, kind="ExternalOutput").ap()
        S.op("sp", lambda e: e.dma_start(out=dd, in_=ap), reads=[buf], dsem=dbg_sem[0])

    cast_rr = [0]

    def cast_eng():
        cast_rr[0] += 1
        return ("act", "dve", "pool")[cast_rr[0] % 3]

    def copy_op(eng, out_ap, in_ap, reads, writes):
        if eng == "act":
            return S.op("act", lambda e: e.activation(out=out_ap, in_=in_ap, func=AF.Copy), reads=reads, writes=writes)
        return S.op(eng, lambda e: e.tensor_copy(out=out_ap, in_=in_ap), reads=reads, writes=writes)

    def stage_prologue():
        A.push()
        xin_r = Rot(A, 2, D, F32, S, "sp")
        xo_r = Rot(A, 2, (NCT, 128), F32, S, "sp")
        xT_v = xT.rearrange("(ct p) t -> p ct t", p=128)
        nblk = NTOK // 128
        for tb in range(nblk):
            t0 = tb * 128
            src = ctx_in[t0:t0 + 128, :] if t0 < CTX else x_in[t0 - CTX:t0 - CTX + 128, :]
            xin, bxin, dsx = xin_r.next()
            S.op("sp", lambda e, xin=xin, src=src: e.dma_start(out=xin, in_=src), writes=[bxin], dsem=dsx)
            xo, bxo, dso = xo_r.next()
            for q in range((NCT + 3) // 4):
                ps, bps = PS[q % 2]
                nq = min(4, NCT - 4 * q)
                for i in range(nq):
                    ct = 4 * q + i
                    S.op("pe", lambda e, ps=ps, xin=xin, i=i, ct=ct: e.transpose(
                        ps[:, i * 128:(i + 1) * 128], xin[:, ct * 128:(ct + 1) * 128], ident),
                        reads=[bxin, b_const], writes=[bps])
                copy_op(("act", "dve")[q % 2], xo[:, 4 * q:4 * q + nq, :],
                        ps[:, 0:nq * 128].rearrange("p (a b) -> p a b", a=nq), [bps], [bxo])
            S.op("sp", lambda e, xo=xo, t0=t0: e.dma_start(out=xT_v[:, :, t0:t0 + 128], in_=xo),
                 reads=[bxo], dsem=dso, defer=1)
        S.barrier()
        A.pop()
        A.push()
        wl_r = Rot(A, 3, (NCT, 128), F32, S, "sp")
        wc_r = Rot(A, 3, NCT * 128, BF16, S, "sp")

        def cast_weight(src2d, dst3d, ncols):
            sv = src2d.rearrange("(kc p) c -> p kc c", p=128)
            for co in range(ncols // 128):
                wl, bwl, dsl = wl_r.next()
                S.op("sp", lambda e, wl=wl, co=co, sv=sv: e.dma_start(out=wl, in_=sv[:, :, co * 128:(co + 1) * 128]),
                     writes=[bwl], dsem=dsl)
                wc, bwc, dsc = wc_r.next()
                copy_op(cast_eng(), wc, wl.rearrange("p a b -> p (a b)"), [bwl], [bwc])
                S.op("sp", lambda e, wc=wc, co=co, dst3d=dst3d: e.dma_start(out=dst3d[co], in_=wc),
                     reads=[bwc], dsem=dsc, defer=2)

        for l in range(nlayers):
            cast_weight(w_in[l], wb_in[l], DIN)
            for n in wsq:
                cast_weight(wsq[n][l], wb_sq[n][l], D)
        S.barrier()
        A.pop()
        A.push()
        crow = A.alloc(D, F32)
        b_crow = Buf()
        ds = S.dma_sem("sp")
        S.op("sp", lambda e: e.dma_start(out=crow[0:1, :], in_=c_in), writes=[b_crow], dsem=ds)
        S.op("sp", lambda e: e.dma_start(out=crow[1:2, :], in_=cctx_in), writes=[b_crow], dsem=ds)
        ps, bps = PS[2]
        for ct in range(NCT):
            S.op("pe", lambda e, ct=ct: e.transpose(ps[:, 2 * ct:2 * ct + 2], crow[0:2, ct * 128:(ct + 1) * 128],
                                                    ident[0:2, 0:2]), reads=[b_crow, b_const], writes=[bps])
        S.op("act", lambda e: e.activation(out=scT.rearrange("p a b -> p (a b)"), in_=ps[:, 0:2 * NCT], func=AF.Silu),
             reads=[bps], writes=[b_sc])
        A.pop()
        S.barrier()

    def stage_mod(l):
        A.push()
        rows = A.alloc(D, F32)
        b_rows = Buf()
        ds = S.dma_sem("sp")
        names = ("pre_g", "post_g", "ssm_d", "glu_b", "conv_b", "conv_ln_g", "conv_ln_b")
        for i, n in enumerate(names):
            S.op("sp", lambda e, i=i, n=n: e.dma_start(out=rows[i:i + 1, :], in_=vec_in[n][l]), writes=[b_rows], dsem=ds)
        S.op("sp", lambda e: e.dma_start(out=rows[R_MODB:R_MODB + 3, :], in_=mod_b[l]), writes=[b_rows], dsem=ds)
        S.op("sp", lambda e: e.dma_start(out=rows[R_CONVW:R_CONVW + 31, :], in_=conv_w[l]), writes=[b_rows], dsem=ds)
        for ct in range(NCT):
            ps, bps = PS[ct % 2]
            S.op("pe", lambda e, ct=ct, ps=ps: e.transpose(ps[:, 0:N_ROWS], rows[0:N_ROWS, ct * 128:(ct + 1) * 128],
                                                           ident[0:N_ROWS, 0:N_ROWS]),
                 reads=[b_rows, b_const], writes=[bps])
            copy_op(("act", "dve")[ct % 2], vecT[:, ct, :], ps[:, 0:N_ROWS], [bps], [b_vec])
        mw_r = Rot(A, 3, (NCT, 128), F32, S, "sp")
        mwv = mod_w[l].rearrange("(kc p) c -> p kc c", p=128)
        psm, bpsm = PS[2]
        for j in range(3 * NCT):
            mw, bmw, dsm = mw_r.next()
            S.op("sp", lambda e, mw=mw, j=j: e.dma_start(out=mw, in_=mwv[:, :, j * 128:(j + 1) * 128]),
                 writes=[bmw], dsem=dsm)
            for kc in range(NCT):
                S.op("pe", lambda e, mw=mw, j=j, kc=kc: e.matmul(psm[:, 2 * j:2 * j + 2], lhsT=mw[:, kc, :],
                                                                  rhs=scT[:, kc, :], start=(kc == 0), stop=(kc == NCT - 1)),
                     reads=[bmw, b_sc], writes=[bpsm])
        psv = psm[:, 0:6 * NCT].rearrange("p (j v) -> p j v", v=2)
        for r in range(3):
            for v in range(2):
                S.op("dve", lambda e, r=r, v=v: e.tensor_tensor(out=modv[:, r * NCT:(r + 1) * NCT, v],
                                                                in0=psv[:, r * NCT:(r + 1) * NCT, v],
                                                                in1=vecT[:, :, R_MODB + r], op=ALU.add),
                     reads=[bpsm, b_vec], writes=[b_mod])
        for v in range(2):
            S.op("dve", lambda e, v=v: e.scalar_tensor_tensor(out=gmod[:, :, v], in0=modv[:, NCT:2 * NCT, v], scalar=1.0,
                                                              in1=vecT[:, :, R_PRE], op0=ALU.add, op1=ALU.mult),
                 reads=[b_mod, b_vec], writes=[b_mod])
            S.op("dve", lambda e, v=v: e.tensor_tensor(out=gpost[:, :, v], in0=modv[:, 2 * NCT:3 * NCT, v],
                                                       in1=vecT[:, :, R_POST], op=ALU.mult),
                 reads=[b_mod, b_vec], writes=[b_mod])
        A.pop()
        S.barrier()

    tiles = []
    t = 0
    while t < CTX:
        n = min(512, CTX - t)
        tiles.append((t, n, 1))
        t += n
    while t < NTOK:
        n = min(512, NTOK - t)
        tiles.append((t, n, 0))
        t += n

    PART_FUNC = [AF.Copy, AF.Silu, AF.Copy, AF.Sigmoid, AF.Silu, AF.Sigmoid, AF.Sigmoid]

    def stage_A(l):
        A.push()
        xt_r = Rot(A, 2, (NCT, 512), F32, S, "sp")
        hT_r = Rot(A, 2, (NCT, 512), BF16)
        sq_r = Rot(A, 2, 512, F32)
        tmp_r = Rot(A, 2, 512, F32)
        rstd_r = Rot(A, 2, 512, F32)
        w_r = Rot(A, 3, NCT * 128, BF16, S, "sp")
        ev_r = Rot(A, 4, 512, BF16, S, "sp")
        xT_v = xT.rearrange("(ct p) t -> p ct t", p=128)
        psi = [0]
        for (t0, n, v) in tiles:
            xt, bxt, dsx = xt_r.next()
            S.op("sp", lambda e, xt=xt, t0=t0, n=n: e.dma_start(out=xt[:, :, 0:n], in_=xT_v[:, :, t0:t0 + n]),
                 writes=[bxt], dsem=dsx)
            pst, bpst = PS[7]
            for ct in range(NCT):
                sq, bsq, _ = sq_r.next()
                S.op("act", lambda e, sq=sq, xt=xt, ct=ct, n=n: e.activation(out=sq[:, 0:n], in_=xt[:, ct, 0:n], func=AF.Square),
                     reads=[bxt], writes=[bsq])
                S.op("pe", lambda e, sq=sq, ct=ct, n=n: e.matmul(pst[:, 0:n], lhsT=ones, rhs=sq[:, 0:n], start=(ct == 0),
                                                                  stop=(ct == NCT - 1)), reads=[bsq, b_const], writes=[bpst])
            rstd, brs, _ = rstd_r.next()
            S.op("act", lambda e, rstd=rstd, n=n: e.activation(out=rstd[:, 0:n], in_=pst[:, 0:n], func=AF.Sqrt, bias=eps_t[:, 0:1],
                                                               scale=1.0 / D), reads=[bpst, b_const], writes=[brs])
            S.op("dve", lambda e, rstd=rstd, n=n: e.reciprocal(out=rstd[:, 0:n], in_=rstd[:, 0:n]), reads=[brs], writes=[brs])
            hT, bhT, _ = hT_r.next()
            for ct in range(NCT):
                tmp, btmp, _ = tmp_r.next()
                S.op("dve", lambda e, tmp=tmp, xt=xt, ct=ct, rstd=rstd, n=n: e.tensor_tensor(
                    out=tmp[:, 0:n], in0=xt[:, ct, 0:n], in1=rstd[:, 0:n], op=ALU.mult), reads=[bxt, brs], writes=[btmp])
                S.op("act", lambda e, tmp=tmp, hT=hT, ct=ct, n=n, v=v: e.activation(
                    out=hT[:, ct, 0:n], in_=tmp[:, 0:n], func=AF.Identity, scale=gmod[:, ct, v:v + 1],
                    bias=modv[:, ct, v:v + 1]), reads=[btmp, b_mod], writes=[bhT])
            for co in range(7 * NCT):
                w, bw, dsw = w_r.next()
                S.op("sp", lambda e, w=w, co=co: e.dma_start(out=w, in_=wb_in[l, co]), writes=[bw], dsem=dsw)
                ps, bps = PS[psi[0] % 4]
                psi[0] += 1
                for kc in range(NCT):
                    S.op("pe", lambda e, ps=ps, w=w, hT=hT, kc=kc, n=n: e.matmul(
                        ps[:, 0:n], lhsT=w[:, kc * 128:(kc + 1) * 128], rhs=hT[:, kc, 0:n], start=(kc == 0),
                        stop=(kc == NCT - 1)), reads=[bw, bhT], writes=[bps])
                ev, bev, dse = ev_r.next()
                fn = PART_FUNC[co // NCT]
                if fn == AF.Copy:
                    S.op("dve", lambda e, ev=ev, ps=ps, n=n: e.tensor_copy(out=ev[:, 0:n], in_=ps[:, 0:n]), reads=[bps], writes=[bev])
                else:
                    S.op("act", lambda e, ev=ev, ps=ps, n=n, fn=fn: e.activation(out=ev[:, 0:n], in_=ps[:, 0:n], func=fn),
                         reads=[bps], writes=[bev])
                S.op("sp", lambda e, ev=ev, co=co, t0=t0, n=n: e.dma_start(out=Pd[co * 128:(co + 1) * 128, t0:t0 + n],
                                                                            in_=ev[:, 0:n]), reads=[bev], dsem=dse, defer=2)
        A.pop()
        S.barrier()

    MI = lambda m: m + 7

    def tt(eng, out_ap, in0, in1, op, reads, writes):
        return S.op(eng, lambda e: e.tensor_tensor(out=out_ap, in0=in0, in1=in1, op=op), reads=reads, writes=writes)

    def ts(eng, out_ap, in0, s1, s2, op0, op1, reads, writes):
        if op1 is None:
            return S.op(eng, lambda e: e.tensor_scalar(out=out_ap, in0=in0, scalar1=s1, scalar2=None, op0=op0),
                        reads=reads, writes=writes)
        return S.op(eng, lambda e: e.tensor_scalar(out=out_ap, in0=in0, scalar1=s1, scalar2=s2, op0=op0, op1=op1),
                    reads=reads, writes=writes)

    def act(out_ap, in_ap, func, reads, writes, scale=1.0, bias=None):
        if bias is None:
            return S.op("act", lambda e: e.activation(out=out_ap, in_=in_ap, func=func, scale=scale), reads=reads, writes=writes)
        return S.op("act", lambda e: e.activation(out=out_ap, in_=in_ap, func=func, scale=scale, bias=bias),
                    reads=reads, writes=writes)

    def stage_prep(l):
        A.push()
        ds = S.dma_sem("sp")
        maskf_t = A.alloc(512, F32)
        maskr_t = A.alloc(512, F32)
        mvec_t = A.alloc(32, F32)
        b_pc = Buf("prepconst")
        for dst, src in ((maskf_t, cst["maskf"]), (maskr_t, cst["maskr"]), (mvec_t, cst["mvec"])):
            S.op("sp", lambda e, dst=dst, src=src: e.dma_start(out=dst, in_=src), writes=[b_pc], dsem=ds)
        areT = A.alloc(Q2, F32)
        aimT = A.alloc(Q2, F32)
        dtT = A.alloc(Q2, F32)
        b_a = Buf("aT")
        z_r = Rot(A, 4, 128, F32, S, "sp")
        for r0 in range(0, Q2, 128):
            rows = min(128, Q2 - r0)
            for src, dstT, pi_ in ((a_re, areT, 0), (a_im, aimT, 1)):
                z, bz, dsz = z_r.next()
                S.op("sp", lambda e, z=z, src=src, r0=r0, rows=rows: e.dma_start(out=z[0:rows, 0:64], in_=src[l, r0:r0 + rows, :]),
                     writes=[bz], dsem=dsz)
                S.op("sp", lambda e, z=z, src=src, r0=r0, rows=rows: e.dma_start(out=z[0:rows, 64:128], in_=src[l, r0:r0 + rows, :]),
                     writes=[bz], dsem=dsz)
                ps, bps = PS[pi_]
                S.op("pe", lambda e, z=z, ps=ps, rows=rows: e.transpose(ps[:, 0:rows], z[0:rows, :], ident[0:rows, 0:rows]),
                     reads=[bz, b_const], writes=[bps])
                copy_op("dve", dstT[:, r0:r0 + rows], ps[:, 0:rows], [bps], [b_a])
        S.op("sp", lambda e: e.dma_start(out=dtT, in_=log_dt[l].to_broadcast([128, Q2])), writes=[b_a], dsem=ds)
        act(dtT, dtT, AF.Exp, [b_a], [b_a])
        xre = A.alloc(Q2, F32)
        th = A.alloc(Q2, F32)
        tt("dve", xre, dtT, areT, ALU.mult, [b_a], [b_a])
        tt("dve", th, dtT, aimT, ALU.mult, [b_a], [b_a])
        NM = 16
        TS_ = A.alloc((NM, Q2), F32)
        TC_ = A.alloc((NM, Q2), F32)
        R_ = A.alloc((NM, Q2), F32)
        MG = A.alloc((NM, Q2), F32)
        PIMN = A.alloc((NM, Q2), F32)
        b_p = Buf("pow")
        fl = lambda ap: ap.rearrange("p a b -> p (a b)")
        bc_m = lambda v: v.unsqueeze(2).to_broadcast([128, NM, Q2])
        bc_q = lambda v: v.unsqueeze(1).to_broadcast([128, NM, Q2])
        tt("dve", TS_, bc_q(th), bc_m(mvec_t[:, 0:NM]), ALU.mult, [b_a, b_pc], [b_p])
        ts("dve", fl(R_), fl(TS_), MAGIC, -MAGIC, ALU.add, ALU.add, [b_p], [b_p])
        ts("dve", fl(TC_), fl(TS_), 0.25, None, ALU.add, None, [b_p], [b_p])
        tt("dve", fl(TS_), fl(TS_), fl(R_), ALU.subtract, [b_p], [b_p])
        ts("dve", fl(R_), fl(TC_), MAGIC, -MAGIC, ALU.add, ALU.add, [b_p], [b_p])
        tt("dve", fl(TC_), fl(TC_), fl(R_), ALU.subtract, [b_p], [b_p])
        copy_op("dve", fCT, TS_[:, MI(8), :], [b_p], [b_scan])
        ts("dve", fST, TS_[:, MI(8), :], sgn_t[:, 0:1], None, ALU.mult, None, [b_p, b_const], [b_scan])
        act(fl(TS_), fl(TS_), AF.Sin, [b_p], [b_p], scale=TWO_PI)
        act(fl(TC_), fl(TC_), AF.Sin, [b_p], [b_p], scale=TWO_PI)
        tt("dve", MG, bc_q(xre), bc_m(mvec_t[:, NM:2 * NM]), ALU.mult, [b_a, b_pc], [b_p])
        act(fl(MG), fl(MG), AF.Exp, [b_p], [b_p])
        copy_op("dve", r8T, MG[:, MI(8), :], [b_p], [b_scan])
        tt("dve", fl(TS_), fl(TS_), fl(MG), ALU.mult, [b_p], [b_p])
        tt("dve", fl(TC_), fl(TC_), fl(MG), ALU.mult, [b_p], [b_p])
        PIM, PRE = TS_, TC_
        sm = [A.alloc(Q2, F32) for _ in range(6)]
        u_, den, t1_, t2_, fre, fim = sm
        b_f = Buf("f")
        ts("dve", u_, PRE[:, MI(1), :], -1.0, None, ALU.add, None, [b_p], [b_f])
        tt("dve", den, areT, areT, ALU.mult, [b_a], [b_f])
        tt("dve", t1_, aimT, aimT, ALU.mult, [b_a], [b_f])
        tt("dve", den, den, t1_, ALU.add, [b_f], [b_f])
        S.op("dve", lambda e: e.reciprocal(out=den, in_=den), reads=[b_f], writes=[b_f])
        tt("dve", t1_, u_, areT, ALU.mult, [b_f, b_a], [b_f])
        tt("dve", t2_, PIM[:, MI(1), :], aimT, ALU.mult, [b_p, b_a], [b_f])
        tt("dve", t1_, t1_, t2_, ALU.add, [b_f], [b_f])
        tt("dve", fre, t1_, den, ALU.mult, [b_f], [b_f])
        tt("dve", t1_, PIM[:, MI(1), :], areT, ALU.mult, [b_p, b_a], [b_f])
        tt("dve", t2_, u_, aimT, ALU.mult, [b_f, b_a], [b_f])
        tt("dve", t1_, t1_, t2_, ALU.subtract, [b_f], [b_f])
        tt("dve", fim, t1_, den, ALU.mult, [b_f], [b_f])
        QRE = A.alloc((8, Q2), F32)
        QIMS = A.alloc((8, Q2), F32)
        QT = A.alloc((8, Q2), F32)
        b_q = Buf("Q")
        bq8 = lambda v: v.unsqueeze(1).to_broadcast([128, 8, Q2])
        P8re, P8im = PRE[:, MI(0):MI(8), :], PIM[:, MI(0):MI(8), :]
        tt("dve", QRE, bq8(fre), P8re, ALU.mult, [b_f, b_p], [b_q])
        tt("dve", QT, bq8(fim), P8im, ALU.mult, [b_f, b_p], [b_q])
        tt("dve", QRE, QRE, QT, ALU.subtract, [b_q], [b_q])
        tt("dve", QIMS, bq8(fre), P8im, ALU.mult, [b_f, b_p], [b_q])
        tt("dve", QT, bq8(fim), P8re, ALU.mult, [b_f, b_p], [b_q])
        tt("dve", QIMS, QIMS, QT, ALU.add, [b_q], [b_q])
        ts("dve", fl(QIMS), fl(QIMS), sgn_t[:, 1:2], None, ALU.mult, None, [b_q, b_const], [b_q])
        PRES2, PRES1 = R_, MG
        ts("dve", fl(PRES2), fl(PRE), sgn_t[:, 0:1], None, ALU.mult, None, [b_p, b_const], [b_p])
        ts("dve", fl(PRES1), fl(PRE), sgn_t[:, 1:2], None, ALU.mult, None, [b_p, b_const], [b_p])
        ts("dve", fl(PIMN), fl(PIM), -1.0, None, ALU.mult, None, [b_p], [b_p])
        zb_r = Rot(A, 4, 128, F32, S, "sp")
        BC = A.alloc((4, 128), F32)
        b_bc = Buf("BC")
        tA_r = Rot(A, 2, 1024, F32)
        tB_r = Rot(A, 2, 1024, F32)
        t2b_r = Rot(A, 2, (8, 128), BF16)
        tcb_r = Rot(A, 2, (8, 128), BF16)
        mms_r = Rot(A, 1, (8, 9, 128), BF16, S, "sp")
        m1acc = A.alloc((2, 512), F32)
        m1tmp = A.alloc((2, 512), F32)
        b_m1 = Buf("m1acc")
        MMv = MMd.rearrange("g p x -> p g x")
        eng_rr = [0]
        NB8 = min(8, G)
        for g0 in range(0, G, NB8):
            mms, bmms, dsmm = mms_r.next()
            for d in range(2):
                q0 = d * G + g0
                srcs = (("ssm_b_re", "ssm_b_im"), ("ssm_b_im", "ssm_b_re"), ("ssm_c_re", "ssm_c_im"), ("ssm_c_im", "ssm_c_re"))
                ps, bps = PS[2]
                for i, (n0, n1) in enumerate(srcs):
                    z, bz, dsz = zb_r.next()
                    for hh, nm in enumerate((n0, n1)):
                        S.op("sp", lambda e, z=z, nm=nm, hh=hh, d=d, g0=g0: e.dma_start(
                            out=z[:, hh * 64:(hh + 1) * 64], in_=bc_in[nm][l, d, g0 * 16:(g0 + NB8) * 16, :]), writes=[bz], dsem=dsz)
                    S.op("pe", lambda e, z=z, ps=ps, i=i: e.transpose(ps[:, i * 128:(i + 1) * 128], z, ident),
                         reads=[bz, b_const], writes=[bps])
                copy_op("act", fl(BC), ps[:, 0:512], [bps], [b_bc])
                Ba, Bb, Ca, Cb = [BC[:, i, :] for i in range(4)]

                def coef(tab, lo, hi, rev):
                    v = tab[:, lo:hi, q0:q0 + NB8]
                    if rev:
                        v = v[:, ::-1, :]
                    return v.rearrange("p m g -> p g m").unsqueeze(3).to_broadcast([128, NB8, 8, 16])

                def data(X):
                    return X.rearrange("p (g c) -> p g c", c=16).unsqueeze(2).to_broadcast([128, NB8, 8, 16])

                def table(out4, cA, dA, cB, dB, deps_r):
                    eng = ("dve", "pool")[eng_rr[0] % 2]
                    eng_rr[0] += 1
                    tA, btA, _ = tA_r.next()
                    tB, btB, _ = tB_r.next()
                    tA4 = tA.rearrange("p (g j c) -> p g j c", g=NB8, j=8)
                    tB4 = tB.rearrange("p (g j c) -> p g j c", g=NB8, j=8)
                    tt(eng, tA4, cA, dA, ALU.mult, deps_r, [btA])
                    tt(eng, tB4, cB, dB, ALU.mult, deps_r, [btB])
                    return eng, tA4, tB4, btA, btB

                fwd = (d == 0)
                t2b, bt2b, _ = t2b_r.next()
                tcb, btcb, _ = tcb_r.next()
                rd = [b_p, b_q, b_bc]
                eng, tA4, tB4, btA, btB = table(None, coef(QRE, 0, 8, fwd), data(Ba), coef(QIMS, 0, 8, fwd), data(Bb), rd)
                tt(eng, t2b.rearrange("p g (j c) -> p g j c", j=8), tA4, tB4, ALU.add, [btA, btB], [bt2b])
                base = 1 + 4 * d
                lo, hi, rv = MI(1), MI(8) + 1, (not fwd)
                eng, tA4, tB4, btA, btB = table(None, coef(PRES2, lo, hi, rv), data(Ca), coef(PIMN, lo, hi, rv), data(Cb), rd)
                tt(eng, mms[:, :, base + 2, :].rearrange("p g (j c) -> p g j c", j=8), tA4, tB4, ALU.add, [btA, btB], [bmms])
                eng, tA4, tB4, btA, btB = table(None, coef(PIMN, lo, hi, rv), data(Ca), coef(PRES1, lo, hi, rv), data(Cb), rd)
                tt(eng, mms[:, :, base + 3, :].rearrange("p g (j c) -> p g j c", j=8), tA4, tB4, ALU.add, [btA, btB], [bmms])
                lo, hi, rv = MI(-7), MI(0) + 1, (not fwd)
                eng, tA4, tB4, btA, btB = table(None, coef(PRES2, lo, hi, rv), data(Ca), coef(PIMN, lo, hi, rv), data(Cb), rd)
                tt(eng, tcb.rearrange("p g (j c) -> p g j c", j=8), tA4, tB4, ALU.add, [btA, btB], [btcb])
                for h in range((NB8 + 3) // 4):
                    ng = min(4, NB8 - 4 * h)
                    p2, bp2 = PS[3]
                    p2s, bp2s = PS[4]
                    p1, bp1 = PS[5 + (h % 2)]
                    for i in range(ng):
                        gi = 4 * h + i
                        S.op("pe", lambda e, p2=p2, t2b=t2b, gi=gi, i=i: e.matmul(p2[:, i * 128:(i + 1) * 128], lhsT=t2b[:, gi, :],
                                                                              rhs=identb, start=True, stop=True),
                             reads=[bt2b, b_const], writes=[bp2])
                        S.op("pe", lambda e, p2s=p2s, t2b=t2b, gi=gi, i=i: e.matmul(p2s[:, i * 128:(i + 1) * 128], lhsT=t2b[:, gi, :],
                                                                                rhs=pswapb, start=True, stop=True),
                             reads=[bt2b, b_const], writes=[bp2s])
                        S.op("pe", lambda e, p1=p1, t2b=t2b, tcb=tcb, gi=gi, i=i: e.matmul(p1[:, i * 128:(i + 1) * 128], lhsT=t2b[:, gi, :],
                                                                                       rhs=tcb[:, gi, :], start=True, stop=True),
                             reads=[bt2b, btcb], writes=[bp1])
                    copy_op("act", mms[:, 4 * h:4 * h + ng, base + 0, :], p2[:, 0:ng * 128].rearrange("p (g x) -> p g x", g=ng),
                            [bp2], [bmms])
                    copy_op("act", mms[:, 4 * h:4 * h + ng, base + 1, :], p2s[:, 0:ng * 128].rearrange("p (g x) -> p g x", g=ng),
                            [bp2s], [bmms])
                    if fwd:
                        tt("dve", m1acc[:, h, 0:ng * 128], p1[:, 0:ng * 128], maskf_t[:, 0:ng * 128], ALU.mult, [bp1, b_pc], [b_m1])
                    else:
                        tt("dve", m1tmp[:, h, 0:ng * 128], p1[:, 0:ng * 128], maskr_t[:, 0:ng * 128], ALU.mult, [bp1, b_pc], [b_m1])
                        tt("dve", mms[:, 4 * h:4 * h + ng, 0, :], m1acc[:, h, 0:ng * 128].rearrange("p (g x) -> p g x", g=ng),
                           m1tmp[:, h, 0:ng * 128].rearrange("p (g x) -> p g x", g=ng), ALU.add, [b_m1], [bmms])
            S.op("sp", lambda e, mms=mms, g0=g0: e.dma_start(out=MMv[:, g0:g0 + NB8, :], in_=mms.rearrange("p g s x -> p g (s x)")),
                 reads=[bmms], dsem=dsmm, defer=2)
        A.pop()
        S.barrier()

    def stage_S(l):
        A.push()
        ds = S.dma_sem("sp")
        selb = A.alloc((64, 128), BF16)
        selTb = A.alloc((64, 128), BF16)
        tau_t = A.alloc((2, NK), F32)
        b_sel = Buf("sel")
        A.push()
        stg_r = Rot(A, 2, 2048, F32, S, "sp")
        for dst, src in ((selb, cst["sel"]), (selTb, cst["selT"])):
            dflat = dst.rearrange("p a b -> p (a b)")
            for pc in range(4):
                stg, bstg, dss = stg_r.next()
                S.op("sp", lambda e, stg=stg, src=src, pc=pc: e.dma_start(out=stg, in_=src[:, pc * 2048:(pc + 1) * 2048]),
                     writes=[bstg], dsem=dss)
                copy_op(("dve", "act")[pc % 2], dflat[:, pc * 2048:(pc + 1) * 2048], stg, [bstg], [b_sel])
        S.op("sp", lambda e: e.dma_start(out=tau_t.rearrange("p a b -> p (a b)"), in_=cst["tau"]), writes=[b_sel], dsem=ds)
        S.barrier()
        A.pop()
        blocks = [(0, NKC)] + [(k0, min(512, NK - k0)) for k0 in range(NKC, NK, 512)]
        U_r = Rot(A, 1, NTOK, BF16, S, "sp")
        yct_r = Rot(A, 1, NTOK, BF16, S, "sp")
        mm_r = Rot(A, 2, (9, 128), BF16, S, "sp")
        ug_r = Rot(A, 2, NK, BF16)
        yg_all = A.alloc((8, NK), BF16)
        b_yg = [Buf() for _ in range(8)]
        ctab_r = Rot(A, 2, NK, F32)
        stab_r = Rot(A, 2, NK, F32)
        ptt_r = Rot(A, 2, NK, F32)
        pr_r = Rot(A, 2, NK, F32)
        t1_r = Rot(A, 2, 512, F32)
        t2_r = Rot(A, 2, 512, F32)
        lt_r = Rot(A, 2, NK, F32)
        st_r = Rot(A, 2, NK, F32)
        x1_r = Rot(A, 4, NK, BF16)
        x2_r = Rot(A, 4, NK, BF16)
        MMg = MMd.rearrange("g p (s x) -> g p s x", s=9)
        for ct in range(NCT):
            U, bU, dsU = U_r.next()
            S.op("sp", lambda e, U=U, ct=ct: e.dma_start(out=U, in_=Pd[ct * 128:(ct + 1) * 128, :]), writes=[bU], dsem=dsU)
            ngl = min(8, G - 8 * ct)
            for gl in range(ngl):
                g = ct * 8 + gl
                mm, bmm, dsm = mm_r.next()
                S.op("sp", lambda e, mm=mm, g=g: e.dma_start(out=mm, in_=MMg[g]), writes=[bmm], dsem=dsm)
                ug, bug, _ = ug_r.next()
                for bi, (k0, nb) in enumerate(blocks):
                    ps, bps = PS[bi % 2]
                    for j in range(8):
                        S.op("pe", lambda e, ps=ps, gl=gl, j=j, U=U, k0=k0, nb=nb: e.matmul(
                            ps[:, 0:nb], lhsT=selb[:, gl * 8 + j, :], rhs=U[:, 8 * k0 + j:8 * (k0 + nb):8],
                            start=(j == 0), stop=(j == 7)), reads=[b_sel, bU], writes=[bps])
                    copy_op("act", ug[:, k0:k0 + nb], ps[:, 0:nb], [bps], [bug])
                Xs = {}
                tabs = {}
                for d in range(2):
                    q = d * G + g
                    ctab, bct, _ = ctab_r.next()
                    stab, bst, _ = stab_r.next()
                    ptt, bptt, _ = ptt_r.next()
                    pr, bpr, _ = pr_r.next()
                    tau_d = tau_t[:, d, :]
                    TE = "dve"
                    ts(TE, ptt, tau_d, fST[:, q:q + 1], None, ALU.mult, None, [b_sel, b_scan], [bptt])
                    ts(TE, pr, ptt, MAGIC, -MAGIC, ALU.add, ALU.add, [bptt], [bpr])
                    tt(TE, stab, ptt, pr, ALU.subtract, [bptt, bpr], [bst])
                    ts(TE, ptt, tau_d, fCT[:, q:q + 1], 0.25, ALU.mult, ALU.add, [b_sel, b_scan], [bptt])
                    ts(TE, pr, ptt, MAGIC, -MAGIC, ALU.add, ALU.add, [bptt], [bpr])
                    tt(TE, ctab, ptt, pr, ALU.subtract, [bptt, bpr], [bct])
                    act(stab, stab, AF.Sin, [bst], [bst], scale=TWO_PI)
                    act(ctab, ctab, AF.Sin, [bct], [bct], scale=TWO_PI)
                    tabs[d] = (ctab, bct, stab, bst)
                for d in range(2):
                    q = d * G + g
                    ctab, bct, stab, bst = tabs[d]
                    base = 1 + 4 * d
                    lt, blt, _ = lt_r.next()
                    for bi, (k0, nb) in enumerate(blocks):
                        pL, bpL = PS[2 + bi % 2]
                        pLs, bpLs = PS[4 + bi % 2]
                        S.op("pe", lambda e, pL=pL, mm=mm, ug=ug, k0=k0, nb=nb, base=base: e.matmul(
                            pL[:, 0:nb], lhsT=mm[:, base, :], rhs=ug[:, k0:k0 + nb], start=True, stop=True),
                            reads=[bmm, bug], writes=[bpL])
                        S.op("pe", lambda e, pLs=pLs, mm=mm, ug=ug, k0=k0, nb=nb, base=base: e.matmul(
                            pLs[:, 0:nb], lhsT=mm[:, base + 1, :], rhs=ug[:, k0:k0 + nb], start=True, stop=True),
                            reads=[bmm, bug], writes=[bpLs])
                        t1, bt1, _ = t1_r.next()
                        t2, bt2, _ = t2_r.next()
                        tt("dve", t1[:, 0:nb], pL[:, 0:nb], ctab[:, k0:k0 + nb], ALU.mult, [bpL, bct], [bt1])
                        tt("dve", t2[:, 0:nb], pLs[:, 0:nb], stab[:, k0:k0 + nb], ALU.mult, [bpLs, bst], [bt2])
                        tt("dve", lt[:, k0:k0 + nb], t1[:, 0:nb], t2[:, 0:nb], ALU.add, [bt1, bt2], [blt])
                    st, bst_, _ = st_r.next()
                    r8c = r8T[:, q:q + 1]

                    def scan(o_ap, d1_ap, n, init, extra=(), r8c=r8c, blt=blt, bst_=bst_):
                        S.op("dve", lambda e: e.tensor_tensor_scan(out=o_ap, data0=r8c.to_broadcast([128, n]), data1=d1_ap,
                                                                   initial=init, op0=ALU.mult, op1=ALU.add),
                             reads=[blt, b_scan] + list(extra), writes=[bst_])
                    if d == 0:
                        scan(st[:, 0:NK], lt[:, 0:NK], NK, 0.0)
                    else:
                        scan(st[:, 0:NKC][:, ::-1], lt[:, 0:NKC][:, ::-1], NKC, 0.0)
                        scan(st[:, NKC:NK][:, ::-1], lt[:, NKC:NK][:, ::-1], NK - NKC, st[:, 0:1], extra=[bst_])
                    x1, bx1, _ = x1_r.next()
                    x2, bx2, _ = x2_r.next()
                    tt("dve", x1, ctab, st, ALU.mult, [bct, bst_], [bx1])
                    tt("dve", x2, stab, st, ALU.mult, [bst, bst_], [bx2])
                    Xs[d] = (x1, bx1, x2, bx2)
                    if g == 0:
                        dump("ctab%d" % d, ctab, bct, [128, NK])
                        dump("stab%d" % d, stab, bst, [128, NK])
                        dump("lt%d" % d, lt, blt, [128, NK])
                        dump("st%d" % d, st, bst_, [128, NK])
                        dump("x1%d" % d, x1, bx1, [128, NK], BF16)
                        if d == 0:
                            dump("ug", ug, bug, [128, NK], BF16)
                            dump("r8T", r8T, b_scan, [128, Q2])
                            dump("fCT", fCT, b_scan, [128, Q2])
                for bi, (k0, nb) in enumerate(blocks):
                    pY, bpY = PS[6 + bi % 2]
                    mml = [(slice(0, nb), 0, ug[:, k0:k0 + nb], bug)]
                    x1, bx1, x2, bx2 = Xs[0]
                    lo = max(k0, 1)
                    if k0 + nb > lo:
                        mml.append((slice(lo - k0, nb), 3, x1[:, lo - 1:k0 + nb - 1], bx1))
                        mml.append((slice(lo - k0, nb), 4, x2[:, lo - 1:k0 + nb - 1], bx2))
                    x1, bx1, x2, bx2 = Xs[1]
                    if k0 < NKC:
                        n1 = nb - 1
                        if n1 > 0:
                            mml.append((slice(0, n1), 7, x1[:, 1:1 + n1], bx1))
                            mml.append((slice(0, n1), 8, x2[:, 1:1 + n1], bx2))
                    else:
                        hi = min(k0 + nb, NK - 1)
                        n1 = hi - k0
                        if n1 > 0:
                            mml.append((slice(0, n1), 7, x1[:, k0 + 1:k0 + 1 + n1], bx1))
                            mml.append((slice(0, n1), 8, x2[:, k0 + 1:k0 + 1 + n1], bx2))
                        if k0 + nb == NK:
                            mml.append((slice(nb - 1, nb), 7, x1[:, 0:1], bx1))
                            mml.append((slice(nb - 1, nb), 8, x2[:, 0:1], bx2))
                    for i, (sl, mi_, rhs, brhs) in enumerate(mml):
                        S.op("pe", lambda e, pY=pY, sl=sl, mi_=mi_, rhs=rhs, i=i, nmm=len(mml), mm=mm: e.matmul(
                            pY[:, sl], lhsT=mm[:, mi_, :], rhs=rhs, start=(i == 0), stop=(i == nmm - 1)),
                            reads=[bmm, brhs], writes=[bpY])
                    copy_op("act", yg_all[:, gl, k0:k0 + nb], pY[:, 0:nb], [bpY], [b_yg[gl]])
            yct, byct, dsy = yct_r.next()
            cnt = 0
            for bi, (k0, nb) in enumerate(blocks):
                for jp in range(8):
                    ps, bps = PS[cnt % 2]
                    cnt += 1
                    for gl in range(ngl):
                        S.op("pe", lambda e, ps=ps, gl=gl, jp=jp, k0=k0, nb=nb: e.matmul(
                            ps[:, 0:nb], lhsT=selTb[:, gl * 8 + jp, :], rhs=yg_all[:, gl, k0:k0 + nb],
                            start=(gl == 0), stop=(gl == ngl - 1)), reads=[b_sel, b_yg[gl]], writes=[bps])
                    S.op("dve", lambda e, yct=yct, U=U, ps=ps, jp=jp, k0=k0, nb=nb, ct=ct: e.scalar_tensor_tensor(
                        out=yct[:, 8 * k0 + jp:8 * (k0 + nb):8], in0=U[:, 8 * k0 + jp:8 * (k0 + nb):8],
                        scalar=vecT[:, ct, R_SSD:R_SSD + 1], in1=ps[:, 0:nb], op0=ALU.mult, op1=ALU.add),
                        reads=[bU, bps, b_vec], writes=[byct])
            S.op("sp", lambda e, yct=yct, ct=ct: e.dma_start(out=Yd[ct * 128:(ct + 1) * 128, :], in_=yct), reads=[byct], dsem=dsy, defer=2)
        A.pop()
        S.barrier()

    def stage_diag(l):
        A.push()
        dg_r = Rot(A, 2, (31, 128), BF16, S, "sp")
        for ct in range(NCT):
            dg, bdg, dsd = dg_r.next()
            for k in range(31):
                if k % 2 == 0:
                    S.op("act", lambda e, dg=dg, k=k, ct=ct: e.activation(out=dg[:, k, :], in_=identb, func=AF.Copy,
                                                                          scale=vecT[:, ct, R_CONVW + k:R_CONVW + k + 1]),
                         reads=[b_const, b_vec], writes=[bdg])
                else:
                    ts("dve", dg[:, k, :], identb, vecT[:, ct, R_CONVW + k:R_CONVW + k + 1], None, ALU.mult, None,
                       [b_const, b_vec], [bdg])
            S.op("sp", lambda e, dg=dg, ct=ct: e.dma_start(out=DGd[ct], in_=dg.rearrange("p k x -> p (k x)")), reads=[bdg], dsem=dsd)
        A.pop()
        S.barrier()

    def stage_B(l, last):
        import os
        BSTOP = int(os.environ.get("B_STOP", "9"))
        A.push()
        big = lambda dt: (A.alloc((NCT, 512), dt), Buf())
        o_t, b_o = big(BF16)
        cv_t, b_cv = big(BF16)
        gy_t, b_gy = big(BF16)
        y2_t, b_y2 = cv_t, b_cv
        ya_t, b_ya = big(BF16)
        yb_t, b_yb = big(BF16)
        mb_t, b_mb = big(BF16)
        ds_y = S.dma_sem("sp")
        PADL = 64 + 30
        vp_r = Rot(A, 2, 8 * PADL, BF16)
        vpc_r = Rot(A, 2, 512 + 30, BF16)
        for vp, bvp, _ in vp_r.items + vpc_r.items:
            S.op("pool", lambda e, vp=vp: e.memset(vp, 0.0), writes=[bvp])
        dg_r = Rot(A, 2, (31, 128), BF16, S, "sp")
        w_r = Rot(A, 3, NCT * 128, BF16, S, "sp")
        la_r = Rot(A, 3, 512, BF16, S, "sp")
        lb_r = Rot(A, 3, 512, BF16, S, "sp")
        xr_r = Rot(A, 3, 512, F32, S, "sp")
        xo_r = Rot(A, 3, 512, F32, S, "sp")
        sq_r = Rot(A, 2, 512, BF16)
        f1_r = Rot(A, 2, 512, F32)
        f2_r = Rot(A, 2, 512, F32)
        h1_r = Rot(A, 2, 512, BF16)
        st_t = [A.alloc(512, F32) for _ in range(4)]
        b_stat = Buf()
        onesb = A.alloc(128, BF16)
        b_ob = Buf()
        S.op("dve", lambda e: e.tensor_copy(out=onesb, in_=ones), reads=[b_const], writes=[b_ob])
        xT_v = xT.rearrange("(ct p) t -> p ct t", p=128)
        Yv = Yd.rearrange("(ct p) t -> p ct t", p=128)
        psi = [0]

        def load_part(rot, part, ct, t0, n):
            tl, btl, dstl = rot.next()
            r0 = part * D + ct * 128
            S.op("sp", lambda e: e.dma_start(out=tl[:, 0:n], in_=Pd[r0:r0 + 128, t0:t0 + n]), writes=[btl], dsem=dstl)
            return tl, btl

        def proj(wname, rhs_t, b_rhs, n, evac):
            for co in range(NCT):
                w, bw, dsw = w_r.next()
                S.op("sp", lambda e, w=w, co=co: e.dma_start(out=w, in_=wb_sq[wname][l, co]), writes=[bw], dsem=dsw)
                ps, bps = PS[psi[0] % 4]
                psi[0] += 1
                for kc in range(NCT):
                    S.op("pe", lambda e, ps=ps, w=w, kc=kc: e.matmul(ps[:, 0:n], lhsT=w[:, kc * 128:(kc + 1) * 128],
                                                                     rhs=rhs_t[:, kc, 0:n], start=(kc == 0), stop=(kc == NCT - 1)),
                         reads=[bw, b_rhs], writes=[bps])
                evac(co, ps, bps)

        def rstd_from(ps_ap, bps_, out_t, n, sub_msq=None):
            if sub_msq is None:
                act(out_t[:, 0:n], ps_ap, AF.Sqrt, [bps_, b_const], [b_stat], scale=1.0 / D, bias=eps_t[:, 0:1])
            else:
                S.op("dve", lambda e: e.scalar_tensor_tensor(out=out_t[:, 0:n], in0=ps_ap, scalar=1.0 / D, in1=sub_msq,
                                                             op0=ALU.mult, op1=ALU.subtract), reads=[bps_, b_stat], writes=[b_stat])
                act(out_t[:, 0:n], out_t[:, 0:n], AF.Sqrt, [b_stat, b_const], [b_stat], bias=eps_t[:, 0:1])
            S.op("dve", lambda e: e.reciprocal(out=out_t[:, 0:n], in_=out_t[:, 0:n]), reads=[b_stat], writes=[b_stat])

        for (t0, n, v) in tiles:
            if (last and v == 1) or BSTOP <= 0:
                continue
            rowlen = n if v == 1 else 64
            nrows = n // rowlen
            padl = rowlen + 30
            S.op("sp", lambda e, t0=t0, n=n: e.dma_start(out=ya_t[:, :, 0:n], in_=Yv[:, :, t0:t0 + n]), writes=[b_ya], dsem=ds_y)
            for ct in range(NCT):
                act(gy_t[:, ct, 0:n], ya_t[:, ct, 0:n], AF.Gelu_apprx_tanh, [b_ya], [b_gy])
            ps1, bps1 = PS[6]
            ps2, bps2 = PS[7]
            for ct in range(NCT):
                vb, bvb = load_part(la_r, 2, ct, t0, n)
                sg, bsg = load_part(lb_r, 3, ct, t0, n)
                vp, bvp, _ = (vpc_r if v == 1 else vp_r).next()
                vpv = vp[:, 0:nrows * padl].rearrange("p (r x) -> p r x", x=padl)
                tt("dve", vpv[:, :, 15:15 + rowlen], vb[:, 0:n].rearrange("p (r x) -> p r x", x=rowlen),
                   sg[:, 0:n].rearrange("p (r x) -> p r x", x=rowlen), ALU.mult, [bvb, bsg], [bvp])
                dg, bdg, dsd = dg_r.next()
                S.op("sp", lambda e, dg=dg, ct=ct: e.dma_start(out=dg.rearrange("p k x -> p (k x)"), in_=DGd[ct]), writes=[bdg], dsem=dsd)
                pc, bpc = PS[4 + ct % 2]
                for k in range(31):
                    S.op("pe", lambda e, pc=pc, dg=dg, k=k, vpv=vpv, n=n, rowlen=rowlen: e.matmul(
                        pc[:, 0:n].rearrange("p (r x) -> p r x", x=rowlen), lhsT=dg[:, k, :], rhs=vpv[:, :, k:k + rowlen],
                        start=(k == 0), stop=(k == 30)), reads=[bdg, bvp], writes=[bpc])
                cb = vecT[:, ct, R_CONVB:R_CONVB + 1]
                act(cv_t[:, ct, 0:n], pc[:, 0:n], AF.Identity, [bpc, b_vec], [b_cv], bias=cb)
                sq, bsq, _ = sq_r.next()
                act(sq[:, 0:n], pc[:, 0:n], AF.Square, [bpc, b_vec], [bsq], bias=cb)
                S.op("pe", lambda e, ct=ct, n=n: e.matmul(ps1[:, 0:n], lhsT=onesb, rhs=cv_t[:, ct, 0:n], start=(ct == 0), stop=(ct == NCT - 1)),
                     reads=[b_ob, b_cv], writes=[bps1])
                S.op("pe", lambda e, ct=ct, sq=sq, n=n: e.matmul(ps2[:, 0:n], lhsT=onesb, rhs=sq[:, 0:n], start=(ct == 0), stop=(ct == NCT - 1)),
                     reads=[b_ob, bsq], writes=[bps2])
            if BSTOP <= 1:
                continue
            mean_t, rstd_t, nmr_t, rstdo_t = st_t
            ts("dve", mean_t[:, 0:n], ps1[:, 0:n], 1.0 / D, None, ALU.mult, None, [bps1], [b_stat])
            msq, bmsq, _ = f1_r.next()
            tt("dve", msq[:, 0:n], mean_t[:, 0:n], mean_t[:, 0:n], ALU.mult, [b_stat], [bmsq])
            rstd_from(ps2[:, 0:n], bps2, rstd_t, n, sub_msq=msq[:, 0:n])
            S.op("dve", lambda e, n=n: e.scalar_tensor_tensor(out=nmr_t[:, 0:n], in0=mean_t[:, 0:n], scalar=-1.0, in1=rstd_t[:, 0:n],
                                                              op0=ALU.mult, op1=ALU.mult), reads=[b_stat, bmsq], writes=[b_stat])
            for ct in range(NCT):
                f1, bf1, _ = f1_r.next()
                tt("dve", f1[:, 0:n], cv_t[:, ct, 0:n], rstd_t[:, 0:n], ALU.mult, [b_cv, b_stat], [bf1])
                tt("dve", f1[:, 0:n], f1[:, 0:n], nmr_t[:, 0:n], ALU.add, [bf1, b_stat], [bf1])
                h1, bh1, _ = h1_r.next()
                act(h1[:, 0:n], f1[:, 0:n], AF.Silu, [bf1, b_vec], [bh1], scale=vecT[:, ct, R_LNG:R_LNG + 1],
                    bias=vecT[:, ct, R_LNB:R_LNB + 1])
                zb, bzb = load_part(la_r, 4, ct, t0, n)
                tt("dve", yb_t[:, ct, 0:n], h1[:, 0:n], zb[:, 0:n], ALU.mult, [bh1, bzb], [b_yb])

            def ev_cp(co, ps, bps):
                rb, brb = load_part(lb_r, 6, co, t0, n)
                tt("dve", mb_t[:, co, 0:n], ps[:, 0:n], rb[:, 0:n], ALU.mult, [bps, brb], [b_mb])
            if BSTOP <= 2:
                continue
            proj("conv_proj", yb_t, b_yb, n, ev_cp)
            if BSTOP <= 3:
                continue

            def ev_glu(co, ps, bps):
                gt_, bgt, _ = h1_r.next()
                act(gt_[:, 0:n], ps[:, 0:n], AF.Sigmoid, [bps, b_vec], [bgt], bias=vecT[:, co, R_GLUB:R_GLUB + 1])
                za, bza = load_part(la_r, 1, co, t0, n)
                tt("dve", gt_[:, 0:n], gt_[:, 0:n], ya_t[:, co, 0:n], ALU.mult, [bgt, b_ya], [bgt])
                tt("dve", y2_t[:, co, 0:n], gt_[:, 0:n], za[:, 0:n], ALU.mult, [bgt, bza], [b_y2])
            proj("glu_w", gy_t, b_gy, n, ev_glu)

            def ev_sp(co, ps, bps):
                ra, bra = load_part(lb_r, 5, co, t0, n)
                f1, bf1, _ = f1_r.next()
                tt("dve", f1[:, 0:n], ps[:, 0:n], ra[:, 0:n], ALU.mult, [bps, bra], [bf1])
                tt("dve", mb_t[:, co, 0:n], f1[:, 0:n], mb_t[:, co, 0:n], ALU.add, [bf1, b_mb], [b_mb])
            proj("ssm_proj", y2_t, b_y2, n, ev_sp)
            if BSTOP <= 4:
                continue
            pso, bpso = PS[6]

            def ev_wo(co, ps, bps):
                act(o_t[:, co, 0:n], ps[:, 0:n], AF.Copy, [bps], [b_o])
                sq, bsq, _ = sq_r.next()
                act(sq[:, 0:n], ps[:, 0:n], AF.Square, [bps], [bsq])
                S.op("pe", lambda e, co=co, sq=sq, n=n: e.matmul(pso[:, 0:n], lhsT=onesb, rhs=sq[:, 0:n], start=(co == 0), stop=(co == NCT - 1)),
                     reads=[b_ob, bsq], writes=[bpso])
            proj("w_out", mb_t, b_mb, n, ev_wo)
            if BSTOP <= 5:
                continue
            rstd_from(pso[:, 0:n], bpso, rstdo_t, n)
            if BSTOP <= 6:
                continue
            for ct in range(NCT):
                xr, bxr, dsxr = xr_r.next()
                S.op("sp", lambda e, xr=xr, ct=ct, t0=t0, n=n: e.dma_start(out=xr[:, 0:n], in_=xT[ct * 128:(ct + 1) * 128, t0:t0 + n]),
                     writes=[bxr], dsem=dsxr)
                f2, bf2, _ = f2_r.next()
                tt("dve", f2[:, 0:n], o_t[:, ct, 0:n], rstdo_t[:, 0:n], ALU.mult, [b_o, b_stat], [bf2])
                xo, bxo, dsxo = xo_r.next()
                S.op("dve", lambda e, xo=xo, f2=f2, xr=xr, ct=ct, n=n, v=v: e.scalar_tensor_tensor(
                    out=xo[:, 0:n], in0=f2[:, 0:n], scalar=gpost[:, ct, v:v + 1], in1=xr[:, 0:n], op0=ALU.mult, op1=ALU.add),
                    reads=[bf2, bxr, b_mod], writes=[bxo])
                S.op("sp", lambda e, xo=xo, ct=ct, t0=t0, n=n: e.dma_start(out=xT[ct * 128:(ct + 1) * 128, t0:t0 + n], in_=xo[:, 0:n]),
                     reads=[bxo], dsem=dsxo, defer=1)
        A.pop()
        S.barrier()

    def stage_epilogue():
        A.push()
        xi_r = Rot(A, 2, (NCT, 128), F32, S, "sp")
        xo_r = Rot(A, 2, D, F32, S, "sp")
        xT_v = xT.rearrange("(ct p) t -> p ct t", p=128)
        for tb in range(LAT // 128):
            t0 = CTX + tb * 128
            xi, bxi, dsi = xi_r.next()
            S.op("sp", lambda e, xi=xi, t0=t0: e.dma_start(out=xi, in_=xT_v[:, :, t0:t0 + 128]), writes=[bxi], dsem=dsi)
            xo, bxo, dso = xo_r.next()
            for q in range((NCT + 3) // 4):
                ps, bps = PS[q % 2]
                nq = min(4, NCT - 4 * q)
                for i in range(nq):
                    ct = 4 * q + i
                    S.op("pe", lambda e, ps=ps, xi=xi, i=i, ct=ct: e.transpose(ps[:, i * 128:(i + 1) * 128], xi[:, ct, :], ident),
                         reads=[bxi, b_const], writes=[bps])
                copy_op(("act", "dve")[q % 2], xo[:, 4 * q * 128:(4 * q + nq) * 128], ps[:, 0:nq * 128], [bps], [bxo])
            S.op("sp", lambda e, xo=xo, tb=tb: e.dma_start(out=out[tb * 128:(tb + 1) * 128, :], in_=xo), reads=[bxo], dsem=dso, defer=1)
        A.pop()
        S.barrier()

    if "pro" in stages:
        stage_prologue()
    if "layers" in stages:
        for l in range(nlayers):
            stage_mod(l)
            if "prep" in stages or "all" in stages:
                stage_prep(l)
            if "A" in stages or "all" in stages:
                stage_A(l)
            if "S" in stages or "all" in stages:
                stage_S(l)
            if "B" in stages or "all" in stages:
                stage_diag(l)
                stage_B(l, l == DEPTH - 1)
    if "epi" in stages:
        stage_epilogue()
    S.barrier()
    S.emit()
    return nc, S


def core_inputs(cfg, inp, b):
    D, DEPTH = cfg["D"], cfg["DEPTH"]
    f = lambda a: np.ascontiguousarray(a, dtype=np.float32)
    m = {
        "x": f(inp["x"][b]), "c": f(inp["c"][b]).reshape(1, D), "ctx": f(inp["ctx"][b]),
        "c_ctx": f(inp["c_ctx"]).reshape(1, D),
        "mod_w": f(inp["mod_w"]), "mod_b": f(inp["mod_b"]).reshape(DEPTH, 3, D),
        "conv_w": f(inp["conv_w"]), "w_in": f(inp["w_in"]),
    }
    for n in ("pre_g", "post_g", "ssm_d", "glu_b", "conv_b", "conv_ln_g", "conv_ln_b"):
        m[n] = f(inp[n]).reshape(DEPTH, 1, D)
    for n in ("glu_w", "ssm_proj", "conv_proj", "w_out"):
        m[n] = f(inp[n])
    G = cfg["G"]
    m["ssm_a_re"] = f(inp["ssm_a_re"]).reshape(DEPTH, 2 * G, 64)
    m["ssm_a_im"] = f(inp["ssm_a_im"]).reshape(DEPTH, 2 * G, 64)
    m["ssm_log_dt"] = f(inp["ssm_log_dt"]).reshape(DEPTH, 1, 2 * G)
    for n in ("ssm_b_re", "ssm_b_im", "ssm_c_re", "ssm_c_im"):
        m[n] = f(inp[n]).reshape(DEPTH, 2, G * 16, 64)
    for k, v in make_consts(cfg).items():
        m["cst_" + k] = v
    return m


N_CORES_USED = 4


def kernel(**inputs):
    cfg = make_cfg()
    nc, _ = build(cfg)
    maps = [core_inputs(cfg, inputs, b) for b in range(N_CORES_USED)]
    res = run_bass_kernel_spmd(nc, maps, core_ids=list(range(N_CORES_USED)))
    out = np.stack([np.asarray(r["out"], dtype=np.float32) for r in res.results], 0)
    return out
```
